# Optimizing a Trainium2 kernel written in Bass

```python
import jax, jax.numpy as jnp
from jax import lax
import numpy as np

D_MODEL = 1024
BATCH = 8
SEQ = 2048
DEPTH = 1

N_META = 16
GRID_W = 64
NA_WIN_ROWS = 8
NA_WIN_COLS = 16
NA_HEADS = 8
NA_HEAD_DIM = 64
NA_WIDTH = NA_HEADS * NA_HEAD_DIM
GLA_HEADS = 4
GLA_DK = 64
GLA_DV = 128
GLA_KW = GLA_HEADS * GLA_DK
GLA_VW = GLA_HEADS * GLA_DV
GLA_GATE_RANK = 16
GLA_GATE_TAU = 16.0
GLA_CHUNK = 64
MIX_WIDTH = NA_WIDTH + GLA_VW
IN_COLS = 3 * NA_WIDTH + 2 * GLA_KW + 2 * GLA_VW + 2 * GLA_GATE_RANK
D_FF = 2816
CONV_W = 3
RMS_EPS = 1e-6

kernel_name = 'hybrid_na_gla_convffn_encoder'


def rmsnorm(x, g):
    xf = x.astype(jnp.float32)
    y = xf * lax.rsqrt(jnp.mean(xf * xf, axis=-1, keepdims=True) + RMS_EPS)
    return (y * g.astype(jnp.float32)).astype(x.dtype)


def neighbourhood_attention(q, k, v, rpb):
    B, L, H, dh = q.shape
    T = L - N_META
    W = GRID_W
    R = T // W
    kh = min(NA_WIN_ROWS, R)
    kw = NA_WIN_COLS
    q = q * (dh ** -0.5)
    qm, km, vm = q[:, :N_META], k[:, :N_META], v[:, :N_META]

    def grid(t):
        return t[:, N_META:].reshape(B, R, W, H, dh).transpose(0, 3, 1, 2, 4)

    qg, kg, vg = grid(q), grid(k), grid(v)
    rs = jnp.clip(jnp.arange(R) - kh // 2, 0, R - kh)
    row_idx = rs[:, None] + jnp.arange(kh)[None, :]
    k_rows = kg[:, :, row_idx]
    v_rows = vg[:, :, row_idx]
    cq = jnp.arange(W)
    cs = jnp.clip(cq - kw // 2, 0, W - kw)
    in_win = (cq[None, :] >= cs[:, None]) & (cq[None, :] < cs[:, None] + kw)
    dr = row_idx - jnp.arange(R)[:, None] + (NA_WIN_ROWS - 1)
    dc = jnp.clip(cq[None, :] - cq[:, None], -(kw - 1), kw - 1) + (NA_WIN_COLS - 1)
    bias = rpb[:, dr[:, None, :, None], dc[None, :, None, :]]

    s_loc = jnp.einsum('bhrcd,bhrikd->bhrcik', qg, k_rows).astype(jnp.float32) + bias.astype(jnp.float32)
    s_loc = jnp.where(in_win[:, None, :], s_loc, -jnp.inf)
    s_meta = jnp.einsum('bhrcd,bmhd->bhrcm', qg, km).astype(jnp.float32)
    s_all = jnp.concatenate([s_loc.reshape(B, H, R, W, kh * W), s_meta], axis=-1)
    p = jax.nn.softmax(s_all, axis=-1).astype(v.dtype)
    p_loc = p[..., :kh * W].reshape(B, H, R, W, kh, W)
    p_meta = p[..., kh * W:]
    og = (jnp.einsum('bhrcik,bhrikd->bhrcd', p_loc, v_rows)
          + jnp.einsum('bhrcm,bmhd->bhrcd', p_meta, vm))
    og = og.transpose(0, 2, 3, 1, 4).reshape(B, T, H * dh)
    sm = jnp.einsum('bqhd,bkhd->bhqk', qm, km).astype(jnp.float32)
    pm = jax.nn.softmax(sm, axis=-1).astype(v.dtype)
    om = jnp.einsum('bhqk,bkhd->bqhd', pm, vm).reshape(B, N_META, H * dh)
    return jnp.concatenate([om, og], axis=1)


def gla_chunked(q, k, v, log_a, strict):
    C = q.shape[-2]
    b = jnp.cumsum(log_a, axis=-2)
    b_last = b[..., -1:, :]
    q_dec = q * jnp.exp(b)
    k_inv = k * jnp.exp(-b)
    k_end = k * jnp.exp(b_last - b)
    mask = jnp.tril(jnp.ones((C, C), dtype=bool), k=-1 if strict else 0)
    a = jnp.where(mask, jnp.einsum('bhncd,bhnsd->bhncs', q_dec, k_inv), 0.0)
    o = jnp.einsum('bhncs,bhnse->bhnce', a, v)
    ds = jnp.einsum('bhncd,bhnce->nbhde', k_end, v)
    decay = jnp.exp(b_last[..., 0, :]).transpose(2, 0, 1, 3)

    def step(s, inp):
        dec, d = inp
        return dec[..., None] * s + d, s

    s0 = jnp.zeros(ds.shape[1:], ds.dtype)
    _, s_prev = lax.scan(step, s0, (decay, ds))
    return o + jnp.einsum('bhncd,nbhde->bhnce', q_dec, s_prev)


def gla_bidirectional(q, k, v, g, z, up_f, bias_f, up_b, bias_b, out_gain):
    B, L, _ = q.shape
    H, dk, dv, C = GLA_HEADS, GLA_DK, GLA_DV, GLA_CHUNK
    f32 = jnp.float32
    pad = (-N_META) % C
    Lp = L + pad
    N = Lp // C
    la_f = jax.nn.log_sigmoid((z[..., :GLA_GATE_RANK] @ up_f + bias_f).astype(f32)) / GLA_GATE_TAU
    la_b = jax.nn.log_sigmoid((z[..., GLA_GATE_RANK:] @ up_b + bias_b).astype(f32)) / GLA_GATE_TAU

    def pad_seq(t, d):
        return jnp.pad(t.astype(f32).reshape(B, L, H, d), ((0, 0), (pad, 0), (0, 0), (0, 0)))

    def chunk(t):
        return t.reshape(B, N, C, H, t.shape[-1]).transpose(0, 3, 1, 2, 4)

    def unchunk(t):
        return t.transpose(0, 2, 3, 1, 4).reshape(B, Lp, H, t.shape[-1])

    qp = pad_seq(q * (dk ** -0.5), dk)
    kp = pad_seq(k, dk)
    vp = pad_seq(v, dv)
    lfp = pad_seq(la_f, dk)
    lbp = pad_seq(la_b, dk)
    o_f = unchunk(gla_chunked(chunk(qp), chunk(kp), chunk(vp), chunk(lfp), False))
    rev = lambda t: t[:, ::-1]
    o_b = rev(unchunk(gla_chunked(chunk(rev(qp)), chunk(rev(kp)), chunk(rev(vp)), chunk(rev(lbp)), True)))
    o = (o_f + o_b)[:, pad:].astype(q.dtype)
    o = rmsnorm(o, out_gain) * jax.nn.silu(g.reshape(B, L, H, dv))
    return o.reshape(B, L, H * dv)


def hybrid_layer(h, norm_mix_pre, w_in, rpb, na_gain, up_f, bias_f, up_b, bias_b, gla_gain,
                 w_o, norm_mix_post, norm_ffn_pre, w_ffn_in, conv_w, conv_b, w_ffn_out, norm_ffn_post):
    B, L, _ = h.shape
    u = rmsnorm(h, norm_mix_pre)
    proj = u @ w_in
    sizes = [NA_WIDTH, NA_WIDTH, NA_WIDTH, GLA_KW, GLA_KW, GLA_VW, GLA_VW, 2 * GLA_GATE_RANK]
    offs = np.cumsum([0] + sizes)
    parts = [proj[..., int(offs[i]):int(offs[i + 1])] for i in range(len(sizes))]
    q_na, k_na, v_na, q_gl, k_gl, v_gl, g_gl, z_gl = parts
    shp = (B, L, NA_HEADS, NA_HEAD_DIM)
    o_na = neighbourhood_attention(q_na.reshape(shp), k_na.reshape(shp), v_na.reshape(shp), rpb)
    o_na = rmsnorm(o_na, na_gain)
    o_gl = gla_bidirectional(q_gl, k_gl, v_gl, g_gl, z_gl, up_f, bias_f, up_b, bias_b, gla_gain)
    mix = jnp.concatenate([o_na, o_gl], axis=-1) @ w_o
    h = h + rmsnorm(mix, norm_mix_post)
    a = rmsnorm(h, norm_ffn_pre) @ w_ffn_in
    half = CONV_W // 2
    ap = jnp.pad(a, ((0, 0), (half, half), (0, 0)))
    a = sum(ap[:, j:j + L] * conv_w[j] for j in range(CONV_W)) + conv_b
    val, gate = a[..., :D_FF], a[..., D_FF:]
    y = (jax.nn.gelu(gate, approximate=True) * val) @ w_ffn_out
    return h + rmsnorm(y, norm_ffn_post)


def setup_inputs(seed: int = 0) -> dict:
    key = jax.random.key(seed)
    ks = jax.random.split(key, 20)
    f32 = jnp.float32

    def nrm(k, shape, scale):
        return jax.random.normal(k, shape, f32) * scale

    def gain(k, shape):
        return 1.0 + 0.1 * jax.random.normal(k, shape, f32)

    return {
        'x': nrm(ks[0], (BATCH, SEQ, D_MODEL), 1.0),
        'meta_tokens': nrm(ks[1], (N_META, D_MODEL), 1.0),
        'norm_mix_pre': gain(ks[2], (DEPTH, D_MODEL)),
        'w_in': nrm(ks[3], (DEPTH, D_MODEL, IN_COLS), D_MODEL ** -0.5),
        'na_rel_bias': nrm(ks[4], (DEPTH, NA_HEADS, 2 * NA_WIN_ROWS - 1, 2 * NA_WIN_COLS - 1), 0.5),
        'na_out_gain': gain(ks[5], (DEPTH, NA_WIDTH)),
        'gla_gate_up_fwd': nrm(ks[6], (DEPTH, GLA_GATE_RANK, GLA_KW), GLA_GATE_RANK ** -0.5),
        'gla_gate_bias_fwd': nrm(ks[7], (DEPTH, GLA_KW), 0.5),
        'gla_gate_up_bwd': nrm(ks[8], (DEPTH, GLA_GATE_RANK, GLA_KW), GLA_GATE_RANK ** -0.5),
        'gla_gate_bias_bwd': nrm(ks[9], (DEPTH, GLA_KW), 0.5),
        'gla_out_gain': gain(ks[10], (DEPTH, GLA_DV)),
        'w_o': nrm(ks[11], (DEPTH, MIX_WIDTH, D_MODEL), MIX_WIDTH ** -0.5),
        'norm_mix_post': gain(ks[12], (DEPTH, D_MODEL)),
        'norm_ffn_pre': gain(ks[13], (DEPTH, D_MODEL)),
        'w_ffn_in': nrm(ks[14], (DEPTH, D_MODEL, 2 * D_FF), D_MODEL ** -0.5),
        'ffn_conv_w': nrm(ks[15], (DEPTH, CONV_W, 2 * D_FF), CONV_W ** -0.5),
        'ffn_conv_b': nrm(ks[16], (DEPTH, 2 * D_FF), 0.02),
        'w_ffn_out': nrm(ks[17], (DEPTH, D_FF, D_MODEL), D_FF ** -0.5),
        'norm_ffn_post': gain(ks[18], (DEPTH, D_MODEL)),
    }


def reference(x, meta_tokens, norm_mix_pre, w_in, na_rel_bias, na_out_gain, gla_gate_up_fwd,
              gla_gate_bias_fwd, gla_gate_up_bwd, gla_gate_bias_bwd, gla_out_gain, w_o, norm_mix_post,
              norm_ffn_pre, w_ffn_in, ffn_conv_w, ffn_conv_b, w_ffn_out, norm_ffn_post):
    B = x.shape[0]
    meta = jnp.broadcast_to(meta_tokens.astype(x.dtype)[None], (B, N_META, x.shape[-1]))
    h = jnp.concatenate([meta, x], axis=1)
    for l in range(DEPTH):
        h = hybrid_layer(h, norm_mix_pre[l], w_in[l], na_rel_bias[l], na_out_gain[l],
                         gla_gate_up_fwd[l], gla_gate_bias_fwd[l], gla_gate_up_bwd[l], gla_gate_bias_bwd[l],
                         gla_out_gain[l], w_o[l], norm_mix_post[l], norm_ffn_pre[l], w_ffn_in[l],
                         ffn_conv_w[l], ffn_conv_b[l], w_ffn_out[l], norm_ffn_post[l])
    return h[:, N_META:]
```

```python
import numpy as np
import concourse.bass as bass
import concourse.mybir as mybir
from concourse.bass_utils import run_bass_kernel_spmd

F32 = mybir.dt.float32
BF16 = mybir.dt.bfloat16
AF = mybir.ActivationFunctionType
ALU = mybir.AluOpType
AX = mybir.AxisListType

D = 1024
SEQ = 2048
NMETA = 16
NT_X = 16
UTW = 2072
DFF = 2816
NCH = 22
EPS = 1e-6
NEG = -30000.0
import os
NDS = int(os.environ.get("K_NDS", "24"))
NQS = 8
ENGS = ("pe", "act", "dve", "pool", "sp")
DEBUG = False


class _Stop(Exception):
    pass


class Prog:
    def __init__(self):
        self.streams = {e: [] for e in ENGS}
        self.cnt = {e: 0 for e in ENGS}
        self.seen = {e: {} for e in ENGS}
        self.lastw = {}
        self.readers = {}
        self.ndma = 0
        self.nq = 0
        self.dma_last = {}

    def _deps(self, reads, writes):
        deps = {}

        def add(tok):
            k, v = tok
            if deps.get(k, 0) < v:
                deps[k] = v
        for r in reads:
            if r in self.lastw:
                add(self.lastw[r])
        for w in writes:
            if w in self.lastw:
                add(self.lastw[w])
            for k, v in self.readers.get(w, {}).items():
                add((k, v))
        return deps

    def _commit(self, tok, reads, writes):
        k, v = tok
        for r in reads:
            d = self.readers.setdefault(r, {})
            if d.get(k, 0) < v:
                d[k] = v
        for w in writes:
            self.lastw[w] = tok
            self.readers[w] = {}

    def _waits(self, eng, deps):
        waits = []
        for k, v in deps.items():
            if self.seen[eng].get(k, 0) < v:
                waits.append((k, v))
                self.seen[eng][k] = v
        return waits

    def op(self, eng, fn, reads=(), writes=()):
        deps = self._deps(reads, writes)
        waits = self._waits(eng, deps)
        self.cnt[eng] += 1
        tok = (eng, self.cnt[eng])
        self.streams[eng].append((waits, fn, None))
        self._commit(tok, reads, writes)
        return tok

    def dma(self, eng, fn, reads=(), writes=()):
        deps = self._deps(reads, writes)
        if eng == "pool":
            idx = ("q", self.nq)
            self.nq += 1
            assert self.nq <= NQS
            val = 16
        else:
            idx = ("d", self.ndma % NDS)
            val = 16 * (self.ndma // NDS + 1)
            self.ndma += 1
            if val > 16:
                deps[idx] = max(deps.get(idx, 0), val - 16)
        waits = self._waits(eng, deps)
        self.streams[eng].append((waits, fn, idx))
        tok = (idx, val)
        self.dma_last[idx] = val
        self._commit(tok, reads, writes)
        return tok

    def barrier(self):
        snap = dict(self.cnt)
        dsnap = dict(self.dma_last)
        for e in ENGS:
            deps = {o: snap[o] for o in ENGS if o != e and snap[o] > 0}
            for idx, v in dsnap.items():
                deps[idx] = v
            waits = self._waits(e, deps)
            if waits:
                self.streams[e].append((waits, None, None))

    def finish(self):
        deps = {idx: v for idx, v in self.dma_last.items()}
        waits = self._waits("sp", deps)
        if waits:
            self.streams["sp"].append((waits, None, None))

    def emit(self, eng, handle, sems):
        for waits, fn, dmaidx in self.streams[eng]:
            for k, v in waits:
                handle.wait_ge(sems[k], v)
            if fn is None:
                continue
            ins = fn(handle)
            if dmaidx is None:
                ins.then_inc(sems[eng], 1)
            else:
                ins.then_inc(sems[dmaidx], 16)


class Arena:
    def __init__(self, tensor, nbytes):
        self.t = tensor
        self.n = nbytes
        self.top = 0
        self.hi = nbytes
        self.peak = 0

    def mark(self):
        return self.top

    def reset(self, m):
        self.top = m

    def alloc(self, dtype, shape, top=False):
        esz = 4 if dtype == F32 else 2
        nel = int(np.prod(shape))
        nb = (nel * esz + 15) // 16 * 16
        if top:
            self.hi -= nb
            off = self.hi
        else:
            off = self.top
            self.top += nb
        self.peak = max(self.peak, self.top + (self.n - self.hi))
        assert self.top <= self.hi, f"SBUF arena overflow {self.top} > {self.hi}"
        ap = self.t[:, off // 4:(off + nb) // 4]
        if dtype == BF16:
            ap = ap.bitcast(BF16)
        ap = ap[:, 0:nel]
        if len(shape) == 2:
            ap = ap.rearrange("p (a b) -> p a b", a=shape[0])
        elif len(shape) == 3:
            ap = ap.rearrange("p (a b c) -> p a b c", a=shape[0], b=shape[1])
        return ap


def tile_cols(i):
    if i == 16:
        return 0, 16
    return 16 + 128 * i, 128


def tiles_overlapping(c0, n):
    res = []
    for i in range(17):
        a, w = tile_cols(i)
        if a < c0 + n and c0 < a + w:
            res.append(i)
    return res


FM_NT = [(0, 512), (512, 512), (1024, 512), (1536, 512), (2048, 16)]


def build_program(stage=9, dumps=()):
    nc = bass.Bass("TRN2", target_bir_lowering=False)

    def din(name, shape):
        return nc.dram_tensor(name, list(shape), F32, kind="ExternalInput").ap()

    x_d = din("x", (SEQ, D))
    meta_d = din("meta", (NMETA, D))
    w_in_d = din("w_in_t", (25, 128, 1024))
    up_d = din("up33", (33, 512))
    tabs_d = din("tabs", (8, 128, 1664))
    w_o_d = din("w_o", (D, D))
    w_f1_d = din("w_f1_t", (44, 128, 1024))
    w_f2_d = din("w_f2", (DFF, D))
    cw_d = din("cw", (128, 44 * 4))
    gfm_d = din("gfm", (128, 24))
    grep_d = din("grep", (4, 128, D))
    ident_d = din("identf", (128, 128))
    tri_d = din("tri", (128, 256))
    mask_d = din("maskfb", (128, 512))
    y_d = nc.dram_tensor("y", [SEQ, D], F32, kind="ExternalOutput").ap()
    dump_reg = {}
    w1bf_d = nc.dram_tensor("w1bf", [44, 128, 1024], BF16, kind="Internal").ap()
    w2bf_d = nc.dram_tensor("w2bf", [NCH, 128, 1024], BF16, kind="Internal").ap()
    dump_shapes = {"uT": (8 * UTW, BF16), "mix": (17 * 1024, BF16), "h1": (17 * 1024, F32), "ona": (17 * 512, F32)}
    dump_dram = {name: nc.dram_tensor("dbg_" + name, [128, dump_shapes[name][0]], dump_shapes[name][1],
                                      kind="ExternalOutput").ap() for name in dumps}

    P = Prog()
    SB_BYTES = 212000

    from contextlib import ExitStack
    with ExitStack() as es:
        sb_t = es.enter_context(nc.sbuf_tensor("arena", [128, SB_BYTES // 4], F32))
        ps_t = es.enter_context(nc.psum_tensor("psum", [128, 4096], F32))
        sems = {}
        for e in ENGS:
            sems[e] = es.enter_context(nc.semaphore("s_" + e))
        for i in range(NDS):
            sems[("d", i)] = es.enter_context(nc.semaphore("d%d" % i))
        for i in range(NQS):
            sems[("q", i)] = es.enter_context(nc.semaphore("q%d" % i))
        block = es.enter_context(nc.Block())

        A = Arena(sb_t, SB_BYTES)

        def bank(b, n=1):
            return ps_t[:, b * 512:(b + n) * 512]

        ident = A.alloc(BF16, (128,))
        tri = A.alloc(F32, (256,))
        gfm = A.alloc(F32, (24,))
        uT = A.alloc(BF16, (8, UTW))
        stat = A.alloc(F32, (8, 17))
        junk = A.alloc(BF16, (1024,))

        P.dma("pool", lambda e: e.dma_start(out=ident, in_=ident_d), writes=[("ident",)])
        P.dma("sp", lambda e: e.dma_start(out=tri, in_=tri_d), writes=[("tri",)])
        P.dma("sp", lambda e: e.dma_start(out=gfm, in_=gfm_d), writes=[("gfm",)])
        P.op("pool", lambda e: e.memset(uT[:, :, 2064:UTW], 0.0), writes=[("uTpad",)])

        def rstd_ops(ss_ap, tmp_ap, out_ap, n, rkey):
            P.op("act", lambda e: e.activation(out=tmp_ap, in_=ss_ap, func=AF.Ln, scale=1.0 / n, bias=EPS),
                 reads=[rkey + ("ss",)], writes=[rkey + ("ln",)])
            P.op("act", lambda e: e.activation(out=out_ap, in_=tmp_ap, func=AF.Exp, scale=-0.5),
                 reads=[rkey + ("ln",)], writes=[rkey + ("rstd",)])

        def norm_transpose(i, src, src_key, gcol, xs_slot, xs_key, psT, psT_key, stat_base, tag):
            c0, rows = tile_cols(i)
            ss = stat[:rows, stat_base, i:i + 1]
            ln = stat[:rows, stat_base + 1, i:i + 1]
            rs = stat[:rows, stat_base + 2, i:i + 1]
            rkey = (tag, i)
            P.op("act", lambda e: e.activation(out=junk[:rows, :], in_=src, func=AF.Square, accum_out=ss),
                 reads=[src_key], writes=[("junk",), rkey + ("ss",)])
            rstd_ops(ss, ln, rs, D, rkey)
            if tag == "A":
                P.op("dve", lambda e: e.tensor_scalar(out=xs_slot[:rows, :], in0=src, scalar1=rs, scalar2=None, op0=ALU.mult),
                     reads=[src_key, rkey + ("rstd",)], writes=[xs_key])
            else:
                P.op("act", lambda e: e.activation(out=xs_slot[:rows, :], in_=src, func=AF.Copy, scale=rs),
                     reads=[src_key, rkey + ("rstd",)], writes=[xs_key])
            psv = psT.bitcast(BF16)

            def tr(e):
                ins = None
                for k in range(8):
                    ins = e.transpose(out=psv[:, k * 128:k * 128 + rows], in_=xs_slot[:rows, k * 128:(k + 1) * 128],
                                      identity=ident[:rows, :rows])
                return ins
            P.op("pe", tr, reads=[xs_key, ("ident",)], writes=[psT_key])
            src_v = psv.rearrange("p (k c) -> p k c", k=8)[:, :, 0:rows]
            gb = gfm[:, gcol:gcol + 8].to_broadcast([128, 8, rows])

            def fin():
                P.op("dve", lambda e: e.tensor_tensor(out=uT[:, :, c0:c0 + rows], in0=src_v, in1=gb, op=ALU.mult),
                     reads=[psT_key, ("gfm",)], writes=[("uT", i)])
            return fin

        def dump_all():
            for name in dumps:
                ap, dt = dump_reg[name]
                shp = list(ap.shape)
                dd = dump_dram[name]
                src = ap
                if len(shp) == 3:
                    src = ap.rearrange("p a b -> p (a b)")
                elif len(shp) == 4:
                    src = ap.rearrange("p a b c -> p (a b c)")
                P.dma("sp", (lambda dd, src: lambda e: e.dma_start(out=dd, in_=src))(dd, src))

        dump_reg["uT"] = (uT, BF16)
        try:
            hiA = A.hi
            grepA = A.alloc(F32, (1024,), top=True)
            xin = [A.alloc(F32, (2, 1024), top=True) for _ in range(2)]
            xmeta = A.alloc(F32, (1024,), top=True)
            xs = [A.alloc(BF16, (1024,), top=True) for _ in range(2)]
            P.dma("sp", lambda e: e.dma_start(out=grepA, in_=grep_d[0]), writes=[("grep", "A")])
            P.dma("sp", lambda e: e.dma_start(out=xmeta[:16, :], in_=meta_d), writes=[("xmeta",)])
            for pr in range(2):
                srcd = x_d[pr * 256:(pr + 1) * 256, :].rearrange("(t p) d -> p t d", p=128)
                P.dma("act" if pr == 1 else "sp", (lambda pr, srcd: lambda e: e.dma_start(out=xin[pr % 2], in_=srcd))(pr, srcd),
                      writes=[("xin", pr % 2)])
            finA = [None]
            for idxA, i in enumerate([16] + list(range(16))):
                sl = idxA % 2
                c0, rows = tile_cols(i)
                if i == 16:
                    src_ap, src_key = xmeta[:16, :], ("xmeta",)
                else:
                    pr = i // 2
                    if i % 2 == 0 and pr >= 2:
                        srcd = x_d[pr * 256:(pr + 1) * 256, :].rearrange("(t p) d -> p t d", p=128)
                        P.dma("sp" if pr % 2 == 0 else "act", (lambda pr, srcd: lambda e: e.dma_start(out=xin[pr % 2], in_=srcd))(pr, srcd),
                              writes=[("xin", pr % 2)])
                    src_ap, src_key = xin[pr % 2][:, i % 2, :], ("xin", pr % 2)
                fin_ = norm_transpose(i, src_ap, src_key, 8, xs[sl], ("xs", sl),
                                      bank(6 + sl), ("psP", sl), 0, "A")
                if finA[0] is not None:
                    finA[0]()
                finA[0] = fin_
            finA[0]()
            if stage <= 1:
                raise _Stop()

            mB = A.mark()
            mix = A.alloc(BF16, (17, 1024))
            mB2 = A.mark()
            dump_reg["mix"] = (mix, BF16)
            P.op("pool", lambda e: e.memset(mix[:, 16, :], 0.0), writes=[("mix", 16, "na")] + [("mix", 16, "g", h) for h in range(4)])
            zT = A.alloc(BF16, (UTW,))
            upsb = A.alloc(BF16, (512,))
            wb = [A.alloc(BF16, (2, 8, 128)) for _ in range(3)]
            wb_i = [0]
            psP = [bank(6), bank(7)]
            psP_i = [0]
            ev_i = [0]

            P.dma("pool", lambda e: e.dma_start(out=upsb[0:33, :], in_=up_d), writes=[("upsb",)])
            P.op("pool", lambda e: e.memset(zT[32:33, :], 1.0), writes=[("zT1",)])

            def evac_copy(dst, src, reads, writes, func=None):
                if func is not None:
                    P.op("act", lambda e: e.activation(out=dst, in_=src, func=func), reads=reads, writes=writes)
                    return
                ev_i[0] += 1
                if ev_i[0] % 2 == 0:
                    P.op("act", lambda e: e.copy(out=dst, in_=src), reads=reads, writes=writes)
                else:
                    P.op("dve", lambda e: e.tensor_copy(out=dst, in_=src), reads=reads, writes=writes)

            wstg = [A.alloc(F32, (8, 128)) for _ in range(2)]
            stg_i = [0]

            def load_cast(src_ap, dst_ap, dst_key, stg, eng="pool"):
                ss_ = stg_i[0] % len(stg)
                stg_i[0] += 1
                P.dma("sp", lambda e: e.dma_start(out=stg[ss_], in_=src_ap), writes=[("stg", id(stg), ss_)])
                if eng == "act":
                    P.op("act", lambda e: e.copy(out=dst_ap, in_=stg[ss_]), reads=[("stg", id(stg), ss_)], writes=[dst_key])
                else:
                    P.op(eng, lambda e: e.tensor_copy(out=dst_ap, in_=stg[ss_]), reads=[("stg", id(stg), ss_)], writes=[dst_key])

            def load_wblk(col0, ncols):
                sl = wb_i[0] % 3
                wb_i[0] += 1
                assert col0 % 128 == 0
                nblk = (ncols + 127) // 128
                for j in range(nblk):
                    srcap = w_in_d[col0 // 128 + j].rearrange("p (k c) -> p k c", k=8)
                    load_cast(srcap, wb[sl][:, j, :, :], ("wb", sl, j), wstg)
                return sl, nblk

            def proj_fm(col0, ncols, dst, dst_key):
                sl, _nb = load_wblk(col0, ncols)
                for (c0, n) in FM_NT:
                    pb = psP_i[0] % 2
                    psP_i[0] += 1
                    ps = psP[pb]

                    def mm(e, sl=sl, c0=c0, n=n, ps=ps):
                        ins = None
                        for k in range(8):
                            ins = e.matmul(ps[0:ncols, 0:n], lhsT=wb[sl][:, 0, k, 0:ncols], rhs=uT[:, k, c0:c0 + n],
                                           start=(k == 0), stop=(k == 7))
                        return ins
                    tl = tiles_overlapping(c0, n)
                    P.op("pe", mm, reads=[("wb", sl, 0)] + [("uT", t) for t in tl], writes=[("psP", pb)])
                    evac_copy(dst[0:ncols, c0:c0 + n], ps[0:ncols, 0:n], [("psP", pb)],
                              [dst_key + (t,) for t in tl])
                    yield

            def proj_tm(col0, ncols, evac):
                sl, nblk = load_wblk(col0, ncols)
                for i in range(17):
                    c0, rows = tile_cols(i)
                    pb = psP_i[0] % 2
                    psP_i[0] += 1
                    ps = psP[pb]

                    def mm(e, sl=sl, c0=c0, rows=rows, ps=ps):
                        ins = None
                        for k in range(8):
                            ins = e.matmul(ps[0:rows, 0:ncols], lhsT=uT[:, k, c0:c0 + rows], rhs=wb[sl][:, 0:nblk, k, :],
                                           start=(k == 0), stop=(k == 7))
                        return ins
                    P.op("pe", mm, reads=[("wb", sl, j) for j in range(nblk)] + [("uT", i)], writes=[("psP", pb)])
                    evac(i, rows, ps[0:rows, 0:ncols], ("psP", pb))
                    yield

            def run(gen):
                for _ in gen:
                    pass

            def interleave(main, side, ratio=1):
                side_done = False
                for _ in main:
                    for _r in range(ratio):
                        if not side_done:
                            try:
                                next(side)
                            except StopIteration:
                                side_done = True
                if not side_done:
                    for _ in side:
                        pass

            def gen_interleave(ga, gb):
                da = db = False
                while not (da and db):
                    if not da:
                        try:
                            next(ga)
                        except StopIteration:
                            da = True
                    if not db:
                        try:
                            next(gb)
                        except StopIteration:
                            db = True
                    yield

            def gen_chain(*gens):
                for g_ in gens:
                    yield from g_

            mNA = A.mark()
            ona = A.alloc(F32, (17, 512))
            dump_reg["ona"] = (ona, F32)
            qT = [A.alloc(BF16, (UTW,)) for _ in range(2)]
            kT = [A.alloc(BF16, (UTW,)) for _ in range(2)]
            vaug = [A.alloc(BF16, (17, 2, 65)) for _ in range(2)]
            tab = [A.alloc(BF16, (2, 1664)) for _ in range(2)]
            S_sb = [A.alloc(F32, (640,)) for _ in range(2)]
            PT = [[A.alloc(BF16, (640,)) for _ in range(2)] for _h in range(2)]
            PTm = [[A.alloc(BF16, (128,)) for _ in range(2)] for _h in range(2)]
            PTq = [A.alloc(BF16, (16,)) for _ in range(2)]
            rec = A.alloc(F32, (4,))
            for b in range(2):
                P.op("pool", (lambda b: lambda e: e.memset(vaug[b][:, 16, :, :], 0.0))(b), writes=[("vz", b)])
                P.op("pool", (lambda b: lambda e: e.memset(vaug[b][:, :, :, 64:65], 1.0))(b), reads=[("vz", b)], writes=[("vones", b)])
            for hl_ in range(2):
                for par_ in range(2):
                    P.op("pool", (lambda hl_, par_: lambda e: e.memset(PTm[hl_][par_], 0.0))(hl_, par_),
                         writes=[("PTm0",), ("PTm", hl_, par_)])

            psS = [bank(0, 2), bank(2, 2)]
            psO = [bank(4), bank(5)]

            def na_proj(hp):
                b = hp % 2
                srcap = tabs_d[2 * hp:2 * hp + 2].rearrange("h p c -> p h c")
                P.dma("pool", lambda e: e.dma_start(out=tab[b], in_=srcap), writes=[("tab", b)])
                yield from proj_fm(hp * 128, 128, qT[b], ("qT", b))
                yield from proj_fm(512 + hp * 128, 128, kT[b], ("kT", b))

                def ev(i, rows, ps, pkey):
                    dst = vaug[b][:rows, i, :, 0:64]
                    evac_copy(dst, ps.rearrange("p (h d) -> p h d", h=2), [pkey, ("vones", b)], [("vaug", b, i)])
                yield from proj_tm(1024 + hp * 128, 128, ev)

            na_it = [0]

            def na_attn(hp):
                b = hp % 2
                its = []
                for i in range(16):
                    if 2 <= i <= 13:
                        js = [i + 2 - c for c in range(5)]
                        tcol = 1024
                    else:
                        lo = 0 if i < 2 else 12
                        js = [lo + 3 - c for c in range(4)]
                        tcol = (7 - 2 * (js[0] - i)) * 64
                    its.append((i, js, tcol))
                its.append((16, [], 0))

                def stageA(t):
                    i, js, tcol = its[t]
                    par = t % 2
                    if i == 16:
                        def mmM(e):
                            ins = None
                            for hl in range(2):
                                r0 = hl * 64
                                ins = e.matmul(psS[hl][0:16, 768:784], lhsT=kT[b][r0:r0 + 64, 0:16],
                                               rhs=qT[b][r0:r0 + 64, 0:16], start=True, stop=True)
                            return ins
                        P.op("pe", mmM, reads=[("kT", b, 16), ("qT", b, 16)], writes=[("psS", 0), ("psS", 1)])
                        for hl in range(2):
                            P.op("act", (lambda hl: lambda e: e.activation(out=PTq[hl][0:16, 0:16], in_=psS[hl][0:16, 768:784],
                                                                          func=AF.Exp, scale=0.125))(hl),
                                 reads=[("psS", hl)], writes=[("PTq", hl)])
                        return
                    qc0 = 16 + 128 * i
                    ncol = 128 * len(js)

                    for hl in range(2):
                        pS = psS[hl]
                        r0 = hl * 64

                        def mmS(e, pS=pS, r0=r0):
                            for c, j in enumerate(js):
                                kc0 = 16 + 128 * j
                                e.matmul(pS[:, c * 128:(c + 1) * 128], lhsT=kT[b][r0:r0 + 64, kc0:kc0 + 128],
                                         rhs=qT[b][r0:r0 + 64, qc0:qc0 + 128], start=True, stop=True)
                            return e.matmul(pS[:, 768:896], lhsT=kT[b][r0:r0 + 64, 0:128],
                                            rhs=qT[b][r0:r0 + 64, qc0:qc0 + 128], start=True, stop=True)
                        P.op("pe", mmS, reads=[("kT", b, j) for j in js] + [("kT", b, 16), ("kT", b, 0), ("qT", b, i)],
                             writes=[("psS", hl)])
                        P.op("dve", (lambda hl, pS: lambda e: e.scalar_tensor_tensor(
                            out=S_sb[hl][:, 0:ncol], in0=pS[:, 0:ncol], scalar=0.125, in1=tab[b][:, hl, tcol:tcol + ncol],
                            op0=ALU.mult, op1=ALU.add))(hl, pS),
                            reads=[("psS", hl), ("tab", b)], writes=[("S_sb", hl)])
                        P.op("act", (lambda hl: lambda e: e.activation(out=PT[hl][par][:, 0:ncol], in_=S_sb[hl][:, 0:ncol],
                                                                      func=AF.Exp))(hl),
                             reads=[("S_sb", hl)], writes=[("PT", hl, par)])
                        P.op("act", (lambda hl, pS: lambda e: e.activation(out=PTm[hl][par][0:16, :], in_=pS[0:16, 768:896],
                                                                          func=AF.Exp, scale=0.125))(hl, pS),
                             reads=[("psS", hl), ("PTm0",)], writes=[("PTm", hl, par)])

                def stageB(t):
                    i, js, tcol = its[t]
                    par = t % 2
                    for hl in range(2):
                        pO = psO[hl]
                        head = 2 * hp + hl
                        if i == 16:
                            P.op("pe", (lambda hl, pO: lambda e: e.matmul(pO[0:16, 0:65], lhsT=PTq[hl][0:16, 0:16],
                                                                          rhs=vaug[b][0:16, 16, hl, :], start=True, stop=True))(hl, pO),
                                 reads=[("PTq", hl), ("vaug", b, 16)], writes=[("psO", hl)])
                            rows = 16
                        else:
                            def mmO(e, hl=hl, pO=pO):
                                for c, j in enumerate(js):
                                    e.matmul(pO[:, 0:65], lhsT=PT[hl][par][:, c * 128:(c + 1) * 128], rhs=vaug[b][:, j, hl, :],
                                             start=(c == 0), stop=False)
                                return e.matmul(pO[:, 0:65], lhsT=PTm[hl][par][:, :], rhs=vaug[b][:, 16, hl, :],
                                                start=False, stop=True)
                            P.op("pe", mmO, reads=[("PT", hl, par), ("PTm", hl, par), ("vz", b)] + [("vaug", b, j) for j in js] + [("vaug", b, 16)],
                                 writes=[("psO", hl)])
                            rows = 128
                        P.op("dve", (lambda hl, pO, rows: lambda e: e.reciprocal(out=rec[0:rows, hl:hl + 1], in_=pO[0:rows, 64:65]))(hl, pO, rows),
                             reads=[("psO", hl)], writes=[("rec", hl)])
                        P.op("act", (lambda hl, pO, rows, head: lambda e: e.activation(
                            out=ona[0:rows, i, head * 64:(head + 1) * 64], in_=pO[0:rows, 0:64], func=AF.Copy,
                            scale=rec[0:rows, hl:hl + 1]))(hl, pO, rows, head),
                            reads=[("psO", hl), ("rec", hl)], writes=[("ona", i, head)])

                n = len(its)
                for t in range(n + 1):
                    if t < n:
                        stageA(t)
                    if t >= 1:
                        stageB(t - 1)
                    if hp == 3 and t >= 2:
                        na_final(its[t - 2][0])
                    yield
                    yield
                if hp == 3:
                    na_final(its[n - 1][0])

            def na_final(i):
                c0, rows = tile_cols(i)
                ss = stat[:rows, 3, i:i + 1]
                ln = stat[:rows, 4, i:i + 1]
                rs = stat[:rows, 5, i:i + 1]
                P.op("act", (lambda rows, i, ss: lambda e: e.activation(out=junk[:rows, 0:512], in_=ona[:rows, i, :],
                                                                         func=AF.Square, accum_out=ss))(rows, i, ss),
                     reads=[("ona", i, h) for h in range(8)], writes=[("junk",), ("N", i, "ss")])
                rstd_ops(ss, ln, rs, 512, ("N", i))
                P.op("dve", (lambda rows, i, rs: lambda e: e.tensor_scalar(out=mix[:rows, i, 0:512], in0=ona[:rows, i, :],
                                                                            scalar1=rs, scalar2=None, op0=ALU.mult))(rows, i, rs),
                     reads=[("ona", i, h) for h in range(8)] + [("N", i, "rstd")], writes=[("mix", i, "na")])
            def precast_gen():
                blocks = [(w_f2_d[k_ * 128:(k_ + 1) * 128, :], w2bf_d[k_], ("w2bf", k_)) for k_ in range(NCH)]
                pin = [xin[k_][:, 0, :] for k_ in range(2)]
                pout = []
                for k_ in range(2):
                    v_ = xin[k_][:, 1, :].bitcast(BF16)
                    pout += [v_[:, 0:1024], v_[:, 1024:2048]]
                nb = len(blocks)

                def dma_in(b_):
                    k_ = b_ % 2
                    P.dma("sp", lambda e: e.dma_start(out=pin[k_], in_=blocks[b_][0]),
                          writes=[("pin", k_)] + ([("xin", k_)] if b_ < 2 else []))

                def dma_out(b_):
                    o_ = b_ % 4
                    P.dma("sp", lambda e: e.dma_start(out=blocks[b_][1], in_=pout[o_]),
                          reads=[("pout", o_)], writes=[blocks[b_][2]])
                dma_in(0)
                for b_ in range(nb):
                    if b_ + 1 < nb:
                        dma_in(b_ + 1)
                    P.op("pool", (lambda b_: lambda e: e.tensor_copy(out=pout[b_ % 4], in_=pin[b_ % 2]))(b_),
                         reads=[("pin", b_ % 2)], writes=[("pout", b_ % 4)] + ([("xin", b_ % 2)] if b_ < 2 else []))
                    if b_ >= 2:
                        dma_out(b_ - 2)
                    for _ in range(6):
                        yield
                dma_out(nb - 2)
                dma_out(nb - 1)
                yield

            pre_g = precast_gen()

            def take(g_, n_):
                for _ in range(n_):
                    try:
                        next(g_)
                    except StopIteration:
                        return
                    yield

            run(na_proj(0))
            for hp in range(4):
                if hp < 3:
                    interleave(na_attn(hp), gen_interleave(na_proj(hp + 1), take(pre_g, 36)), ratio=1)
                else:
                    interleave(na_attn(hp), gen_interleave(proj_fm(3072, 32, zT, ("zT",)), take(pre_g, 36)), ratio=1)
            run(pre_g)

            A.reset(mNA)
            P.barrier()
            A.hi = hiA
            if stage <= 2:
                raise _Stop()

            mG = A.mark()
            maskfb = A.alloc(F32, (4, 128))
            qg = A.alloc(F32, (UTW,))
            kg = A.alloc(F32, (UTW,))
            vg2 = [A.alloc(BF16, (17, 256)) for _ in range(2)]
            sg2 = [A.alloc(BF16, (17, 256)) for _ in range(2)]
            qdec = A.alloc(BF16, (2, UTW))
            kinvT = A.alloc(BF16, (2, UTW))
            kinv_tm = A.alloc(BF16, (17, 256))
            Sst = [A.alloc(BF16, (34, 128)) for _ in range(2)]
            dec = A.alloc(F32, (2, 34))
            Tst = [[A.alloc(F32, (128,)) for _ in range(2)] for _d in range(2)]
            lae = [A.alloc(F32, (256,)) for _ in range(3)]
            lap = [A.alloc(F32, (256,)) for _ in range(3)]
            e1 = [A.alloc(F32, (2, 128)) for _ in range(3)]
            e2 = [A.alloc(F32, (2, 128)) for _ in range(3)]
            ATb = [A.alloc(BF16, (4, 128)) for _ in range(2)]
            gst = A.alloc(F32, (3, 4))
            P.dma("sp", lambda e: e.dma_start(out=maskfb, in_=mask_d.rearrange("p (a b) -> p a b", a=4)), writes=[("maskfb",)])

            psA_ = [bank(2), bank(3)]
            psOg_bank = [[0, 1], [4, 5]]
            psOg = [[bank(b_) for b_ in row] for row in psOg_bank]


            def chunk_loc(n):
                if n == 0:
                    return 16, 0, 16
                t = (n - 1) // 2
                a = (n - 1) % 2
                return t, a * 64, 64

            def gla_proj_fm(hp):
                yield from proj_fm(1536 + hp * 128, 128, qg, ("qg",))
                yield from proj_fm(1792 + hp * 128, 128, kg, ("kg",))

            def gla_proj_tm(hp):
                vg, sg = vg2[hp % 2], sg2[hp % 2]

                def evv(i, rows, ps, pkey):
                    evac_copy(vg[:rows, i, :], ps, [pkey], [("vg", hp % 2, i)])
                yield from proj_tm(2048 + hp * 256, 256, evv)

                def evg(i, rows, ps, pkey):
                    evac_copy(sg[:rows, i, :], ps, [pkey], [("sg", hp % 2, i)], func=AF.Silu)
                yield from proj_tm(2560 + hp * 256, 256, evg)

            def gla_prep(hp, tiles, par):
                bLA, bBT, bKT = ((0, 1, 0), (4, 5, 1), (2, 3, 0))[par]
                pLA, kLA = bank(bLA), ("pb", bLA)
                pBT, kBT = bank(bBT), ("pb", bBT)
                pKT, kKT = bank(6 + bKT), ("psP", bKT)
                sl = par
                for i in tiles:
                    c0, rows = tile_cols(i)
                    zreads = [("zT", i), ("zT1",), ("upsb",)]

                    def mmla(e, c0=c0, rows=rows):
                        e.matmul(pLA[0:rows, 0:128], lhsT=zT[0:33, c0:c0 + rows], rhs=upsb[0:33, hp * 128:(hp + 1) * 128],
                                 start=True, stop=True)
                        return e.matmul(pLA[0:rows, 128:256], lhsT=zT[0:33, c0:c0 + rows],
                                        rhs=upsb[0:33, 256 + hp * 128:256 + (hp + 1) * 128], start=True, stop=True)
                    P.op("pe", mmla, reads=zreads, writes=[kLA])
                    yield
                    P.op("act", (lambda rows: lambda e: e.activation(out=lae[sl][:rows, :], in_=pLA[0:rows, 0:256],
                                                                      func=AF.Exp, scale=-1.0))(rows),
                         reads=[kLA], writes=[("lae", sl)])
                    P.op("act", (lambda rows: lambda e: e.activation(out=lap[sl][:rows, :], in_=lae[sl][:rows, :],
                                                                      func=AF.Ln, bias=1.0))(rows),
                         reads=[("lae", sl)], writes=[("lap", sl)])
                    yield

                    def mmbt(e, rows=rows):
                        e.matmul(pBT[:, 0:rows], lhsT=lap[sl][:rows, 0:128], rhs=tri[:rows, 0:rows],
                                 start=True, stop=True)
                        return e.matmul(pBT[:, 128:128 + rows], lhsT=lap[sl][:rows, 128:256], rhs=tri[:rows, 128:128 + rows],
                                        start=True, stop=True)
                    P.op("pe", mmbt, reads=[("lap", sl), ("tri",)], writes=[kBT])
                    yield
                    bt = pBT[:, 0:256].rearrange("p (a b) -> p a b", a=2)[:, :, 0:rows]
                    P.op("act", (lambda rows, bt: lambda e: e.activation(out=e1[sl][:, :, 0:rows], in_=bt, func=AF.Exp))(rows, bt),
                         reads=[kBT], writes=[("e1", sl)])
                    P.op("act", (lambda rows, bt: lambda e: e.activation(out=e2[sl][:, :, 0:rows], in_=bt, func=AF.Exp,
                                                                          scale=-1.0))(rows, bt),
                         reads=[kBT], writes=[("e2", sl)])
                    yield
                    for d_ in range(2):
                        P.op("dve", (lambda rows, c0, d_: lambda e: e.scalar_tensor_tensor(
                            out=qdec[:, d_, c0:c0 + rows], in0=qg[:, c0:c0 + rows], scalar=0.125, in1=e1[sl][:, d_, 0:rows],
                            op0=ALU.mult, op1=ALU.mult))(rows, c0, d_),
                            reads=[("qg", i), ("e1", sl)], writes=[("qdec", i)])
                        P.op("pool", (lambda rows, c0, d_: lambda e: e.tensor_tensor(
                            out=kinvT[:, d_, c0:c0 + rows], in0=kg[:, c0:c0 + rows], in1=e2[sl][:, d_, 0:rows],
                            op=ALU.mult))(rows, c0, d_),
                            reads=[("kg", i), ("e2", sl)], writes=[("kinvT", i, d_)])
                    if i < 16:
                        n1 = 2 * i + 1
                        P.op("dve", (lambda n1: lambda e: e.tensor_copy(
                            out=dec[:, 0, n1:n1 + 2], in_=e1[sl][:, 0, :].rearrange("p (a b) -> p a b", a=2)[:, :, 63]))(n1),
                            reads=[("e1", sl)], writes=[("dec", i, 0)])
                        P.op("dve", (lambda n1: lambda e: e.tensor_copy(
                            out=dec[:, 1, n1:n1 + 2], in_=e1[sl][:, 1, :].rearrange("p (a b) -> p a b", a=2)[:, :, 0]))(n1),
                            reads=[("e1", sl)], writes=[("dec", i, 1)])
                    else:
                        P.op("dve", lambda e: e.tensor_copy(out=dec[:, 0, 0:1], in_=e1[sl][:, 0, 15:16]),
                             reads=[("e1", sl)], writes=[("dec", i, 0)])
                        P.op("dve", lambda e: e.tensor_copy(out=dec[:, 1, 0:1], in_=e1[sl][:, 1, 0:1]),
                             reads=[("e1", sl)], writes=[("dec", i, 1)])
                    yield
                    pkt = pKT.bitcast(BF16)

                    def trk(e, rows=rows, c0=c0, pkt=pkt):
                        e.transpose(out=pkt[0:rows, 0:128], in_=kinvT[:, 0, c0:c0 + rows], identity=ident[:, :])
                        return e.transpose(out=pkt[0:rows, 128:256], in_=kinvT[:, 1, c0:c0 + rows], identity=ident[:, :])
                    P.op("pe", trk, reads=[("kinvT", i, 0), ("kinvT", i, 1), ("ident",)], writes=[kKT])
                    P.op("dve", (lambda rows, i, pkt: lambda e: e.tensor_copy(out=kinv_tm[:rows, i, :], in_=pkt[0:rows, 0:256]))(rows, i, pkt),
                         reads=[kKT], writes=[("kinv_tm", i)])
                    yield

            def gla_scan(hp, d_):
                vg = vg2[hp % 2]
                order = list(range(33)) if d_ == 0 else list(range(32, -1, -1))
                banks_ = (4, 5) if d_ == 0 else (2, 3)
                prev = None
                for step, n in enumerate(order):
                    t, r0, rows = chunk_loc(n)
                    bk = banks_[step % 2]
                    pD = bank(bk)[:, 0:128]

                    def mmds(e, pD=pD, t=t, r0=r0, rows=rows):
                        ins = None
                        for hl in range(2):
                            ins = e.matmul(pD[hl * 64:(hl + 1) * 64, :],
                                           lhsT=kinv_tm[r0:r0 + rows, t, d_ * 128 + hl * 64:d_ * 128 + (hl + 1) * 64],
                                           rhs=vg[r0:r0 + rows, t, hl * 128:(hl + 1) * 128],
                                           start=True, stop=True, tile_position=(r0, hl * 64))
                        return ins
                    P.op("pe", mmds, reads=[("kinv_tm", t), ("vg", hp % 2, t)], writes=[("pb", bk)])
                    tn = step % 2
                    if prev is None:
                        P.op("dve", (lambda pD, tn: lambda e: e.tensor_copy(out=Tst[d_][tn], in_=pD))(pD, tn),
                             reads=[("pb", bk)], writes=[("T", d_, tn)])
                    else:
                        P.op("dve", (lambda pD, tn, prev: lambda e: e.scalar_tensor_tensor(
                            out=Tst[d_][tn], in0=Tst[d_][1 - tn], scalar=dec[:, d_, prev:prev + 1], in1=pD,
                            op0=ALU.mult, op1=ALU.add))(pD, tn, prev),
                            reads=[("pb", bk), ("T", d_, 1 - tn), ("dec", chunk_loc(prev)[0], d_)], writes=[("T", d_, tn)])
                    if step < 32:
                        P.op("act", (lambda tn, n: lambda e: e.activation(out=Sst[d_][:, n, :], in_=Tst[d_][tn], func=AF.Copy,
                                                                          scale=dec[:, d_, n:n + 1]))(tn, n),
                             reads=[("T", d_, tn), ("dec", chunk_loc(n)[0], d_)], writes=[("S", d_, n)])
                    prev = n
                    yield

            def gla_out(hp, tiles=None):
                tiles = tiles if tiles is not None else list(range(16)) + [16]
                vg, sg = vg2[hp % 2], sg2[hp % 2]

                def stageA(t):
                    i = tiles[t]
                    c0, rows = tile_cols(i)
                    sl = t % 2

                    for hl in range(2):
                        def mmA(e, hl=hl):
                            ins = None
                            for d_ in range(2):
                                ins = e.matmul(psA_[hl][0:rows, d_ * 128:d_ * 128 + rows],
                                               lhsT=kinvT[hl * 64:(hl + 1) * 64, d_, c0:c0 + rows],
                                               rhs=qdec[hl * 64:(hl + 1) * 64, d_, c0:c0 + rows], start=True, stop=True)
                            return ins
                        P.op("pe", mmA, reads=[("kinvT", i, 0), ("kinvT", i, 1), ("qdec", i)], writes=[("pb", 2 + hl)])
                        pav = psA_[hl][:, 0:256].rearrange("p (a b) -> p a b", a=2)
                        P.op("dve", (lambda pav, hl: lambda e: e.tensor_tensor(
                            out=ATb[sl][:rows, 2 * hl:2 * hl + 2, 0:rows], in0=pav[0:rows, :, 0:rows],
                            in1=maskfb[:rows, 0:2, 0:rows], op=ALU.mult))(pav, hl),
                            reads=[("pb", 2 + hl), ("maskfb",)], writes=[("AT", sl, hl)])

                def stageB(t):
                    i = tiles[t]
                    c0, rows = tile_cols(i)
                    sl = t % 2
                    if i < 16:
                        chunks = [(2 * i + 1, 0), (2 * i + 2, 64)]
                    else:
                        chunks = [(0, 0)]
                    sreads = []
                    for (n, _r) in chunks:
                        if n >= 1:
                            sreads.append(("S", 0, n - 1))
                        if n <= 31:
                            sreads.append(("S", 1, n + 1))
                    for hl in range(2):
                        pO = psOg[sl][hl]
                        pkey = ("pb", psOg_bank[sl][hl])
                        oc = slice(hl * 128, (hl + 1) * 128)

                        def mmO(e, pO=pO, hl=hl, oc=oc):
                            extra = []
                            for (n, r0) in chunks:
                                cr = 64 if i < 16 else 16
                                if n >= 1:
                                    extra.append((0, n - 1, r0, cr))
                                if n <= 31:
                                    extra.append((1, n + 1, r0, cr))
                            e.matmul(pO[0:rows, 0:128], lhsT=ATb[sl][:rows, hl * 2, 0:rows], rhs=vg[:rows, i, oc],
                                     start=True, stop=False)
                            ins = e.matmul(pO[0:rows, 0:128], lhsT=ATb[sl][:rows, hl * 2 + 1, 0:rows], rhs=vg[:rows, i, oc],
                                           start=False, stop=(len(extra) == 0))
                            for xi, (d_, sn, r0, cr) in enumerate(extra):
                                last_of_range = (xi == len(extra) - 1) or (extra[xi + 1][2] != r0)
                                ins = e.matmul(pO[r0:r0 + cr, 0:128], lhsT=qdec[hl * 64:(hl + 1) * 64, d_, c0 + r0:c0 + r0 + cr],
                                               rhs=Sst[d_][hl * 64:(hl + 1) * 64, sn, :], start=False,
                                               stop=last_of_range, tile_position=(hl * 64, r0))
                            return ins
                        P.op("pe", mmO, reads=[("AT", sl, hl), ("vg", hp % 2, i), ("qdec", i)] + sreads, writes=[pkey])
                        gi = sl * 2 + hl
                        P.op("act", (lambda pO, gi: lambda e: e.activation(
                            out=junk[:rows, 0:128], in_=pO[0:rows, 0:128], func=AF.Square,
                            accum_out=gst[:rows, 0, gi:gi + 1]))(pO, gi),
                            reads=[pkey], writes=[("junk",), ("G", gi, "ss")])
                        rstd_ops(gst[:rows, 0, gi:gi + 1], gst[:rows, 1, gi:gi + 1], gst[:rows, 2, gi:gi + 1], 128, ("G", gi))
                        head = 2 * hp + hl
                        P.op("dve", (lambda pO, hl, head, gi: lambda e: e.scalar_tensor_tensor(
                            out=mix[:rows, i, 512 + head * 128:512 + (head + 1) * 128], in0=pO[0:rows, 0:128],
                            scalar=gst[:rows, 2, gi:gi + 1], in1=sg[:rows, i, hl * 128:(hl + 1) * 128],
                            op0=ALU.mult, op1=ALU.mult))(pO, hl, head, gi),
                            reads=[pkey, ("G", gi, "rstd"), ("sg", hp % 2, i)], writes=[("mix", i, "g", head)])

                n = len(tiles)
                for t in range(n + 1):
                    if t < n:
                        stageA(t)
                    if t >= 1:
                        stageB(t - 1)
                    yield

            def prep_all(hp):
                return gen_interleave(gen_interleave(gla_prep(hp, [16, 0, 3, 6, 9, 12, 15], 0),
                                                     gla_prep(hp, [1, 4, 7, 10, 13], 1)),
                                      gla_prep(hp, [2, 5, 8, 11, 14], 2))

            run(gla_proj_fm(0))
            interleave(prep_all(0), gla_proj_tm(0), ratio=1)
            interleave(gen_chain(gen_interleave(gla_scan(0, 0), gla_scan(0, 1)), gla_out(0)),
                       gen_chain(gla_proj_fm(1), gla_proj_tm(1)), ratio=1)
            def lagged(main_g, side_g, lead):
                k_ = 0
                for _ in main_g:
                    k_ += 1
                    if k_ > lead:
                        try:
                            next(side_g)
                        except StopIteration:
                            pass
                    yield
                yield from side_g

            def scan_out(hp):
                sc = gla_scan(hp, 1)
                ou = gla_out(hp, list(range(15, -1, -1)) + [16])
                for _ in range(3):
                    next(sc)
                for _ in ou:
                    for _r in range(2):
                        try:
                            next(sc)
                        except StopIteration:
                            pass
                    yield
                yield from sc

            prep_two = gen_interleave(gla_prep(1, [16, 0, 2, 4, 6, 8, 10, 12, 14], 0),
                                      gla_prep(1, [1, 3, 5, 7, 9, 11, 13, 15], 2))
            run(lagged(prep_two, gla_scan(1, 0), 24))
            run(scan_out(1))
            A.reset(mG)
            P.barrier()
            if stage <= 3:
                raise _Stop()

            A.reset(mB2)
            h1 = A.alloc(F32, (17, 1024), top=True)
            dump_reg["h1"] = (h1, F32)
            mC = A.mark()
            wo_sb = A.alloc(BF16, (8, 1024))
            wstg = [A.alloc(F32, (1024,)) for _ in range(2)]
            grepP = A.alloc(F32, (1024,))
            grepF = A.alloc(F32, (1024,))
            tmpE = [A.alloc(F32, (1024,)) for _ in range(2)]
            xs2 = [A.alloc(BF16, (1024,)) for _ in range(3)]
            P.dma("sp", lambda e: e.dma_start(out=grepP, in_=grep_d[1]), writes=[("grep", "P")])
            P.dma("sp", lambda e: e.dma_start(out=grepF, in_=grep_d[2]), writes=[("grep", "F")])
            for k in range(8):
                sl = k % 2
                P.dma("sp" if k % 2 == 0 else "act",
                      (lambda k, sl: lambda e: e.dma_start(out=wstg[sl], in_=w_o_d[k * 128:(k + 1) * 128, :]))(k, sl),
                      writes=[("wstg", sl)])
                P.op("act" if k % 2 == 0 else "dve",
                     (lambda k, sl: lambda e: e.activation(out=wo_sb[:, k, :], in_=wstg[sl], func=AF.Copy,
                                                           scale=gfm[:, k:k + 1])) (k, sl) if k % 2 == 0 else
                     (lambda k, sl: lambda e: e.tensor_scalar(out=wo_sb[:, k, :], in0=wstg[sl], scalar1=gfm[:, k:k + 1],
                                                              scalar2=None, op0=ALU.mult))(k, sl),
                     reads=[("wstg", sl), ("gfm",)], writes=[("wo_sb", k)])
            P.dma("sp", lambda e: e.dma_start(out=h1[:16, 16, :], in_=meta_d), writes=[("h1", 16)])
            for pr in range(8):
                srcd = x_d[pr * 256:(pr + 1) * 256, :].rearrange("(t p) d -> p t d", p=128)
                P.dma("sp" if pr % 2 == 0 else "act",
                      (lambda pr, srcd: lambda e: e.dma_start(out=h1[:, 2 * pr:2 * pr + 2, :], in_=srcd))(pr, srcd),
                      writes=[("h1", 2 * pr), ("h1", 2 * pr + 1)])
            for i in range(17):
                c0, rows = tile_cols(i)
                sl = i % 2
                psv = bank(sl).bitcast(BF16)

                def trm(e, i=i, rows=rows, psv=psv):
                    ins = None
                    for k in range(8):
                        ins = e.transpose(out=psv[:, k * 128:k * 128 + rows], in_=mix[:rows, i, k * 128:(k + 1) * 128],
                                          identity=ident[:rows, :rows])
                    return ins
                mreads = [("mix", i, "na")] + [("mix", i, "g", h) for h in range(4)] + [("ident",)]
                P.op("pe", trm, reads=mreads, writes=[("ps", sl)])
                src_v = psv.rearrange("p (k c) -> p k c", k=8)[:, :, 0:rows]
                evac_copy(uT[:, :, c0:c0 + rows], src_v, [("ps", sl)], [("uT", i)])
            psY = [bank(2, 2), bank(4, 2)]

            finC = [None]

            def u2_tile(j):
                _c0, rj = tile_cols(j)
                fin_ = norm_transpose(j, h1[:rj, j, :], ("h1", j), 16, xs2[j % 3], ("xs2", j % 3),
                                      bank(6 + j % 2), ("psT2", j % 2), 3, "F")
                if finC[0] is not None:
                    finC[0]()
                finC[0] = fin_
            for i in range(17):
                c0, rows = tile_cols(i)
                sl = i % 2
                pY = psY[sl]

                def mmY(e, c0=c0, rows=rows, pY=pY):
                    ins = None
                    for nh in range(2):
                        for k in range(8):
                            ins = e.matmul(pY[0:rows, nh * 512:(nh + 1) * 512], lhsT=uT[:, k, c0:c0 + rows],
                                           rhs=wo_sb[:, k, nh * 512:(nh + 1) * 512], start=(k == 0), stop=(k == 7))
                    return ins
                P.op("pe", mmY, reads=[("uT", i)] + [("wo_sb", k) for k in range(8)], writes=[("psY", sl)])
                if i >= 2:
                    u2_tile(i - 2)
                ss = stat[:rows, 0, i:i + 1]
                ln = stat[:rows, 1, i:i + 1]
                rs = stat[:rows, 2, i:i + 1]
                P.op("act", (lambda rows, pY, ss: lambda e: e.activation(out=junk[:rows, :], in_=pY[0:rows, :], func=AF.Square,
                                                                          accum_out=ss))(rows, pY, ss),
                     reads=[("psY", sl)], writes=[("junk",), ("C", i, "ss")])
                rstd_ops(ss, ln, rs, D, ("C", i))
                P.op("dve", (lambda sl, rows, pY, rs: lambda e: e.scalar_tensor_tensor(
                    out=tmpE[sl][:rows, :], in0=pY[0:rows, :], scalar=rs, in1=grepP[:rows, :], op0=ALU.mult, op1=ALU.mult))(sl, rows, pY, rs),
                    reads=[("psY", sl), ("C", i, "rstd"), ("grep", "P")], writes=[("tmpE", sl)])
                P.op("pool", (lambda sl, rows, i: lambda e: e.tensor_tensor(out=h1[:rows, i, :], in0=h1[:rows, i, :],
                                                                             in1=tmpE[sl][:rows, :], op=ALU.add))(sl, rows, i),
                     reads=[("tmpE", sl), ("h1", i)], writes=[("h1", i)])
            u2_tile(15)
            u2_tile(16)
            finC[0]()
            A.reset(mB)
            P.barrier()
            if stage <= 4:
                raise _Stop()

            hid = A.alloc(BF16, (NCH, 1024))
            cw = A.alloc(F32, (44, 4))
            grepO = A.alloc(F32, (1024,))
            wf = [A.alloc(BF16, (2, 8, 128)) for _ in range(3)]
            fstg = [A.alloc(F32, (8, 128)) for _ in range(3)]
            wo2 = [A.alloc(BF16, (1024,)) for _ in range(4)]
            accV = [A.alloc(F32, (344,)) for _ in range(3)]
            accG = [A.alloc(F32, (344,)) for _ in range(3)]
            gel = [A.alloc(F32, (344,)) for _ in range(2)]
            tmpO = [A.alloc(F32, (1024,)) for _ in range(2)]
            P.dma("sp", lambda e: e.dma_start(out=cw, in_=cw_d.rearrange("p (c j) -> p c j", j=4)), writes=[("cw",)])
            P.dma("sp", lambda e: e.dma_start(out=grepO, in_=grep_d[3]), writes=[("grep", "O")])

            pb_i = [0]
            cv_i = [0]
            wf_i = [0]
            wo_i = [0]
            ep_i = [0]
            NTS = [(0, 342), (342, 342), (684, 340)]
            for hf in range(2):
                T0 = hf * 1024
                def load_wf(c, hf=hf):
                    sl_ = c % 3
                    for vg_ in range(2):
                        blk_ = vg_ * NCH + c
                        scr = w1bf_d[blk_].rearrange("p (k c) -> p k c", k=8)
                        if hf == 0:
                            srcap = w_f1_d[blk_].rearrange("p (k c) -> p k c", k=8)
                            load_cast(srcap, wf[sl_][:, vg_, :, :], ("wf", sl_, vg_), fstg, eng=("act" if vg_ == 0 else "dve"))
                            P.dma("act" if vg_ == 0 else "sp",
                                  (lambda sl_, vg_, scr: lambda e: e.dma_start(out=scr, in_=wf[sl_][:, vg_, :, :]))(sl_, vg_, scr),
                                  reads=[("wf", sl_, vg_)], writes=[("w1bf", blk_)])
                        else:
                            P.dma("sp", (lambda sl_, vg_, scr: lambda e: e.dma_start(out=wf[sl_][:, vg_, :, :], in_=scr))(sl_, vg_, scr),
                                  reads=[("w1bf", blk_)], writes=[("wf", sl_, vg_)])
                def tail_ops(cs, n, c, o0):
                    P.op("act", lambda e: e.activation(out=gel[cs % 2][:, 0:n], in_=accG[cs][:, 0:n], func=AF.Gelu_apprx_tanh),
                         reads=[("acc", cs, 1)], writes=[("gel", cs % 2)])
                    P.op("pool", lambda e: e.tensor_tensor(out=hid[:, c, o0:o0 + n], in0=accV[cs][:, 0:n],
                                                           in1=gel[cs % 2][:, 0:n], op=ALU.mult),
                         reads=[("acc", cs, 0), ("gel", cs % 2)], writes=[("hid", c, o0)])
                pending = [None]
                load_wf(0)
                for c in range(NCH):
                    sl = c % 3
                    if c + 1 < NCH:
                        load_wf(c + 1)
                    for (o0, n) in NTS:
                        ci = 16 + T0 + o0 - 1
                        nin = n + 2
                        tl = tiles_overlapping(ci, min(nin, 2064 - ci))
                        pbs = []
                        for vg_ in range(2):
                            pb = pb_i[0] % 8
                            pb_i[0] += 1
                            pbs.append(pb)
                            ps = bank(pb)

                            def mm(e, sl=sl, vg_=vg_, ci=ci, nin=nin, ps=ps):
                                ins = None
                                for k in range(8):
                                    ins = e.matmul(ps[:, 0:nin], lhsT=wf[sl][:, vg_, k, :],
                                                   rhs=uT[:, k, ci:ci + nin], start=(k == 0), stop=(k == 7))
                                return ins
                            P.op("pe", mm, reads=[("wf", sl, vg_), ("uTpad",)] + [("uT", t) for t in tl],
                                 writes=[("psF", pb)])
                        cs = cv_i[0] % 3
                        cv_i[0] += 1
                        for vg_, acc in ((0, accV[cs]), (1, accG[cs])):
                            ps = bank(pbs[vg_])
                            cc = vg_ * NCH + c
                            P.op("act", (lambda ps, acc, cc, n: lambda e: e.activation(
                                out=acc[:, 0:n], in_=ps[:, 1:n + 1], func=AF.Identity, scale=cw[:, cc, 1:2], bias=cw[:, cc, 3:4]))(ps, acc, cc, n),
                                reads=[("psF", pbs[vg_]), ("cw",)], writes=[("acc", cs, vg_)])
                            P.op("dve", (lambda ps, acc, cc, n: lambda e: e.scalar_tensor_tensor(
                                out=acc[:, 0:n], in0=ps[:, 0:n], scalar=cw[:, cc, 0:1], in1=acc[:, 0:n],
                                op0=ALU.mult, op1=ALU.add))(ps, acc, cc, n),
                                reads=[("psF", pbs[vg_]), ("cw",), ("acc", cs, vg_)], writes=[("acc", cs, vg_)])
                            P.op("dve", (lambda ps, acc, cc, n: lambda e: e.scalar_tensor_tensor(
                                out=acc[:, 0:n], in0=ps[:, 2:n + 2], scalar=cw[:, cc, 2:3], in1=acc[:, 0:n],
                                op0=ALU.mult, op1=ALU.add))(ps, acc, cc, n),
                                reads=[("psF", pbs[vg_]), ("cw",), ("acc", cs, vg_)], writes=[("acc", cs, vg_)])
                        if pending[0] is not None:
                            pending[0]()
                        pending[0] = (lambda cs, n, c, o0: lambda: tail_ops(cs, n, c, o0))(cs, n, c, o0)
                if pending[0] is not None:
                    pending[0]()
                    pending[0] = None
                def load_wo(q):
                    k_ = q % NCH
                    P.dma("sp", (lambda q, k_: lambda e: e.dma_start(out=wo2[q % 4], in_=w2bf_d[k_]))(q, k_),
                          reads=[("w2bf", k_)], writes=[("wo2", q % 4)])

                qbase = wo_i[0]
                nq_ = 2 * NCH
                load_wo(qbase)
                load_wo(qbase + 1)
                load_wo(qbase + 2)
                for blk in range(2):
                    for k in range(NCH):
                        q = wo_i[0]
                        ws = q % 4
                        wo_i[0] += 1
                        if q + 3 < qbase + nq_:
                            load_wo(q + 3)

                        for tl_ in range(4):
                            tc0 = (blk * 4 + tl_) * 128

                            def mm2(e, ws=ws, k=k, tl_=tl_, tc0=tc0):
                                ins = None
                                for nh in range(2):
                                    ins = e.matmul(bank(tl_ * 2 + nh), lhsT=hid[:, k, tc0:tc0 + 128],
                                                   rhs=wo2[ws][:, nh * 512:(nh + 1) * 512], start=(k == 0), stop=(k == NCH - 1))
                                return ins
                            P.op("pe", mm2, reads=[("wo2", ws)] + [("hid", k, o0) for (o0, _n) in NTS],
                                 writes=[("psF", tl_ * 2), ("psF", tl_ * 2 + 1)])
                    for tl_ in range(4):
                        i = hf * 8 + blk * 4 + tl_
                        pY = bank(tl_ * 2, 2)
                        es_ = ep_i[0] % 2
                        ep_i[0] += 1
                        ss = stat[:, 0, i:i + 1]
                        ln = stat[:, 1, i:i + 1]
                        rs = stat[:, 2, i:i + 1]
                        pk = [("psF", tl_ * 2), ("psF", tl_ * 2 + 1)]
                        P.op("act", (lambda pY, ss: lambda e: e.activation(out=junk[:, :], in_=pY, func=AF.Square, accum_out=ss))(pY, ss),
                             reads=pk, writes=[("junk",), ("O", i, "ss")])
                        rstd_ops(ss, ln, rs, D, ("O", i))
                        P.op("dve", (lambda es_, pY, rs: lambda e: e.scalar_tensor_tensor(
                            out=tmpO[es_], in0=pY, scalar=rs, in1=grepO, op0=ALU.mult, op1=ALU.mult))(es_, pY, rs),
                            reads=pk + [("O", i, "rstd"), ("grep", "O")], writes=[("tmpO", es_)])
                        P.op("dve" if (hf == 1 and blk == 1 and tl_ % 2 == 1) else "pool",
                             (lambda es_, i: lambda e: e.tensor_tensor(out=h1[:, i, :], in0=h1[:, i, :], in1=tmpO[es_],
                                                                       op=ALU.add))(es_, i),
                             reads=[("tmpO", es_), ("h1", i)], writes=[("h1", i)])
                        P.dma("sp", (lambda i: lambda e: e.dma_start(out=y_d[i * 128:(i + 1) * 128, :], in_=h1[:, i, :]))(i),
                              reads=[("h1", i)])
        except _Stop:
            pass
        dump_all()
        P.finish()
        print("SBUF peak bytes/partition:", A.peak, " ops:", {e: len(P.streams[e]) for e in ENGS})

        @block.tensor
        def _(e):
            P.emit("pe", e, sems)

        @block.scalar
        def _(e):
            P.emit("act", e, sems)

        @block.vector
        def _(e):
            P.emit("dve", e, sems)

        @block.gpsimd
        def _(e):
            P.emit("pool", e, sems)

        @block.sync
        def _(e):
            P.emit("sp", e, sems)
    return nc


def _na_tables(rpb):
    H = rpb.shape[0]
    cq = np.arange(64)
    cs = np.clip(cq - 8, 0, 48)
    ck = np.arange(64)
    in_win = (ck[None, :] >= cs[:, None]) & (ck[None, :] < cs[:, None] + 16)
    dc = np.clip(ck[:, None] - cq[None, :], -15, 15) + 15
    G = rpb[:, :, dc]
    G = np.where(in_win.T[None, None], G, np.float32(NEG)).astype(np.float32)
    tabs = np.full((H, 128, 1664), NEG, np.float32)
    for a in range(2):
        for u in range(16):
            dr = a + 14 - u
            if 0 <= dr <= 14:
                tabs[:, a * 64:(a + 1) * 64, u * 64:(u + 1) * 64] = G[:, dr]
    for c in range(5):
        for b in range(2):
            u = 3 + 2 * c + b
            for a in range(2):
                dr = a + 14 - u
                if 3 <= dr <= 10:
                    tabs[:, a * 64:(a + 1) * 64, 1024 + (2 * c + b) * 64:1024 + (2 * c + b + 1) * 64] = G[:, dr]
    return tabs


def _consts():
    s = np.arange(128)
    same = (s[:, None] // 64) == (s[None, :] // 64)
    trif = np.where(same & (s[:, None] <= s[None, :]), -1.0 / 16.0, 0.0)
    trib = np.where(same & (s[:, None] >= s[None, :]), -1.0 / 16.0, 0.0)
    tri = np.concatenate([trif, trib], axis=1).astype(np.float32)
    mf = (same & (s[:, None] <= s[None, :])).astype(np.float32)
    mb = (same & (s[:, None] > s[None, :])).astype(np.float32)
    maskfb = np.concatenate([mf, mb, mf, mb], axis=1).astype(np.float32)
    ident = np.eye(128, dtype=np.float32)
    return ident, tri, maskfb


def _tile_weights(w_in, w_f1):
    wp = np.zeros((1024, 3200), np.float32)
    wp[:, :3104] = w_in
    w_in_t = np.ascontiguousarray(wp.reshape(8, 128, 25, 128).transpose(2, 1, 0, 3)).reshape(25, 128, 1024)
    w_f1_t = np.ascontiguousarray(w_f1.reshape(8, 128, 44, 128).transpose(2, 1, 0, 3)).reshape(44, 128, 1024)
    return w_in_t, w_f1_t


_NC_CACHE = {}


def kernel(x, meta_tokens, norm_mix_pre, w_in, na_rel_bias, na_out_gain, gla_gate_up_fwd,
           gla_gate_bias_fwd, gla_gate_up_bwd, gla_gate_bias_bwd, gla_out_gain, w_o, norm_mix_post,
           norm_ffn_pre, w_ffn_in, ffn_conv_w, ffn_conv_b, w_ffn_out, norm_ffn_post):
    f = lambda a: np.ascontiguousarray(np.asarray(a, dtype=np.float32))
    x = f(x)
    B = x.shape[0]
    up33 = np.zeros((33, 512), np.float32)
    up33[0:16, 0:256] = f(gla_gate_up_fwd)[0]
    up33[16:32, 256:512] = f(gla_gate_up_bwd)[0]
    up33[32, 0:256] = f(gla_gate_bias_fwd)[0]
    up33[32, 256:512] = f(gla_gate_bias_bwd)[0]
    tabs = _na_tables(f(na_rel_bias)[0])
    gvec_o = np.concatenate([f(na_out_gain)[0], np.tile(f(gla_out_gain)[0], 4)])
    gfm = f(np.concatenate([gvec_o.reshape(8, 128).T, f(norm_mix_pre)[0].reshape(8, 128).T,
                            f(norm_ffn_pre)[0].reshape(8, 128).T], axis=1))
    grep = np.stack([np.broadcast_to(f(v)[0][None, :], (128, D)) for v in
                     (norm_mix_pre, norm_mix_post, norm_ffn_pre, norm_ffn_post)])
    grep = f(grep)
    cwj = f(ffn_conv_w)[0].reshape(3, 44, 128)
    cbj = f(ffn_conv_b)[0].reshape(1, 44, 128)
    cw = f(np.concatenate([cwj, cbj], axis=0).transpose(2, 1, 0).reshape(128, 44 * 4))
    ident, tri, maskfb = _consts()
    w_in_t, w_f1_t = _tile_weights(f(w_in)[0], f(w_ffn_in)[0])
    shared = {
        "meta": f(meta_tokens), "w_in_t": w_in_t, "up33": up33, "tabs": tabs, "w_o": f(w_o)[0],
        "w_f1_t": w_f1_t, "w_f2": f(w_ffn_out)[0], "cw": cw, "gfm": gfm, "grep": grep,
        "identf": ident, "tri": tri, "maskfb": maskfb,
    }
    if "nc" not in _NC_CACHE:
        _NC_CACHE["nc"] = build_program()
    nc = _NC_CACHE["nc"]
    in_maps = []
    for b in range(B):
        m = dict(shared)
        m["x"] = np.ascontiguousarray(x[b])
        in_maps.append(m)
    res = run_bass_kernel_spmd(nc, in_maps, core_ids=list(range(B)))
    out = np.stack([np.asarray(r["y"], dtype=np.float32) for r in res.results], axis=0)
    return out
```

```python
import numpy as np
import concourse.bass as bass
import concourse.mybir as mybir
from concourse.bass_utils import run_bass_kernel_spmd

F32 = mybir.dt.float32
BF16 = mybir.dt.bfloat16
AF = mybir.ActivationFunctionType
ALU = mybir.AluOpType
AX = mybir.AxisListType

D = 1024
SEQ = 2048
NMETA = 16
NT_X = 16
UTW = 2072
DFF = 2816
NCH = 22
EPS = 1e-6
NEG = -30000.0
import os
NDS = int(os.environ.get("K_NDS", "24"))
NQS = 8
ENGS = ("pe", "act", "dve", "pool", "sp")
DEBUG = False


class _Stop(Exception):
    pass


class Prog:
    def __init__(self):
        self.streams = {e: [] for e in ENGS}
        self.cnt = {e: 0 for e in ENGS}
        self.seen = {e: {} for e in ENGS}
        self.lastw = {}
        self.readers = {}
        self.ndma = 0
        self.nq = 0
        self.dma_last = {}

    def _deps(self, reads, writes):
        deps = {}

        def add(tok):
            k, v = tok
            if deps.get(k, 0) < v:
                deps[k] = v
        for r in reads:
            if r in self.lastw:
                add(self.lastw[r])
        for w in writes:
            if w in self.lastw:
                add(self.lastw[w])
            for k, v in self.readers.get(w, {}).items():
                add((k, v))
        return deps

    def _commit(self, tok, reads, writes):
        k, v = tok
        for r in reads:
            d = self.readers.setdefault(r, {})
            if d.get(k, 0) < v:
                d[k] = v
        for w in writes:
            self.lastw[w] = tok
            self.readers[w] = {}

    def _waits(self, eng, deps):
        waits = []
        for k, v in deps.items():
            if self.seen[eng].get(k, 0) < v:
                waits.append((k, v))
                self.seen[eng][k] = v
        return waits

    def op(self, eng, fn, reads=(), writes=()):
        deps = self._deps(reads, writes)
        waits = self._waits(eng, deps)
        self.cnt[eng] += 1
        tok = (eng, self.cnt[eng])
        self.streams[eng].append((waits, fn, None))
        self._commit(tok, reads, writes)
        return tok

    def dma(self, eng, fn, reads=(), writes=()):
        deps = self._deps(reads, writes)
        if eng == "pool":
            idx = ("q", self.nq)
            self.nq += 1
            assert self.nq <= NQS
            val = 16
        else:
            idx = ("d", self.ndma % NDS)
            val = 16 * (self.ndma // NDS + 1)
            self.ndma += 1
            if val > 16:
                deps[idx] = max(deps.get(idx, 0), val - 16)
        waits = self._waits(eng, deps)
        self.streams[eng].append((waits, fn, idx))
        tok = (idx, val)
        self.dma_last[idx] = val
        self._commit(tok, reads, writes)
        return tok

    def barrier(self):
        snap = dict(self.cnt)
        dsnap = dict(self.dma_last)
        for e in ENGS:
            deps = {o: snap[o] for o in ENGS if o != e and snap[o] > 0}
            for idx, v in dsnap.items():
                deps[idx] = v
            waits = self._waits(e, deps)
            if waits:
                self.streams[e].append((waits, None, None))

    def finish(self):
        deps = {idx: v for idx, v in self.dma_last.items()}
        waits = self._waits("sp", deps)
        if waits:
            self.streams["sp"].append((waits, None, None))

    def emit(self, eng, handle, sems):
        for waits, fn, dmaidx in self.streams[eng]:
            for k, v in waits:
                handle.wait_ge(sems[k], v)
            if fn is None:
                continue
            ins = fn(handle)
            if dmaidx is None:
                ins.then_inc(sems[eng], 1)
            else:
                ins.then_inc(sems[dmaidx], 16)


class Arena:
    def __init__(self, tensor, nbytes):
        self.t = tensor
        self.n = nbytes
        self.top = 0
        self.hi = nbytes
        self.peak = 0

    def mark(self):
        return self.top

    def reset(self, m):
        self.top = m

    def alloc(self, dtype, shape, top=False):
        esz = 4 if dtype == F32 else 2
        nel = int(np.prod(shape))
        nb = (nel * esz + 15) // 16 * 16
        if top:
            self.hi -= nb
            off = self.hi
        else:
            off = self.top
            self.top += nb
        self.peak = max(self.peak, self.top + (self.n - self.hi))
        assert self.top <= self.hi, f"SBUF arena overflow {self.top} > {self.hi}"
        ap = self.t[:, off // 4:(off + nb) // 4]
        if dtype == BF16:
            ap = ap.bitcast(BF16)
        ap = ap[:, 0:nel]
        if len(shape) == 2:
            ap = ap.rearrange("p (a b) -> p a b", a=shape[0])
        elif len(shape) == 3:
            ap = ap.rearrange("p (a b c) -> p a b c", a=shape[0], b=shape[1])
        return ap


def tile_cols(i):
    if i == 16:
        return 0, 16
    return 16 + 128 * i, 128


def tiles_overlapping(c0, n):
    res = []
    for i in range(17):
        a, w = tile_cols(i)
        if a < c0 + n and c0 < a + w:
            res.append(i)
    return res


FM_NT = [(0, 512), (512, 512), (1024, 512), (1536, 512), (2048, 16)]


def build_program(stage=9, dumps=()):
    nc = bass.Bass("TRN2", target_bir_lowering=False)

    def din(name, shape):
        return nc.dram_tensor(name, list(shape), F32, kind="ExternalInput").ap()

    x_d = din("x", (SEQ, D))
    meta_d = din("meta", (NMETA, D))
    w_in_d = din("w_in_t", (25, 128, 1024))
    up_d = din("up33", (33, 512))
    tabs_d = din("tabs", (8, 128, 1664))
    w_o_d = din("w_o", (D, D))
    w_f1_d = din("w_f1_t", (44, 128, 1024))
    w_f2_d = din("w_f2", (DFF, D))
    cw_d = din("cw", (128, 44 * 4))
    gfm_d = din("gfm", (128, 24))
    grep_d = din("grep", (4, 128, D))
    ident_d = din("identf", (128, 128))
    tri_d = din("tri", (128, 256))
    mask_d = din("maskfb", (128, 512))
    y_d = nc.dram_tensor("y", [SEQ, D], F32, kind="ExternalOutput").ap()
    dump_reg = {}
    w1bf_d = nc.dram_tensor("w1bf", [44, 128, 1024], BF16, kind="Internal").ap()
    w2bf_d = nc.dram_tensor("w2bf", [NCH, 128, 1024], BF16, kind="Internal").ap()
    dump_shapes = {"uT": (8 * UTW, BF16), "mix": (17 * 1024, BF16), "h1": (17 * 1024, F32), "ona": (17 * 512, F32)}
    dump_dram = {name: nc.dram_tensor("dbg_" + name, [128, dump_shapes[name][0]], dump_shapes[name][1],
                                      kind="ExternalOutput").ap() for name in dumps}

    P = Prog()
    SB_BYTES = 212000

    from contextlib import ExitStack
    with ExitStack() as es:
        sb_t = es.enter_context(nc.sbuf_tensor("arena", [128, SB_BYTES // 4], F32))
        ps_t = es.enter_context(nc.psum_tensor("psum", [128, 4096], F32))
        sems = {}
        for e in ENGS:
            sems[e] = es.enter_context(nc.semaphore("s_" + e))
        for i in range(NDS):
            sems[("d", i)] = es.enter_context(nc.semaphore("d%d" % i))
        for i in range(NQS):
            sems[("q", i)] = es.enter_context(nc.semaphore("q%d" % i))
        block = es.enter_context(nc.Block())

        A = Arena(sb_t, SB_BYTES)

        def bank(b, n=1):
            return ps_t[:, b * 512:(b + n) * 512]

        ident = A.alloc(BF16, (128,))
        tri = A.alloc(F32, (256,))
        gfm = A.alloc(F32, (24,))
        uT = A.alloc(BF16, (8, UTW))
        stat = A.alloc(F32, (8, 17))
        junk = A.alloc(BF16, (1024,))

        P.dma("pool", lambda e: e.dma_start(out=ident, in_=ident_d), writes=[("ident",)])
        P.dma("sp", lambda e: e.dma_start(out=tri, in_=tri_d), writes=[("tri",)])
        P.dma("sp", lambda e: e.dma_start(out=gfm, in_=gfm_d), writes=[("gfm",)])
        P.op("pool", lambda e: e.memset(uT[:, :, 2064:UTW], 0.0), writes=[("uTpad",)])

        def rstd_ops(ss_ap, tmp_ap, out_ap, n, rkey):
            P.op("act", lambda e: e.activation(out=tmp_ap, in_=ss_ap, func=AF.Ln, scale=1.0 / n, bias=EPS),
                 reads=[rkey + ("ss",)], writes=[rkey + ("ln",)])
            P.op("act", lambda e: e.activation(out=out_ap, in_=tmp_ap, func=AF.Exp, scale=-0.5),
                 reads=[rkey + ("ln",)], writes=[rkey + ("rstd",)])

        def norm_transpose(i, src, src_key, gcol, xs_slot, xs_key, psT, psT_key, stat_base, tag):
            c0, rows = tile_cols(i)
            ss = stat[:rows, stat_base, i:i + 1]
            ln = stat[:rows, stat_base + 1, i:i + 1]
            rs = stat[:rows, stat_base + 2, i:i + 1]
            rkey = (tag, i)
            P.op("act", lambda e: e.activation(out=junk[:rows, :], in_=src, func=AF.Square, accum_out=ss),
                 reads=[src_key], writes=[("junk",), rkey + ("ss",)])
            rstd_ops(ss, ln, rs, D, rkey)
            if tag == "A":
                P.op("dve", lambda e: e.tensor_scalar(out=xs_slot[:rows, :], in0=src, scalar1=rs, scalar2=None, op0=ALU.mult),
                     reads=[src_key, rkey + ("rstd",)], writes=[xs_key])
            else:
                P.op("act", lambda e: e.activation(out=xs_slot[:rows, :], in_=src, func=AF.Copy, scale=rs),
                     reads=[src_key, rkey + ("rstd",)], writes=[xs_key])
            psv = psT.bitcast(BF16)

            def tr(e):
                ins = None
                for k in range(8):
                    ins = e.transpose(out=psv[:, k * 128:k * 128 + rows], in_=xs_slot[:rows, k * 128:(k + 1) * 128],
                                      identity=ident[:rows, :rows])
                return ins
            P.op("pe", tr, reads=[xs_key, ("ident",)], writes=[psT_key])
            src_v = psv.rearrange("p (k c) -> p k c", k=8)[:, :, 0:rows]
            gb = gfm[:, gcol:gcol + 8].to_broadcast([128, 8, rows])

            def fin():
                P.op("dve", lambda e: e.tensor_tensor(out=uT[:, :, c0:c0 + rows], in0=src_v, in1=gb, op=ALU.mult),
                     reads=[psT_key, ("gfm",)], writes=[("uT", i)])
            return fin

        def dump_all():
            for name in dumps:
                ap, dt = dump_reg[name]
                shp = list(ap.shape)
                dd = dump_dram[name]
                src = ap
                if len(shp) == 3:
                    src = ap.rearrange("p a b -> p (a b)")
                elif len(shp) == 4:
                    src = ap.rearrange("p a b c -> p (a b c)")
                P.dma("sp", (lambda dd, src: lambda e: e.dma_start(out=dd, in_=src))(dd, src))

        dump_reg["uT"] = (uT, BF16)
        try:
            hiA = A.hi
            grepA = A.alloc(F32, (1024,), top=True)
            xin = [A.alloc(F32, (2, 1024), top=True) for _ in range(2)]
            xmeta = A.alloc(F32, (1024,), top=True)
            xs = [A.alloc(BF16, (1024,), top=True) for _ in range(2)]
            P.dma("sp", lambda e: e.dma_start(out=grepA, in_=grep_d[0]), writes=[("grep", "A")])
            P.dma("sp", lambda e: e.dma_start(out=xmeta[:16, :], in_=meta_d), writes=[("xmeta",)])
            for pr in range(2):
                srcd = x_d[pr * 256:(pr + 1) * 256, :].rearrange("(t p) d -> p t d", p=128)
                P.dma("act" if pr == 1 else "sp", (lambda pr, srcd: lambda e: e.dma_start(out=xin[pr % 2], in_=srcd))(pr, srcd),
                      writes=[("xin", pr % 2)])
            finA = [None]
            for idxA, i in enumerate([16] + list(range(16))):
                sl = idxA % 2
                c0, rows = tile_cols(i)
                if i == 16:
                    src_ap, src_key = xmeta[:16, :], ("xmeta",)
                else:
                    pr = i // 2
                    if i % 2 == 0 and pr >= 2:
                        srcd = x_d[pr * 256:(pr + 1) * 256, :].rearrange("(t p) d -> p t d", p=128)
                        P.dma("sp" if pr % 2 == 0 else "act", (lambda pr, srcd: lambda e: e.dma_start(out=xin[pr % 2], in_=srcd))(pr, srcd),
                              writes=[("xin", pr % 2)])
                    src_ap, src_key = xin[pr % 2][:, i % 2, :], ("xin", pr % 2)
                fin_ = norm_transpose(i, src_ap, src_key, 8, xs[sl], ("xs", sl),
                                      bank(6 + sl), ("psP", sl), 0, "A")
                if finA[0] is not None:
                    finA[0]()
                finA[0] = fin_
            finA[0]()
            if stage <= 1:
                raise _Stop()

            mB = A.mark()
            mix = A.alloc(BF16, (17, 1024))
            mB2 = A.mark()
            dump_reg["mix"] = (mix, BF16)
            P.op("pool", lambda e: e.memset(mix[:, 16, :], 0.0), writes=[("mix", 16, "na")] + [("mix", 16, "g", h) for h in range(4)])
            zT = A.alloc(BF16, (UTW,))
            upsb = A.alloc(BF16, (512,))
            wb = [A.alloc(BF16, (2, 8, 128)) for _ in range(3)]
            wb_i = [0]
            psP = [bank(6), bank(7)]
            psP_i = [0]
            ev_i = [0]

            P.dma("pool", lambda e: e.dma_start(out=upsb[0:33, :], in_=up_d), writes=[("upsb",)])
            P.op("pool", lambda e: e.memset(zT[32:33, :], 1.0), writes=[("zT1",)])

            def evac_copy(dst, src, reads, writes, func=None):
                if func is not None:
                    P.op("act", lambda e: e.activation(out=dst, in_=src, func=func), reads=reads, writes=writes)
                    return
                ev_i[0] += 1
                if ev_i[0] % 2 == 0:
                    P.op("act", lambda e: e.copy(out=dst, in_=src), reads=reads, writes=writes)
                else:
                    P.op("dve", lambda e: e.tensor_copy(out=dst, in_=src), reads=reads, writes=writes)

            wstg = [A.alloc(F32, (8, 128)) for _ in range(2)]
            stg_i = [0]

            def load_cast(src_ap, dst_ap, dst_key, stg, eng="pool"):
                ss_ = stg_i[0] % len(stg)
                stg_i[0] += 1
                P.dma("sp", lambda e: e.dma_start(out=stg[ss_], in_=src_ap), writes=[("stg", id(stg), ss_)])
                if eng == "act":
                    P.op("act", lambda e: e.copy(out=dst_ap, in_=stg[ss_]), reads=[("stg", id(stg), ss_)], writes=[dst_key])
                else:
                    P.op(eng, lambda e: e.tensor_copy(out=dst_ap, in_=stg[ss_]), reads=[("stg", id(stg), ss_)], writes=[dst_key])

            def load_wblk(col0, ncols):
                sl = wb_i[0] % 3
                wb_i[0] += 1
                assert col0 % 128 == 0
                nblk = (ncols + 127) // 128
                for j in range(nblk):
                    srcap = w_in_d[col0 // 128 + j].rearrange("p (k c) -> p k c", k=8)
                    load_cast(srcap, wb[sl][:, j, :, :], ("wb", sl, j), wstg)
                return sl, nblk

            def proj_fm(col0, ncols, dst, dst_key):
                sl, _nb = load_wblk(col0, ncols)
                for (c0, n) in FM_NT:
                    pb = psP_i[0] % 2
                    psP_i[0] += 1
                    ps = psP[pb]

                    def mm(e, sl=sl, c0=c0, n=n, ps=ps):
                        ins = None
                        for k in range(8):
                            ins = e.matmul(ps[0:ncols, 0:n], lhsT=wb[sl][:, 0, k, 0:ncols], rhs=uT[:, k, c0:c0 + n],
                                           start=(k == 0), stop=(k == 7))
                        return ins
                    tl = tiles_overlapping(c0, n)
                    P.op("pe", mm, reads=[("wb", sl, 0)] + [("uT", t) for t in tl], writes=[("psP", pb)])
                    evac_copy(dst[0:ncols, c0:c0 + n], ps[0:ncols, 0:n], [("psP", pb)],
                              [dst_key + (t,) for t in tl])
                    yield

            def proj_tm(col0, ncols, evac):
                sl, nblk = load_wblk(col0, ncols)
                for i in range(17):
                    c0, rows = tile_cols(i)
                    pb = psP_i[0] % 2
                    psP_i[0] += 1
                    ps = psP[pb]

                    def mm(e, sl=sl, c0=c0, rows=rows, ps=ps):
                        ins = None
                        for k in range(8):
                            ins = e.matmul(ps[0:rows, 0:ncols], lhsT=uT[:, k, c0:c0 + rows], rhs=wb[sl][:, 0:nblk, k, :],
                                           start=(k == 0), stop=(k == 7))
                        return ins
                    P.op("pe", mm, reads=[("wb", sl, j) for j in range(nblk)] + [("uT", i)], writes=[("psP", pb)])
                    evac(i, rows, ps[0:rows, 0:ncols], ("psP", pb))
                    yield

            def run(gen):
                for _ in gen:
                    pass

            def interleave(main, side, ratio=1):
                side_done = False
                for _ in main:
                    for _r in range(ratio):
                        if not side_done:
                            try:
                                next(side)
                            except StopIteration:
                                side_done = True
                if not side_done:
                    for _ in side:
                        pass

            def gen_interleave(ga, gb):
                da = db = False
                while not (da and db):
                    if not da:
                        try:
                            next(ga)
                        except StopIteration:
                            da = True
                    if not db:
                        try:
                            next(gb)
                        except StopIteration:
                            db = True
                    yield

            def gen_chain(*gens):
                for g_ in gens:
                    yield from g_

            mNA = A.mark()
            ona = A.alloc(F32, (17, 512))
            dump_reg["ona"] = (ona, F32)
            qT = [A.alloc(BF16, (UTW,)) for _ in range(2)]
            kT = [A.alloc(BF16, (UTW,)) for _ in range(2)]
            vaug = [A.alloc(BF16, (17, 2, 65)) for _ in range(2)]
            tab = [A.alloc(BF16, (2, 1664)) for _ in range(2)]
            S_sb = [A.alloc(F32, (640,)) for _ in range(2)]
            PT = [[A.alloc(BF16, (640,)) for _ in range(2)] for _h in range(2)]
            PTm = [[A.alloc(BF16, (128,)) for _ in range(2)] for _h in range(2)]
            PTq = [A.alloc(BF16, (16,)) for _ in range(2)]
            rec = A.alloc(F32, (4,))
            for b in range(2):
                P.op("pool", (lambda b: lambda e: e.memset(vaug[b][:, 16, :, :], 0.0))(b), writes=[("vz", b)])
                P.op("pool", (lambda b: lambda e: e.memset(vaug[b][:, :, :, 64:65], 1.0))(b), reads=[("vz", b)], writes=[("vones", b)])
            for hl_ in range(2):
                for par_ in range(2):
                    P.op("pool", (lambda hl_, par_: lambda e: e.memset(PTm[hl_][par_], 0.0))(hl_, par_),
                         writes=[("PTm0",), ("PTm", hl_, par_)])

            psS = [bank(0, 2), bank(2, 2)]
            psO = [bank(4), bank(5)]

            def na_proj(hp):
                b = hp % 2
                srcap = tabs_d[2 * hp:2 * hp + 2].rearrange("h p c -> p h c")
                P.dma("pool", lambda e: e.dma_start(out=tab[b], in_=srcap), writes=[("tab", b)])
                yield from proj_fm(hp * 128, 128, qT[b], ("qT", b))
                yield from proj_fm(512 + hp * 128, 128, kT[b], ("kT", b))

                def ev(i, rows, ps, pkey):
                    dst = vaug[b][:rows, i, :, 0:64]
                    evac_copy(dst, ps.rearrange("p (h d) -> p h d", h=2), [pkey, ("vones", b)], [("vaug", b, i)])
                yield from proj_tm(1024 + hp * 128, 128, ev)

            na_it = [0]

            def na_attn(hp):
                b = hp % 2
                its = []
                for i in range(16):
                    if 2 <= i <= 13:
                        js = [i + 2 - c for c in range(5)]
                        tcol = 1024
                    else:
                        lo = 0 if i < 2 else 12
                        js = [lo + 3 - c for c in range(4)]
                        tcol = (7 - 2 * (js[0] - i)) * 64
                    its.append((i, js, tcol))
                its.append((16, [], 0))

                def stageA(t):
                    i, js, tcol = its[t]
                    par = t % 2
                    if i == 16:
                        def mmM(e):
                            ins = None
                            for hl in range(2):
                                r0 = hl * 64
                                ins = e.matmul(psS[hl][0:16, 768:784], lhsT=kT[b][r0:r0 + 64, 0:16],
                                               rhs=qT[b][r0:r0 + 64, 0:16], start=True, stop=True)
                            return ins
                        P.op("pe", mmM, reads=[("kT", b, 16), ("qT", b, 16)], writes=[("psS", 0), ("psS", 1)])
                        for hl in range(2):
                            P.op("act", (lambda hl: lambda e: e.activation(out=PTq[hl][0:16, 0:16], in_=psS[hl][0:16, 768:784],
                                                                          func=AF.Exp, scale=0.125))(hl),
                                 reads=[("psS", hl)], writes=[("PTq", hl)])
                        return
                    qc0 = 16 + 128 * i
                    ncol = 128 * len(js)

                    for hl in range(2):
                        pS = psS[hl]
                        r0 = hl * 64

                        def mmS(e, pS=pS, r0=r0):
                            for c, j in enumerate(js):
                                kc0 = 16 + 128 * j
                                e.matmul(pS[:, c * 128:(c + 1) * 128], lhsT=kT[b][r0:r0 + 64, kc0:kc0 + 128],
                                         rhs=qT[b][r0:r0 + 64, qc0:qc0 + 128], start=True, stop=True)
                            return e.matmul(pS[:, 768:896], lhsT=kT[b][r0:r0 + 64, 0:128],
                                            rhs=qT[b][r0:r0 + 64, qc0:qc0 + 128], start=True, stop=True)
                        P.op("pe", mmS, reads=[("kT", b, j) for j in js] + [("kT", b, 16), ("kT", b, 0), ("qT", b, i)],
                             writes=[("psS", hl)])
                        P.op("dve", (lambda hl, pS: lambda e: e.scalar_tensor_tensor(
                            out=S_sb[hl][:, 0:ncol], in0=pS[:, 0:ncol], scalar=0.125, in1=tab[b][:, hl, tcol:tcol + ncol],
                            op0=ALU.mult, op1=ALU.add))(hl, pS),
                            reads=[("psS", hl), ("tab", b)], writes=[("S_sb", hl)])
                        P.op("act", (lambda hl: lambda e: e.activation(out=PT[hl][par][:, 0:ncol], in_=S_sb[hl][:, 0:ncol],
                                                                      func=AF.Exp))(hl),
                             reads=[("S_sb", hl)], writes=[("PT", hl, par)])
                        P.op("act", (lambda hl, pS: lambda e: e.activation(out=PTm[hl][par][0:16, :], in_=pS[0:16, 768:896],
                                                                          func=AF.Exp, scale=0.125))(hl, pS),
                             reads=[("psS", hl), ("PTm0",)], writes=[("PTm", hl, par)])

                def stageB(t):
                    i, js, tcol = its[t]
                    par = t % 2
                    for hl in range(2):
                        pO = psO[hl]
                        head = 2 * hp + hl
                        if i == 16:
                            P.op("pe", (lambda hl, pO: lambda e: e.matmul(pO[0:16, 0:65], lhsT=PTq[hl][0:16, 0:16],
                                                                          rhs=vaug[b][0:16, 16, hl, :], start=True, stop=True))(hl, pO),
                                 reads=[("PTq", hl), ("vaug", b, 16)], writes=[("psO", hl)])
                            rows = 16
                        else:
                            def mmO(e, hl=hl, pO=pO):
                                for c, j in enumerate(js):
                                    e.matmul(pO[:, 0:65], lhsT=PT[hl][par][:, c * 128:(c + 1) * 128], rhs=vaug[b][:, j, hl, :],
                                             start=(c == 0), stop=False)
                                return e.matmul(pO[:, 0:65], lhsT=PTm[hl][par][:, :], rhs=vaug[b][:, 16, hl, :],
                                                start=False, stop=True)
                            P.op("pe", mmO, reads=[("PT", hl, par), ("PTm", hl, par), ("vz", b)] + [("vaug", b, j) for j in js] + [("vaug", b, 16)],
                                 writes=[("psO", hl)])
                            rows = 128
                        P.op("dve", (lambda hl, pO, rows: lambda e: e.reciprocal(out=rec[0:rows, hl:hl + 1], in_=pO[0:rows, 64:65]))(hl, pO, rows),
                             reads=[("psO", hl)], writes=[("rec", hl)])
                        P.op("act", (lambda hl, pO, rows, head: lambda e: e.activation(
                            out=ona[0:rows, i, head * 64:(head + 1) * 64], in_=pO[0:rows, 0:64], func=AF.Copy,
                            scale=rec[0:rows, hl:hl + 1]))(hl, pO, rows, head),
                            reads=[("psO", hl), ("rec", hl)], writes=[("ona", i, head)])

                n = len(its)
                for t in range(n + 1):
                    if t < n:
                        stageA(t)
                    if t >= 1:
                        stageB(t - 1)
                    if hp == 3 and t >= 2:
                        na_final(its[t - 2][0])
                    yield
                    yield
                if hp == 3:
                    na_final(its[n - 1][0])

            def na_final(i):
                c0, rows = tile_cols(i)
                ss = stat[:rows, 3, i:i + 1]
                ln = stat[:rows, 4, i:i + 1]
                rs = stat[:rows, 5, i:i + 1]
                P.op("act", (lambda rows, i, ss: lambda e: e.activation(out=junk[:rows, 0:512], in_=ona[:rows, i, :],
                                                                         func=AF.Square, accum_out=ss))(rows, i, ss),
                     reads=[("ona", i, h) for h in range(8)], writes=[("junk",), ("N", i, "ss")])
                rstd_ops(ss, ln, rs, 512, ("N", i))
                P.op("dve", (lambda rows, i, rs: lambda e: e.tensor_scalar(out=mix[:rows, i, 0:512], in0=ona[:rows, i, :],
                                                                            scalar1=rs, scalar2=None, op0=ALU.mult))(rows, i, rs),
                     reads=[("ona", i, h) for h in range(8)] + [("N", i, "rstd")], writes=[("mix", i, "na")])
            def precast_gen():
                blocks = [(w_f2_d[k_ * 128:(k_ + 1) * 128, :], w2bf_d[k_], ("w2bf", k_)) for k_ in range(NCH)]
                pin = [xin[k_][:, 0, :] for k_ in range(2)]
                pout = []
                for k_ in range(2):
                    v_ = xin[k_][:, 1, :].bitcast(BF16)
                    pout += [v_[:, 0:1024], v_[:, 1024:2048]]
                nb = len(blocks)

                def dma_in(b_):
                    k_ = b_ % 2
                    P.dma("sp", lambda e: e.dma_start(out=pin[k_], in_=blocks[b_][0]),
                          writes=[("pin", k_)] + ([("xin", k_)] if b_ < 2 else []))

                def dma_out(b_):
                    o_ = b_ % 4
                    P.dma("sp", lambda e: e.dma_start(out=blocks[b_][1], in_=pout[o_]),
                          reads=[("pout", o_)], writes=[blocks[b_][2]])
                dma_in(0)
                for b_ in range(nb):
                    if b_ + 1 < nb:
                        dma_in(b_ + 1)
                    P.op("pool", (lambda b_: lambda e: e.tensor_copy(out=pout[b_ % 4], in_=pin[b_ % 2]))(b_),
                         reads=[("pin", b_ % 2)], writes=[("pout", b_ % 4)] + ([("xin", b_ % 2)] if b_ < 2 else []))
                    if b_ >= 2:
                        dma_out(b_ - 2)
                    for _ in range(6):
                        yield
                dma_out(nb - 2)
                dma_out(nb - 1)
                yield

            pre_g = precast_gen()

            def take(g_, n_):
                for _ in range(n_):
                    try:
                        next(g_)
                    except StopIteration:
                        return
                    yield

            run(na_proj(0))
            for hp in range(4):
                if hp < 3:
                    interleave(na_attn(hp), gen_interleave(na_proj(hp + 1), take(pre_g, 36)), ratio=1)
                else:
                    interleave(na_attn(hp), gen_interleave(proj_fm(3072, 32, zT, ("zT",)), take(pre_g, 36)), ratio=1)
            run(pre_g)

            A.reset(mNA)
            P.barrier()
            A.hi = hiA
            if stage <= 2:
                raise _Stop()

            mG = A.mark()
            maskfb = A.alloc(F32, (4, 128))
            qg = A.alloc(F32, (UTW,))
            kg = A.alloc(F32, (UTW,))
            vg2 = [A.alloc(BF16, (17, 256)) for _ in range(2)]
            sg2 = [A.alloc(BF16, (17, 256)) for _ in range(2)]
            qdec = A.alloc(BF16, (2, UTW))
            kinvT = A.alloc(BF16, (2, UTW))
            kinv_tm = A.alloc(BF16, (17, 256))
            Sst = [A.alloc(BF16, (34, 128)) for _ in range(2)]
            dec = A.alloc(F32, (2, 34))
            Tst = [[A.alloc(F32, (128,)) for _ in range(2)] for _d in range(2)]
            lae = [A.alloc(F32, (256,)) for _ in range(3)]
            lap = [A.alloc(F32, (256,)) for _ in range(3)]
            e1 = [A.alloc(F32, (2, 128)) for _ in range(3)]
            e2 = [A.alloc(F32, (2, 128)) for _ in range(3)]
            ATb = [A.alloc(BF16, (4, 128)) for _ in range(2)]
            gst = A.alloc(F32, (3, 4))
            P.dma("sp", lambda e: e.dma_start(out=maskfb, in_=mask_d.rearrange("p (a b) -> p a b", a=4)), writes=[("maskfb",)])

            psA_ = [bank(2), bank(3)]
            psOg_bank = [[0, 1], [4, 5]]
            psOg = [[bank(b_) for b_ in row] for row in psOg_bank]


            def chunk_loc(n):
                if n == 0:
                    return 16, 0, 16
                t = (n - 1) // 2
                a = (n - 1) % 2
                return t, a * 64, 64

            def gla_proj_fm(hp):
                yield from proj_fm(1536 + hp * 128, 128, qg, ("qg",))
                yield from proj_fm(1792 + hp * 128, 128, kg, ("kg",))

            def gla_proj_tm(hp):
                vg, sg = vg2[hp % 2], sg2[hp % 2]

                def evv(i, rows, ps, pkey):
                    evac_copy(vg[:rows, i, :], ps, [pkey], [("vg", hp % 2, i)])
                yield from proj_tm(2048 + hp * 256, 256, evv)

                def evg(i, rows, ps, pkey):
                    evac_copy(sg[:rows, i, :], ps, [pkey], [("sg", hp % 2, i)], func=AF.Silu)
                yield from proj_tm(2560 + hp * 256, 256, evg)

            def gla_prep(hp, tiles, par):
                bLA, bBT, bKT = ((0, 1, 0), (4, 5, 1), (2, 3, 0))[par]
                pLA, kLA = bank(bLA), ("pb", bLA)
                pBT, kBT = bank(bBT), ("pb", bBT)
                pKT, kKT = bank(6 + bKT), ("psP", bKT)
                sl = par
                for i in tiles:
                    c0, rows = tile_cols(i)
                    zreads = [("zT", i), ("zT1",), ("upsb",)]

                    def mmla(e, c0=c0, rows=rows):
                        e.matmul(pLA[0:rows, 0:128], lhsT=zT[0:33, c0:c0 + rows], rhs=upsb[0:33, hp * 128:(hp + 1) * 128],
                                 start=True, stop=True)
                        return e.matmul(pLA[0:rows, 128:256], lhsT=zT[0:33, c0:c0 + rows],
                                        rhs=upsb[0:33, 256 + hp * 128:256 + (hp + 1) * 128], start=True, stop=True)
                    P.op("pe", mmla, reads=zreads, writes=[kLA])
                    yield
                    P.op("act", (lambda rows: lambda e: e.activation(out=lae[sl][:rows, :], in_=pLA[0:rows, 0:256],
                                                                      func=AF.Exp, scale=-1.0))(rows),
                         reads=[kLA], writes=[("lae", sl)])
                    P.op("act", (lambda rows: lambda e: e.activation(out=lap[sl][:rows, :], in_=lae[sl][:rows, :],
                                                                      func=AF.Ln, bias=1.0))(rows),
                         reads=[("lae", sl)], writes=[("lap", sl)])
                    yield

                    def mmbt(e, rows=rows):
                        e.matmul(pBT[:, 0:rows], lhsT=lap[sl][:rows, 0:128], rhs=tri[:rows, 0:rows],
                                 start=True, stop=True)
                        return e.matmul(pBT[:, 128:128 + rows], lhsT=lap[sl][:rows, 128:256], rhs=tri[:rows, 128:128 + rows],
                                        start=True, stop=True)
                    P.op("pe", mmbt, reads=[("lap", sl), ("tri",)], writes=[kBT])
                    yield
                    bt = pBT[:, 0:256].rearrange("p (a b) -> p a b", a=2)[:, :, 0:rows]
                    P.op("act", (lambda rows, bt: lambda e: e.activation(out=e1[sl][:, :, 0:rows], in_=bt, func=AF.Exp))(rows, bt),
                         reads=[kBT], writes=[("e1", sl)])
                    P.op("act", (lambda rows, bt: lambda e: e.activation(out=e2[sl][:, :, 0:rows], in_=bt, func=AF.Exp,
                                                                          scale=-1.0))(rows, bt),
                         reads=[kBT], writes=[("e2", sl)])
                    yield
                    for d_ in range(2):
                        P.op("dve", (lambda rows, c0, d_: lambda e: e.scalar_tensor_tensor(
                            out=qdec[:, d_, c0:c0 + rows], in0=qg[:, c0:c0 + rows], scalar=0.125, in1=e1[sl][:, d_, 0:rows],
                            op0=ALU.mult, op1=ALU.mult))(rows, c0, d_),
                            reads=[("qg", i), ("e1", sl)], writes=[("qdec", i)])
                        P.op("pool", (lambda rows, c0, d_: lambda e: e.tensor_tensor(
                            out=kinvT[:, d_, c0:c0 + rows], in0=kg[:, c0:c0 + rows], in1=e2[sl][:, d_, 0:rows],
                            op=ALU.mult))(rows, c0, d_),
                            reads=[("kg", i), ("e2", sl)], writes=[("kinvT", i, d_)])
                    if i < 16:
                        n1 = 2 * i + 1
                        P.op("dve", (lambda n1: lambda e: e.tensor_copy(
                            out=dec[:, 0, n1:n1 + 2], in_=e1[sl][:, 0, :].rearrange("p (a b) -> p a b", a=2)[:, :, 63]))(n1),
                            reads=[("e1", sl)], writes=[("dec", i, 0)])
                        P.op("dve", (lambda n1: lambda e: e.tensor_copy(
                            out=dec[:, 1, n1:n1 + 2], in_=e1[sl][:, 1, :].rearrange("p (a b) -> p a b", a=2)[:, :, 0]))(n1),
                            reads=[("e1", sl)], writes=[("dec", i, 1)])
                    else:
                        P.op("dve", lambda e: e.tensor_copy(out=dec[:, 0, 0:1], in_=e1[sl][:, 0, 15:16]),
                             reads=[("e1", sl)], writes=[("dec", i, 0)])
                        P.op("dve", lambda e: e.tensor_copy(out=dec[:, 1, 0:1], in_=e1[sl][:, 1, 0:1]),
                             reads=[("e1", sl)], writes=[("dec", i, 1)])
                    yield
                    pkt = pKT.bitcast(BF16)

                    def trk(e, rows=rows, c0=c0, pkt=pkt):
                        e.transpose(out=pkt[0:rows, 0:128], in_=kinvT[:, 0, c0:c0 + rows], identity=ident[:, :])
                        return e.transpose(out=pkt[0:rows, 128:256], in_=kinvT[:, 1, c0:c0 + rows], identity=ident[:, :])
                    P.op("pe", trk, reads=[("kinvT", i, 0), ("kinvT", i, 1), ("ident",)], writes=[kKT])
                    P.op("dve", (lambda rows, i, pkt: lambda e: e.tensor_copy(out=kinv_tm[:rows, i, :], in_=pkt[0:rows, 0:256]))(rows, i, pkt),
                         reads=[kKT], writes=[("kinv_tm", i)])
                    yield

            def gla_scan(hp, d_):
                vg = vg2[hp % 2]
                order = list(range(33)) if d_ == 0 else list(range(32, -1, -1))
                banks_ = (4, 5) if d_ == 0 else (2, 3)
                prev = None
                for step, n in enumerate(order):
                    t, r0, rows = chunk_loc(n)
                    bk = banks_[step % 2]
                    pD = bank(bk)[:, 0:128]

                    def mmds(e, pD=pD, t=t, r0=r0, rows=rows):
                        ins = None
                        for hl in range(2):
                            ins = e.matmul(pD[hl * 64:(hl + 1) * 64, :],
                                           lhsT=kinv_tm[r0:r0 + rows, t, d_ * 128 + hl * 64:d_ * 128 + (hl + 1) * 64],
                                           rhs=vg[r0:r0 + rows, t, hl * 128:(hl + 1) * 128],
                                           start=True, stop=True, tile_position=(r0, hl * 64))
                        return ins
                    P.op("pe", mmds, reads=[("kinv_tm", t), ("vg", hp % 2, t)], writes=[("pb", bk)])
                    tn = step % 2
                    if prev is None:
                        P.op("dve", (lambda pD, tn: lambda e: e.tensor_copy(out=Tst[d_][tn], in_=pD))(pD, tn),
                             reads=[("pb", bk)], writes=[("T", d_, tn)])
                    else:
                        P.op("dve", (lambda pD, tn, prev: lambda e: e.scalar_tensor_tensor(
                            out=Tst[d_][tn], in0=Tst[d_][1 - tn], scalar=dec[:, d_, prev:prev + 1], in1=pD,
                            op0=ALU.mult, op1=ALU.add))(pD, tn, prev),
                            reads=[("pb", bk), ("T", d_, 1 - tn)] + [("dec", i, d_) for i in range(17)], writes=[("T", d_, tn)])
                    if step < 32:
                        P.op("act", (lambda tn, n: lambda e: e.activation(out=Sst[d_][:, n, :], in_=Tst[d_][tn], func=AF.Copy,
                                                                          scale=dec[:, d_, n:n + 1]))(tn, n),
                             reads=[("T", d_, tn)] + [("dec", i, d_) for i in range(17)], writes=[("S", d_, n)])
                    prev = n
                    yield

            def gla_out(hp):
                tiles = list(range(16)) + [16]
                vg, sg = vg2[hp % 2], sg2[hp % 2]

                def stageA(t):
                    i = tiles[t]
                    c0, rows = tile_cols(i)
                    sl = t % 2

                    for hl in range(2):
                        def mmA(e, hl=hl):
                            ins = None
                            for d_ in range(2):
                                ins = e.matmul(psA_[hl][0:rows, d_ * 128:d_ * 128 + rows],
                                               lhsT=kinvT[hl * 64:(hl + 1) * 64, d_, c0:c0 + rows],
                                               rhs=qdec[hl * 64:(hl + 1) * 64, d_, c0:c0 + rows], start=True, stop=True)
                            return ins
                        P.op("pe", mmA, reads=[("kinvT", i, 0), ("kinvT", i, 1), ("qdec", i)], writes=[("pb", 2 + hl)])
                        pav = psA_[hl][:, 0:256].rearrange("p (a b) -> p a b", a=2)
                        P.op("dve", (lambda pav, hl: lambda e: e.tensor_tensor(
                            out=ATb[sl][:rows, 2 * hl:2 * hl + 2, 0:rows], in0=pav[0:rows, :, 0:rows],
                            in1=maskfb[:rows, 0:2, 0:rows], op=ALU.mult))(pav, hl),
                            reads=[("pb", 2 + hl), ("maskfb",)], writes=[("AT", sl, hl)])

                def stageB(t):
                    i = tiles[t]
                    c0, rows = tile_cols(i)
                    sl = t % 2
                    if i < 16:
                        chunks = [(2 * i + 1, 0), (2 * i + 2, 64)]
                    else:
                        chunks = [(0, 0)]
                    sreads = []
                    for (n, _r) in chunks:
                        if n >= 1:
                            sreads.append(("S", 0, n - 1))
                        if n <= 31:
                            sreads.append(("S", 1, n + 1))
                    for hl in range(2):
                        pO = psOg[sl][hl]
                        pkey = ("pb", psOg_bank[sl][hl])
                        oc = slice(hl * 128, (hl + 1) * 128)

                        def mmO(e, pO=pO, hl=hl, oc=oc):
                            extra = []
                            for (n, r0) in chunks:
                                cr = 64 if i < 16 else 16
                                if n >= 1:
                                    extra.append((0, n - 1, r0, cr))
                                if n <= 31:
                                    extra.append((1, n + 1, r0, cr))
                            e.matmul(pO[0:rows, 0:128], lhsT=ATb[sl][:rows, hl * 2, 0:rows], rhs=vg[:rows, i, oc],
                                     start=True, stop=False)
                            ins = e.matmul(pO[0:rows, 0:128], lhsT=ATb[sl][:rows, hl * 2 + 1, 0:rows], rhs=vg[:rows, i, oc],
                                           start=False, stop=(len(extra) == 0))
                            for xi, (d_, sn, r0, cr) in enumerate(extra):
                                last_of_range = (xi == len(extra) - 1) or (extra[xi + 1][2] != r0)
                                ins = e.matmul(pO[r0:r0 + cr, 0:128], lhsT=qdec[hl * 64:(hl + 1) * 64, d_, c0 + r0:c0 + r0 + cr],
                                               rhs=Sst[d_][hl * 64:(hl + 1) * 64, sn, :], start=False,
                                               stop=last_of_range, tile_position=(hl * 64, r0))
                            return ins
                        P.op("pe", mmO, reads=[("AT", sl, hl), ("vg", hp % 2, i), ("qdec", i)] + sreads, writes=[pkey])
                        gi = sl * 2 + hl
                        P.op("act", (lambda pO, gi: lambda e: e.activation(
                            out=junk[:rows, 0:128], in_=pO[0:rows, 0:128], func=AF.Square,
                            accum_out=gst[:rows, 0, gi:gi + 1]))(pO, gi),
                            reads=[pkey], writes=[("junk",), ("G", gi, "ss")])
                        rstd_ops(gst[:rows, 0, gi:gi + 1], gst[:rows, 1, gi:gi + 1], gst[:rows, 2, gi:gi + 1], 128, ("G", gi))
                        head = 2 * hp + hl
                        P.op("dve", (lambda pO, hl, head, gi: lambda e: e.scalar_tensor_tensor(
                            out=mix[:rows, i, 512 + head * 128:512 + (head + 1) * 128], in0=pO[0:rows, 0:128],
                            scalar=gst[:rows, 2, gi:gi + 1], in1=sg[:rows, i, hl * 128:(hl + 1) * 128],
                            op0=ALU.mult, op1=ALU.mult))(pO, hl, head, gi),
                            reads=[pkey, ("G", gi, "rstd"), ("sg", hp % 2, i)], writes=[("mix", i, "g", head)])

                n = len(tiles)
                for t in range(n + 1):
                    if t < n:
                        stageA(t)
                    if t >= 1:
                        stageB(t - 1)
                    yield

            def prep_all(hp):
                return gen_interleave(gen_interleave(gla_prep(hp, [16, 0, 3, 6, 9, 12, 15], 0),
                                                     gla_prep(hp, [1, 4, 7, 10, 13], 1)),
                                      gla_prep(hp, [2, 5, 8, 11, 14], 2))

            run(gla_proj_fm(0))
            interleave(prep_all(0), gla_proj_tm(0), ratio=1)
            interleave(gen_chain(gen_interleave(gla_scan(0, 0), gla_scan(0, 1)), gla_out(0)),
                       gen_chain(gla_proj_fm(1), gla_proj_tm(1)), ratio=1)
            run(prep_all(1))
            run(gen_interleave(gla_scan(1, 0), gla_scan(1, 1)))
            run(gla_out(1))
            A.reset(mG)
            P.barrier()
            if stage <= 3:
                raise _Stop()

            A.reset(mB2)
            h1 = A.alloc(F32, (17, 1024), top=True)
            dump_reg["h1"] = (h1, F32)
            mC = A.mark()
            wo_sb = A.alloc(BF16, (8, 1024))
            wstg = [A.alloc(F32, (1024,)) for _ in range(2)]
            grepP = A.alloc(F32, (1024,))
            grepF = A.alloc(F32, (1024,))
            tmpE = [A.alloc(F32, (1024,)) for _ in range(2)]
            xs2 = [A.alloc(BF16, (1024,)) for _ in range(3)]
            P.dma("sp", lambda e: e.dma_start(out=grepP, in_=grep_d[1]), writes=[("grep", "P")])
            P.dma("sp", lambda e: e.dma_start(out=grepF, in_=grep_d[2]), writes=[("grep", "F")])
            for k in range(8):
                sl = k % 2
                P.dma("sp" if k % 2 == 0 else "act",
                      (lambda k, sl: lambda e: e.dma_start(out=wstg[sl], in_=w_o_d[k * 128:(k + 1) * 128, :]))(k, sl),
                      writes=[("wstg", sl)])
                P.op("act" if k % 2 == 0 else "dve",
                     (lambda k, sl: lambda e: e.activation(out=wo_sb[:, k, :], in_=wstg[sl], func=AF.Copy,
                                                           scale=gfm[:, k:k + 1])) (k, sl) if k % 2 == 0 else
                     (lambda k, sl: lambda e: e.tensor_scalar(out=wo_sb[:, k, :], in0=wstg[sl], scalar1=gfm[:, k:k + 1],
                                                              scalar2=None, op0=ALU.mult))(k, sl),
                     reads=[("wstg", sl), ("gfm",)], writes=[("wo_sb", k)])
            P.dma("sp", lambda e: e.dma_start(out=h1[:16, 16, :], in_=meta_d), writes=[("h1", 16)])
            for pr in range(8):
                srcd = x_d[pr * 256:(pr + 1) * 256, :].rearrange("(t p) d -> p t d", p=128)
                P.dma("sp" if pr % 2 == 0 else "act",
                      (lambda pr, srcd: lambda e: e.dma_start(out=h1[:, 2 * pr:2 * pr + 2, :], in_=srcd))(pr, srcd),
                      writes=[("h1", 2 * pr), ("h1", 2 * pr + 1)])
            for i in range(17):
                c0, rows = tile_cols(i)
                sl = i % 2
                psv = bank(sl).bitcast(BF16)

                def trm(e, i=i, rows=rows, psv=psv):
                    ins = None
                    for k in range(8):
                        ins = e.transpose(out=psv[:, k * 128:k * 128 + rows], in_=mix[:rows, i, k * 128:(k + 1) * 128],
                                          identity=ident[:rows, :rows])
                    return ins
                mreads = [("mix", i, "na")] + [("mix", i, "g", h) for h in range(4)] + [("ident",)]
                P.op("pe", trm, reads=mreads, writes=[("ps", sl)])
                src_v = psv.rearrange("p (k c) -> p k c", k=8)[:, :, 0:rows]
                evac_copy(uT[:, :, c0:c0 + rows], src_v, [("ps", sl)], [("uT", i)])
            psY = [bank(2, 2), bank(4, 2)]

            finC = [None]

            def u2_tile(j):
                _c0, rj = tile_cols(j)
                fin_ = norm_transpose(j, h1[:rj, j, :], ("h1", j), 16, xs2[j % 3], ("xs2", j % 3),
                                      bank(6 + j % 2), ("psT2", j % 2), 3, "F")
                if finC[0] is not None:
                    finC[0]()
                finC[0] = fin_
            for i in range(17):
                c0, rows = tile_cols(i)
                sl = i % 2
                pY = psY[sl]

                def mmY(e, c0=c0, rows=rows, pY=pY):
                    ins = None
                    for nh in range(2):
                        for k in range(8):
                            ins = e.matmul(pY[0:rows, nh * 512:(nh + 1) * 512], lhsT=uT[:, k, c0:c0 + rows],
                                           rhs=wo_sb[:, k, nh * 512:(nh + 1) * 512], start=(k == 0), stop=(k == 7))
                    return ins
                P.op("pe", mmY, reads=[("uT", i)] + [("wo_sb", k) for k in range(8)], writes=[("psY", sl)])
                if i >= 2:
                    u2_tile(i - 2)
                ss = stat[:rows, 0, i:i + 1]
                ln = stat[:rows, 1, i:i + 1]
                rs = stat[:rows, 2, i:i + 1]
                P.op("act", (lambda rows, pY, ss: lambda e: e.activation(out=junk[:rows, :], in_=pY[0:rows, :], func=AF.Square,
                                                                          accum_out=ss))(rows, pY, ss),
                     reads=[("psY", sl)], writes=[("junk",), ("C", i, "ss")])
                rstd_ops(ss, ln, rs, D, ("C", i))
                P.op("dve", (lambda sl, rows, pY, rs: lambda e: e.scalar_tensor_tensor(
                    out=tmpE[sl][:rows, :], in0=pY[0:rows, :], scalar=rs, in1=grepP[:rows, :], op0=ALU.mult, op1=ALU.mult))(sl, rows, pY, rs),
                    reads=[("psY", sl), ("C", i, "rstd"), ("grep", "P")], writes=[("tmpE", sl)])
                P.op("pool", (lambda sl, rows, i: lambda e: e.tensor_tensor(out=h1[:rows, i, :], in0=h1[:rows, i, :],
                                                                             in1=tmpE[sl][:rows, :], op=ALU.add))(sl, rows, i),
                     reads=[("tmpE", sl), ("h1", i)], writes=[("h1", i)])
            u2_tile(15)
            u2_tile(16)
            finC[0]()
            A.reset(mB)
            P.barrier()
            if stage <= 4:
                raise _Stop()

            hid = A.alloc(BF16, (NCH, 1024))
            cw = A.alloc(F32, (44, 4))
            grepO = A.alloc(F32, (1024,))
            wf = [A.alloc(BF16, (2, 8, 128)) for _ in range(3)]
            fstg = [A.alloc(F32, (8, 128)) for _ in range(3)]
            wo2 = [A.alloc(BF16, (1024,)) for _ in range(4)]
            accV = [A.alloc(F32, (344,)) for _ in range(3)]
            accG = [A.alloc(F32, (344,)) for _ in range(3)]
            gel = [A.alloc(F32, (344,)) for _ in range(2)]
            tmpO = [A.alloc(F32, (1024,)) for _ in range(2)]
            P.dma("sp", lambda e: e.dma_start(out=cw, in_=cw_d.rearrange("p (c j) -> p c j", j=4)), writes=[("cw",)])
            P.dma("sp", lambda e: e.dma_start(out=grepO, in_=grep_d[3]), writes=[("grep", "O")])

            pb_i = [0]
            cv_i = [0]
            wf_i = [0]
            wo_i = [0]
            ep_i = [0]
            NTS = [(0, 342), (342, 342), (684, 340)]
            for hf in range(2):
                T0 = hf * 1024
                def load_wf(c, hf=hf):
                    sl_ = c % 3
                    for vg_ in range(2):
                        blk_ = vg_ * NCH + c
                        scr = w1bf_d[blk_].rearrange("p (k c) -> p k c", k=8)
                        if hf == 0:
                            srcap = w_f1_d[blk_].rearrange("p (k c) -> p k c", k=8)
                            ce = "act" if (vg_ == 0 or c % 2 == 1) else "dve"
                            load_cast(srcap, wf[sl_][:, vg_, :, :], ("wf", sl_, vg_), fstg, eng=ce)
                            P.dma("act" if ce == "act" else "sp",
                                  (lambda sl_, vg_, scr: lambda e: e.dma_start(out=scr, in_=wf[sl_][:, vg_, :, :]))(sl_, vg_, scr),
                                  reads=[("wf", sl_, vg_)], writes=[("w1bf", blk_)])
                        else:
                            P.dma("sp", (lambda sl_, vg_, scr: lambda e: e.dma_start(out=wf[sl_][:, vg_, :, :], in_=scr))(sl_, vg_, scr),
                                  reads=[("w1bf", blk_)], writes=[("wf", sl_, vg_)])
                def tail_ops(cs, n, c, o0):
                    P.op("act", lambda e: e.activation(out=gel[cs % 2][:, 0:n], in_=accG[cs][:, 0:n], func=AF.Gelu_apprx_tanh),
                         reads=[("acc", cs, 1)], writes=[("gel", cs % 2)])
                    P.op("pool", lambda e: e.tensor_tensor(out=hid[:, c, o0:o0 + n], in0=accV[cs][:, 0:n],
                                                           in1=gel[cs % 2][:, 0:n], op=ALU.mult),
                         reads=[("acc", cs, 0), ("gel", cs % 2)], writes=[("hid", c, o0)])
                pending = [None]
                load_wf(0)
                for c in range(NCH):
                    sl = c % 3
                    if c + 1 < NCH:
                        load_wf(c + 1)
                    for (o0, n) in NTS:
                        ci = 16 + T0 + o0 - 1
                        nin = n + 2
                        tl = tiles_overlapping(ci, min(nin, 2064 - ci))
                        pbs = []
                        for vg_ in range(2):
                            pb = pb_i[0] % 8
                            pb_i[0] += 1
                            pbs.append(pb)
                            ps = bank(pb)

                            def mm(e, sl=sl, vg_=vg_, ci=ci, nin=nin, ps=ps):
                                ins = None
                                for k in range(8):
                                    ins = e.matmul(ps[:, 0:nin], lhsT=wf[sl][:, vg_, k, :],
                                                   rhs=uT[:, k, ci:ci + nin], start=(k == 0), stop=(k == 7))
                                return ins
                            P.op("pe", mm, reads=[("wf", sl, vg_), ("uTpad",)] + [("uT", t) for t in tl],
                                 writes=[("psF", pb)])
                        cs = cv_i[0] % 3
                        cv_i[0] += 1
                        for vg_, acc in ((0, accV[cs]), (1, accG[cs])):
                            ps = bank(pbs[vg_])
                            cc = vg_ * NCH + c
                            P.op("act", (lambda ps, acc, cc, n: lambda e: e.activation(
                                out=acc[:, 0:n], in_=ps[:, 1:n + 1], func=AF.Identity, scale=cw[:, cc, 1:2], bias=cw[:, cc, 3:4]))(ps, acc, cc, n),
                                reads=[("psF", pbs[vg_]), ("cw",)], writes=[("acc", cs, vg_)])
                            P.op("dve", (lambda ps, acc, cc, n: lambda e: e.scalar_tensor_tensor(
                                out=acc[:, 0:n], in0=ps[:, 0:n], scalar=cw[:, cc, 0:1], in1=acc[:, 0:n],
                                op0=ALU.mult, op1=ALU.add))(ps, acc, cc, n),
                                reads=[("psF", pbs[vg_]), ("cw",), ("acc", cs, vg_)], writes=[("acc", cs, vg_)])
                            P.op("dve", (lambda ps, acc, cc, n: lambda e: e.scalar_tensor_tensor(
                                out=acc[:, 0:n], in0=ps[:, 2:n + 2], scalar=cw[:, cc, 2:3], in1=acc[:, 0:n],
                                op0=ALU.mult, op1=ALU.add))(ps, acc, cc, n),
                                reads=[("psF", pbs[vg_]), ("cw",), ("acc", cs, vg_)], writes=[("acc", cs, vg_)])
                        if pending[0] is not None:
                            pending[0]()
                        pending[0] = (lambda cs, n, c, o0: lambda: tail_ops(cs, n, c, o0))(cs, n, c, o0)
                if pending[0] is not None:
                    pending[0]()
                    pending[0] = None
                def load_wo(q):
                    k_ = q % NCH
                    P.dma("sp", (lambda q, k_: lambda e: e.dma_start(out=wo2[q % 4], in_=w2bf_d[k_]))(q, k_),
                          reads=[("w2bf", k_)], writes=[("wo2", q % 4)])

                qbase = wo_i[0]
                nq_ = 2 * NCH
                load_wo(qbase)
                load_wo(qbase + 1)
                load_wo(qbase + 2)
                for blk in range(2):
                    for k in range(NCH):
                        q = wo_i[0]
                        ws = q % 4
                        wo_i[0] += 1
                        if q + 3 < qbase + nq_:
                            load_wo(q + 3)

                        for tl_ in range(4):
                            tc0 = (blk * 4 + tl_) * 128

                            def mm2(e, ws=ws, k=k, tl_=tl_, tc0=tc0):
                                ins = None
                                for nh in range(2):
                                    ins = e.matmul(bank(tl_ * 2 + nh), lhsT=hid[:, k, tc0:tc0 + 128],
                                                   rhs=wo2[ws][:, nh * 512:(nh + 1) * 512], start=(k == 0), stop=(k == NCH - 1))
                                return ins
                            P.op("pe", mm2, reads=[("wo2", ws)] + [("hid", k, o0) for (o0, _n) in NTS],
                                 writes=[("psF", tl_ * 2), ("psF", tl_ * 2 + 1)])
                    for tl_ in range(4):
                        i = hf * 8 + blk * 4 + tl_
                        pY = bank(tl_ * 2, 2)
                        es_ = ep_i[0] % 2
                        ep_i[0] += 1
                        ss = stat[:, 0, i:i + 1]
                        ln = stat[:, 1, i:i + 1]
                        rs = stat[:, 2, i:i + 1]
                        pk = [("psF", tl_ * 2), ("psF", tl_ * 2 + 1)]
                        P.op("act", (lambda pY, ss: lambda e: e.activation(out=junk[:, :], in_=pY, func=AF.Square, accum_out=ss))(pY, ss),
                             reads=pk, writes=[("junk",), ("O", i, "ss")])
                        rstd_ops(ss, ln, rs, D, ("O", i))
                        P.op("dve", (lambda es_, pY, rs: lambda e: e.scalar_tensor_tensor(
                            out=tmpO[es_], in0=pY, scalar=rs, in1=grepO, op0=ALU.mult, op1=ALU.mult))(es_, pY, rs),
                            reads=pk + [("O", i, "rstd"), ("grep", "O")], writes=[("tmpO", es_)])
                        P.op("dve" if (hf == 1 and blk == 1 and tl_ % 2 == 1) else "pool",
                             (lambda es_, i: lambda e: e.tensor_tensor(out=h1[:, i, :], in0=h1[:, i, :], in1=tmpO[es_],
                                                                       op=ALU.add))(es_, i),
                             reads=[("tmpO", es_), ("h1", i)], writes=[("h1", i)])
                        P.dma("sp", (lambda i: lambda e: e.dma_start(out=y_d[i * 128:(i + 1) * 128, :], in_=h1[:, i, :]))(i),
                              reads=[("h1", i)])
        except _Stop:
            pass
        dump_all()
        P.finish()
        print("SBUF peak bytes/partition:", A.peak, " ops:", {e: len(P.streams[e]) for e in ENGS})

        @block.tensor
        def _(e):
            P.emit("pe", e, sems)

        @block.scalar
        def _(e):
            P.emit("act", e, sems)

        @block.vector
        def _(e):
            P.emit("dve", e, sems)

        @block.gpsimd
        def _(e):
            P.emit("pool", e, sems)

        @block.sync
        def _(e):
            P.emit("sp", e, sems)
    return nc


def _na_tables(rpb):
    H = rpb.shape[0]
    cq = np.arange(64)
    cs = np.clip(cq - 8, 0, 48)
    ck = np.arange(64)
    in_win = (ck[None, :] >= cs[:, None]) & (ck[None, :] < cs[:, None] + 16)
    dc = np.clip(ck[:, None] - cq[None, :], -15, 15) + 15
    G = rpb[:, :, dc]
    G = np.where(in_win.T[None, None], G, np.float32(NEG)).astype(np.float32)
    tabs = np.full((H, 128, 1664), NEG, np.float32)
    for a in range(2):
        for u in range(16):
            dr = a + 14 - u
            if 0 <= dr <= 14:
                tabs[:, a * 64:(a + 1) * 64, u * 64:(u + 1) * 64] = G[:, dr]
    for c in range(5):
        for b in range(2):
            u = 3 + 2 * c + b
            for a in range(2):
                dr = a + 14 - u
                if 3 <= dr <= 10:
                    tabs[:, a * 64:(a + 1) * 64, 1024 + (2 * c + b) * 64:1024 + (2 * c + b + 1) * 64] = G[:, dr]
    return tabs


def _consts():
    s = np.arange(128)
    same = (s[:, None] // 64) == (s[None, :] // 64)
    trif = np.where(same & (s[:, None] <= s[None, :]), -1.0 / 16.0, 0.0)
    trib = np.where(same & (s[:, None] >= s[None, :]), -1.0 / 16.0, 0.0)
    tri = np.concatenate([trif, trib], axis=1).astype(np.float32)
    mf = (same & (s[:, None] <= s[None, :])).astype(np.float32)
    mb = (same & (s[:, None] > s[None, :])).astype(np.float32)
    maskfb = np.concatenate([mf, mb, mf, mb], axis=1).astype(np.float32)
    ident = np.eye(128, dtype=np.float32)
    return ident, tri, maskfb


def _tile_weights(w_in, w_f1):
    wp = np.zeros((1024, 3200), np.float32)
    wp[:, :3104] = w_in
    w_in_t = np.ascontiguousarray(wp.reshape(8, 128, 25, 128).transpose(2, 1, 0, 3)).reshape(25, 128, 1024)
    w_f1_t = np.ascontiguousarray(w_f1.reshape(8, 128, 44, 128).transpose(2, 1, 0, 3)).reshape(44, 128, 1024)
    return w_in_t, w_f1_t


_NC_CACHE = {}


def kernel(x, meta_tokens, norm_mix_pre, w_in, na_rel_bias, na_out_gain, gla_gate_up_fwd,
           gla_gate_bias_fwd, gla_gate_up_bwd, gla_gate_bias_bwd, gla_out_gain, w_o, norm_mix_post,
           norm_ffn_pre, w_ffn_in, ffn_conv_w, ffn_conv_b, w_ffn_out, norm_ffn_post):
    f = lambda a: np.ascontiguousarray(np.asarray(a, dtype=np.float32))
    x = f(x)
    B = x.shape[0]
    up33 = np.zeros((33, 512), np.float32)
    up33[0:16, 0:256] = f(gla_gate_up_fwd)[0]
    up33[16:32, 256:512] = f(gla_gate_up_bwd)[0]
    up33[32, 0:256] = f(gla_gate_bias_fwd)[0]
    up33[32, 256:512] = f(gla_gate_bias_bwd)[0]
    tabs = _na_tables(f(na_rel_bias)[0])
    gvec_o = np.concatenate([f(na_out_gain)[0], np.tile(f(gla_out_gain)[0], 4)])
    gfm = f(np.concatenate([gvec_o.reshape(8, 128).T, f(norm_mix_pre)[0].reshape(8, 128).T,
                            f(norm_ffn_pre)[0].reshape(8, 128).T], axis=1))
    grep = np.stack([np.broadcast_to(f(v)[0][None, :], (128, D)) for v in
                     (norm_mix_pre, norm_mix_post, norm_ffn_pre, norm_ffn_post)])
    grep = f(grep)
    cwj = f(ffn_conv_w)[0].reshape(3, 44, 128)
    cbj = f(ffn_conv_b)[0].reshape(1, 44, 128)
    cw = f(np.concatenate([cwj, cbj], axis=0).transpose(2, 1, 0).reshape(128, 44 * 4))
    ident, tri, maskfb = _consts()
    w_in_t, w_f1_t = _tile_weights(f(w_in)[0], f(w_ffn_in)[0])
    shared = {
        "meta": f(meta_tokens), "w_in_t": w_in_t, "up33": up33, "tabs": tabs, "w_o": f(w_o)[0],
        "w_f1_t": w_f1_t, "w_f2": f(w_ffn_out)[0], "cw": cw, "gfm": gfm, "grep": grep,
        "identf": ident, "tri": tri, "maskfb": maskfb,
    }
    if "nc" not in _NC_CACHE:
        _NC_CACHE["nc"] = build_program()
    nc = _NC_CACHE["nc"]
    in_maps = []
    for b in range(B):
        m = dict(shared)
        m["x"] = np.ascontiguousarray(x[b])
        in_maps.append(m)
    res = run_bass_kernel_spmd(nc, in_maps, core_ids=list(range(B)))
    out = np.stack([np.asarray(r["y"], dtype=np.float32) for r in res.results], axis=0)
    return out
```

```python
import numpy as np
import concourse.bass as bass
import concourse.mybir as mybir
from concourse.bass_utils import run_bass_kernel_spmd

F32 = mybir.dt.float32
BF16 = mybir.dt.bfloat16
AF = mybir.ActivationFunctionType
ALU = mybir.AluOpType
AX = mybir.AxisListType

D = 1024
SEQ = 2048
NMETA = 16
NT_X = 16
UTW = 2072
DFF = 2816
NCH = 22
EPS = 1e-6
NEG = -30000.0
import os
NDS = int(os.environ.get("K_NDS", "24"))
NQS = 8
ENGS = ("pe", "act", "dve", "pool", "sp")
DEBUG = False


class _Stop(Exception):
    pass


class Prog:
    def __init__(self):
        self.streams = {e: [] for e in ENGS}
        self.cnt = {e: 0 for e in ENGS}
        self.seen = {e: {} for e in ENGS}
        self.lastw = {}
        self.readers = {}
        self.ndma = 0
        self.nq = 0
        self.dma_last = {}

    def _deps(self, reads, writes):
        deps = {}

        def add(tok):
            k, v = tok
            if deps.get(k, 0) < v:
                deps[k] = v
        for r in reads:
            if r in self.lastw:
                add(self.lastw[r])
        for w in writes:
            if w in self.lastw:
                add(self.lastw[w])
            for k, v in self.readers.get(w, {}).items():
                add((k, v))
        return deps

    def _commit(self, tok, reads, writes):
        k, v = tok
        for r in reads:
            d = self.readers.setdefault(r, {})
            if d.get(k, 0) < v:
                d[k] = v
        for w in writes:
            self.lastw[w] = tok
            self.readers[w] = {}

    def _waits(self, eng, deps):
        waits = []
        for k, v in deps.items():
            if self.seen[eng].get(k, 0) < v:
                waits.append((k, v))
                self.seen[eng][k] = v
        return waits

    def op(self, eng, fn, reads=(), writes=()):
        deps = self._deps(reads, writes)
        waits = self._waits(eng, deps)
        self.cnt[eng] += 1
        tok = (eng, self.cnt[eng])
        self.streams[eng].append((waits, fn, None))
        self._commit(tok, reads, writes)
        return tok

    def dma(self, eng, fn, reads=(), writes=()):
        deps = self._deps(reads, writes)
        if eng == "pool":
            idx = ("q", self.nq)
            self.nq += 1
            assert self.nq <= NQS
            val = 16
        else:
            idx = ("d", self.ndma % NDS)
            val = 16 * (self.ndma // NDS + 1)
            self.ndma += 1
            if val > 16:
                deps[idx] = max(deps.get(idx, 0), val - 16)
        waits = self._waits(eng, deps)
        self.streams[eng].append((waits, fn, idx))
        tok = (idx, val)
        self.dma_last[idx] = val
        self._commit(tok, reads, writes)
        return tok

    def barrier(self):
        snap = dict(self.cnt)
        dsnap = dict(self.dma_last)
        for e in ENGS:
            deps = {o: snap[o] for o in ENGS if o != e and snap[o] > 0}
            for idx, v in dsnap.items():
                deps[idx] = v
            waits = self._waits(e, deps)
            if waits:
                self.streams[e].append((waits, None, None))

    def finish(self):
        deps = {idx: v for idx, v in self.dma_last.items()}
        waits = self._waits("sp", deps)
        if waits:
            self.streams["sp"].append((waits, None, None))

    def emit(self, eng, handle, sems):
        for waits, fn, dmaidx in self.streams[eng]:
            for k, v in waits:
                handle.wait_ge(sems[k], v)
            if fn is None:
                continue
            ins = fn(handle)
            if dmaidx is None:
                ins.then_inc(sems[eng], 1)
            else:
                ins.then_inc(sems[dmaidx], 16)


class Arena:
    def __init__(self, tensor, nbytes):
        self.t = tensor
        self.n = nbytes
        self.top = 0
        self.hi = nbytes
        self.peak = 0

    def mark(self):
        return self.top

    def reset(self, m):
        self.top = m

    def alloc(self, dtype, shape, top=False):
        esz = 4 if dtype == F32 else 2
        nel = int(np.prod(shape))
        nb = (nel * esz + 15) // 16 * 16
        if top:
            self.hi -= nb
            off = self.hi
        else:
            off = self.top
            self.top += nb
        self.peak = max(self.peak, self.top + (self.n - self.hi))
        assert self.top <= self.hi, f"SBUF arena overflow {self.top} > {self.hi}"
        ap = self.t[:, off // 4:(off + nb) // 4]
        if dtype == BF16:
            ap = ap.bitcast(BF16)
        ap = ap[:, 0:nel]
        if len(shape) == 2:
            ap = ap.rearrange("p (a b) -> p a b", a=shape[0])
        elif len(shape) == 3:
            ap = ap.rearrange("p (a b c) -> p a b c", a=shape[0], b=shape[1])
        return ap


def tile_cols(i):
    if i == 16:
        return 0, 16
    return 16 + 128 * i, 128


def tiles_overlapping(c0, n):
    res = []
    for i in range(17):
        a, w = tile_cols(i)
        if a < c0 + n and c0 < a + w:
            res.append(i)
    return res


FM_NT = [(0, 512), (512, 512), (1024, 512), (1536, 512), (2048, 16)]


def build_program(stage=9, dumps=()):
    nc = bass.Bass("TRN2", target_bir_lowering=False)

    def din(name, shape):
        return nc.dram_tensor(name, list(shape), F32, kind="ExternalInput").ap()

    x_d = din("x", (SEQ, D))
    meta_d = din("meta", (NMETA, D))
    w_in_d = din("w_in_t", (25, 128, 1024))
    up_d = din("up33", (33, 512))
    tabs_d = din("tabs", (8, 128, 1664))
    w_o_d = din("w_o", (D, D))
    w_f1_d = din("w_f1_t", (44, 128, 1024))
    w_f2_d = din("w_f2", (DFF, D))
    cw_d = din("cw", (128, 44 * 4))
    gfm_d = din("gfm", (128, 24))
    grep_d = din("grep", (4, 128, D))
    ident_d = din("identf", (128, 128))
    tri_d = din("tri", (128, 256))
    mask_d = din("maskfb", (128, 512))
    y_d = nc.dram_tensor("y", [SEQ, D], F32, kind="ExternalOutput").ap()
    dump_reg = {}
    w1bf_d = nc.dram_tensor("w1bf", [44, 128, 1024], BF16, kind="Internal").ap()
    w2bf_d = nc.dram_tensor("w2bf", [NCH, 128, 1024], BF16, kind="Internal").ap()
    dump_shapes = {"uT": (8 * UTW, BF16), "mix": (17 * 1024, BF16), "h1": (17 * 1024, F32), "ona": (17 * 512, F32)}
    dump_dram = {name: nc.dram_tensor("dbg_" + name, [128, dump_shapes[name][0]], dump_shapes[name][1],
                                      kind="ExternalOutput").ap() for name in dumps}

    P = Prog()
    SB_BYTES = 212000

    from contextlib import ExitStack
    with ExitStack() as es:
        sb_t = es.enter_context(nc.sbuf_tensor("arena", [128, SB_BYTES // 4], F32))
        ps_t = es.enter_context(nc.psum_tensor("psum", [128, 4096], F32))
        sems = {}
        for e in ENGS:
            sems[e] = es.enter_context(nc.semaphore("s_" + e))
        for i in range(NDS):
            sems[("d", i)] = es.enter_context(nc.semaphore("d%d" % i))
        for i in range(NQS):
            sems[("q", i)] = es.enter_context(nc.semaphore("q%d" % i))
        block = es.enter_context(nc.Block())

        A = Arena(sb_t, SB_BYTES)

        def bank(b, n=1):
            return ps_t[:, b * 512:(b + n) * 512]

        ident = A.alloc(BF16, (128,))
        tri = A.alloc(F32, (256,))
        gfm = A.alloc(F32, (24,))
        uT = A.alloc(BF16, (8, UTW))
        stat = A.alloc(F32, (8, 17))
        junk = A.alloc(BF16, (1024,))

        P.dma("pool", lambda e: e.dma_start(out=ident, in_=ident_d), writes=[("ident",)])
        P.dma("sp", lambda e: e.dma_start(out=tri, in_=tri_d), writes=[("tri",)])
        P.dma("sp", lambda e: e.dma_start(out=gfm, in_=gfm_d), writes=[("gfm",)])
        P.op("pool", lambda e: e.memset(uT[:, :, 2064:UTW], 0.0), writes=[("uTpad",)])

        def rstd_ops(ss_ap, tmp_ap, out_ap, n, rkey):
            P.op("act", lambda e: e.activation(out=tmp_ap, in_=ss_ap, func=AF.Ln, scale=1.0 / n, bias=EPS),
                 reads=[rkey + ("ss",)], writes=[rkey + ("ln",)])
            P.op("act", lambda e: e.activation(out=out_ap, in_=tmp_ap, func=AF.Exp, scale=-0.5),
                 reads=[rkey + ("ln",)], writes=[rkey + ("rstd",)])

        def norm_transpose(i, src, src_key, gcol, xs_slot, xs_key, psT, psT_key, stat_base, tag):
            c0, rows = tile_cols(i)
            ss = stat[:rows, stat_base, i:i + 1]
            ln = stat[:rows, stat_base + 1, i:i + 1]
            rs = stat[:rows, stat_base + 2, i:i + 1]
            rkey = (tag, i)
            P.op("act", lambda e: e.activation(out=junk[:rows, :], in_=src, func=AF.Square, accum_out=ss),
                 reads=[src_key], writes=[("junk",), rkey + ("ss",)])
            rstd_ops(ss, ln, rs, D, rkey)
            if tag == "A":
                P.op("dve", lambda e: e.tensor_scalar(out=xs_slot[:rows, :], in0=src, scalar1=rs, scalar2=None, op0=ALU.mult),
                     reads=[src_key, rkey + ("rstd",)], writes=[xs_key])
            else:
                P.op("act", lambda e: e.activation(out=xs_slot[:rows, :], in_=src, func=AF.Copy, scale=rs),
                     reads=[src_key, rkey + ("rstd",)], writes=[xs_key])
            psv = psT.bitcast(BF16)

            def tr(e):
                ins = None
                for k in range(8):
                    ins = e.transpose(out=psv[:, k * 128:k * 128 + rows], in_=xs_slot[:rows, k * 128:(k + 1) * 128],
                                      identity=ident[:rows, :rows])
                return ins
            P.op("pe", tr, reads=[xs_key, ("ident",)], writes=[psT_key])
            src_v = psv.rearrange("p (k c) -> p k c", k=8)[:, :, 0:rows]
            gb = gfm[:, gcol:gcol + 8].to_broadcast([128, 8, rows])

            def fin():
                P.op("dve", lambda e: e.tensor_tensor(out=uT[:, :, c0:c0 + rows], in0=src_v, in1=gb, op=ALU.mult),
                     reads=[psT_key, ("gfm",)], writes=[("uT", i)])
            return fin

        def dump_all():
            for name in dumps:
                ap, dt = dump_reg[name]
                shp = list(ap.shape)
                dd = dump_dram[name]
                src = ap
                if len(shp) == 3:
                    src = ap.rearrange("p a b -> p (a b)")
                elif len(shp) == 4:
                    src = ap.rearrange("p a b c -> p (a b c)")
                P.dma("sp", (lambda dd, src: lambda e: e.dma_start(out=dd, in_=src))(dd, src))

        dump_reg["uT"] = (uT, BF16)
        try:
            hiA = A.hi
            grepA = A.alloc(F32, (1024,), top=True)
            xin = [A.alloc(F32, (2, 1024), top=True) for _ in range(2)]
            xmeta = A.alloc(F32, (1024,), top=True)
            xs = [A.alloc(BF16, (1024,), top=True) for _ in range(2)]
            P.dma("sp", lambda e: e.dma_start(out=grepA, in_=grep_d[0]), writes=[("grep", "A")])
            P.dma("sp", lambda e: e.dma_start(out=xmeta[:16, :], in_=meta_d), writes=[("xmeta",)])
            for pr in range(2):
                srcd = x_d[pr * 256:(pr + 1) * 256, :].rearrange("(t p) d -> p t d", p=128)
                P.dma("act" if pr == 1 else "sp", (lambda pr, srcd: lambda e: e.dma_start(out=xin[pr % 2], in_=srcd))(pr, srcd),
                      writes=[("xin", pr % 2)])
            finA = [None]
            for idxA, i in enumerate([16] + list(range(16))):
                sl = idxA % 2
                c0, rows = tile_cols(i)
                if i == 16:
                    src_ap, src_key = xmeta[:16, :], ("xmeta",)
                else:
                    pr = i // 2
                    if i % 2 == 0 and pr >= 2:
                        srcd = x_d[pr * 256:(pr + 1) * 256, :].rearrange("(t p) d -> p t d", p=128)
                        P.dma("sp" if pr % 2 == 0 else "act", (lambda pr, srcd: lambda e: e.dma_start(out=xin[pr % 2], in_=srcd))(pr, srcd),
                              writes=[("xin", pr % 2)])
                    src_ap, src_key = xin[pr % 2][:, i % 2, :], ("xin", pr % 2)
                fin_ = norm_transpose(i, src_ap, src_key, 8, xs[sl], ("xs", sl),
                                      bank(6 + sl), ("psP", sl), 0, "A")
                if finA[0] is not None:
                    finA[0]()
                finA[0] = fin_
            finA[0]()
            if stage <= 1:
                raise _Stop()

            mB = A.mark()
            mix = A.alloc(BF16, (17, 1024))
            mB2 = A.mark()
            dump_reg["mix"] = (mix, BF16)
            P.op("pool", lambda e: e.memset(mix[:, 16, :], 0.0), writes=[("mix", 16, "na")] + [("mix", 16, "g", h) for h in range(4)])
            zT = A.alloc(BF16, (UTW,))
            upsb = A.alloc(BF16, (512,))
            wb = [A.alloc(BF16, (2, 8, 128)) for _ in range(3)]
            wb_i = [0]
            psP = [bank(6), bank(7)]
            psP_i = [0]
            ev_i = [0]

            P.dma("pool", lambda e: e.dma_start(out=upsb[0:33, :], in_=up_d), writes=[("upsb",)])
            P.op("pool", lambda e: e.memset(zT[32:33, :], 1.0), writes=[("zT1",)])

            def evac_copy(dst, src, reads, writes, func=None):
                if func is not None:
                    P.op("act", lambda e: e.activation(out=dst, in_=src, func=func), reads=reads, writes=writes)
                    return
                ev_i[0] += 1
                if ev_i[0] % 2 == 0:
                    P.op("act", lambda e: e.copy(out=dst, in_=src), reads=reads, writes=writes)
                else:
                    P.op("dve", lambda e: e.tensor_copy(out=dst, in_=src), reads=reads, writes=writes)

            wstg = [A.alloc(F32, (8, 128)) for _ in range(2)]
            stg_i = [0]

            def load_cast(src_ap, dst_ap, dst_key, stg, eng="pool"):
                ss_ = stg_i[0] % len(stg)
                stg_i[0] += 1
                P.dma("sp", lambda e: e.dma_start(out=stg[ss_], in_=src_ap), writes=[("stg", id(stg), ss_)])
                if eng == "act":
                    P.op("act", lambda e: e.copy(out=dst_ap, in_=stg[ss_]), reads=[("stg", id(stg), ss_)], writes=[dst_key])
                else:
                    P.op(eng, lambda e: e.tensor_copy(out=dst_ap, in_=stg[ss_]), reads=[("stg", id(stg), ss_)], writes=[dst_key])

            def load_wblk(col0, ncols):
                sl = wb_i[0] % 3
                wb_i[0] += 1
                assert col0 % 128 == 0
                nblk = (ncols + 127) // 128
                for j in range(nblk):
                    srcap = w_in_d[col0 // 128 + j].rearrange("p (k c) -> p k c", k=8)
                    load_cast(srcap, wb[sl][:, j, :, :], ("wb", sl, j), wstg)
                return sl, nblk

            def proj_fm(col0, ncols, dst, dst_key):
                sl, _nb = load_wblk(col0, ncols)
                for (c0, n) in FM_NT:
                    pb = psP_i[0] % 2
                    psP_i[0] += 1
                    ps = psP[pb]

                    def mm(e, sl=sl, c0=c0, n=n, ps=ps):
                        ins = None
                        for k in range(8):
                            ins = e.matmul(ps[0:ncols, 0:n], lhsT=wb[sl][:, 0, k, 0:ncols], rhs=uT[:, k, c0:c0 + n],
                                           start=(k == 0), stop=(k == 7))
                        return ins
                    tl = tiles_overlapping(c0, n)
                    P.op("pe", mm, reads=[("wb", sl, 0)] + [("uT", t) for t in tl], writes=[("psP", pb)])
                    evac_copy(dst[0:ncols, c0:c0 + n], ps[0:ncols, 0:n], [("psP", pb)],
                              [dst_key + (t,) for t in tl])
                    yield

            def proj_tm(col0, ncols, evac):
                sl, nblk = load_wblk(col0, ncols)
                for i in range(17):
                    c0, rows = tile_cols(i)
                    pb = psP_i[0] % 2
                    psP_i[0] += 1
                    ps = psP[pb]

                    def mm(e, sl=sl, c0=c0, rows=rows, ps=ps):
                        ins = None
                        for k in range(8):
                            ins = e.matmul(ps[0:rows, 0:ncols], lhsT=uT[:, k, c0:c0 + rows], rhs=wb[sl][:, 0:nblk, k, :],
                                           start=(k == 0), stop=(k == 7))
                        return ins
                    P.op("pe", mm, reads=[("wb", sl, j) for j in range(nblk)] + [("uT", i)], writes=[("psP", pb)])
                    evac(i, rows, ps[0:rows, 0:ncols], ("psP", pb))
                    yield

            def run(gen):
                for _ in gen:
                    pass

            def interleave(main, side, ratio=1):
                side_done = False
                for _ in main:
                    for _r in range(ratio):
                        if not side_done:
                            try:
                                next(side)
                            except StopIteration:
                                side_done = True
                if not side_done:
                    for _ in side:
                        pass

            def gen_interleave(ga, gb):
                da = db = False
                while not (da and db):
                    if not da:
                        try:
                            next(ga)
                        except StopIteration:
                            da = True
                    if not db:
                        try:
                            next(gb)
                        except StopIteration:
                            db = True
                    yield

            def gen_chain(*gens):
                for g_ in gens:
                    yield from g_

            mNA = A.mark()
            ona = A.alloc(F32, (17, 512))
            dump_reg["ona"] = (ona, F32)
            qT = [A.alloc(BF16, (UTW,)) for _ in range(2)]
            kT = [A.alloc(BF16, (UTW,)) for _ in range(2)]
            vaug = [A.alloc(BF16, (17, 2, 65)) for _ in range(2)]
            tab = [A.alloc(BF16, (2, 1664)) for _ in range(2)]
            S_sb = [A.alloc(F32, (640,)) for _ in range(2)]
            PT = [[A.alloc(BF16, (640,)) for _ in range(2)] for _h in range(2)]
            PTm = [[A.alloc(BF16, (128,)) for _ in range(2)] for _h in range(2)]
            PTq = [A.alloc(BF16, (16,)) for _ in range(2)]
            rec = A.alloc(F32, (4,))
            for b in range(2):
                P.op("pool", (lambda b: lambda e: e.memset(vaug[b][:, 16, :, :], 0.0))(b), writes=[("vz", b)])
                P.op("pool", (lambda b: lambda e: e.memset(vaug[b][:, :, :, 64:65], 1.0))(b), reads=[("vz", b)], writes=[("vones", b)])
            for hl_ in range(2):
                for par_ in range(2):
                    P.op("pool", (lambda hl_, par_: lambda e: e.memset(PTm[hl_][par_], 0.0))(hl_, par_),
                         writes=[("PTm0",), ("PTm", hl_, par_)])

            psS = [bank(0, 2), bank(2, 2)]
            psO = [bank(4), bank(5)]

            def na_proj(hp):
                b = hp % 2
                srcap = tabs_d[2 * hp:2 * hp + 2].rearrange("h p c -> p h c")
                P.dma("pool", lambda e: e.dma_start(out=tab[b], in_=srcap), writes=[("tab", b)])
                yield from proj_fm(hp * 128, 128, qT[b], ("qT", b))
                yield from proj_fm(512 + hp * 128, 128, kT[b], ("kT", b))

                def ev(i, rows, ps, pkey):
                    dst = vaug[b][:rows, i, :, 0:64]
                    evac_copy(dst, ps.rearrange("p (h d) -> p h d", h=2), [pkey, ("vones", b)], [("vaug", b, i)])
                yield from proj_tm(1024 + hp * 128, 128, ev)

            na_it = [0]

            def na_attn(hp):
                b = hp % 2
                its = []
                for i in range(16):
                    if 2 <= i <= 13:
                        js = [i + 2 - c for c in range(5)]
                        tcol = 1024
                    else:
                        lo = 0 if i < 2 else 12
                        js = [lo + 3 - c for c in range(4)]
                        tcol = (7 - 2 * (js[0] - i)) * 64
                    its.append((i, js, tcol))
                its.append((16, [], 0))

                def stageA(t):
                    i, js, tcol = its[t]
                    par = t % 2
                    if i == 16:
                        def mmM(e):
                            ins = None
                            for hl in range(2):
                                r0 = hl * 64
                                ins = e.matmul(psS[hl][0:16, 768:784], lhsT=kT[b][r0:r0 + 64, 0:16],
                                               rhs=qT[b][r0:r0 + 64, 0:16], start=True, stop=True)
                            return ins
                        P.op("pe", mmM, reads=[("kT", b, 16), ("qT", b, 16)], writes=[("psS", 0), ("psS", 1)])
                        for hl in range(2):
                            P.op("act", (lambda hl: lambda e: e.activation(out=PTq[hl][0:16, 0:16], in_=psS[hl][0:16, 768:784],
                                                                          func=AF.Exp, scale=0.125))(hl),
                                 reads=[("psS", hl)], writes=[("PTq", hl)])
                        return
                    qc0 = 16 + 128 * i
                    ncol = 128 * len(js)

                    for hl in range(2):
                        pS = psS[hl]
                        r0 = hl * 64

                        def mmS(e, pS=pS, r0=r0):
                            for c, j in enumerate(js):
                                kc0 = 16 + 128 * j
                                e.matmul(pS[:, c * 128:(c + 1) * 128], lhsT=kT[b][r0:r0 + 64, kc0:kc0 + 128],
                                         rhs=qT[b][r0:r0 + 64, qc0:qc0 + 128], start=True, stop=True)
                            return e.matmul(pS[:, 768:896], lhsT=kT[b][r0:r0 + 64, 0:128],
                                            rhs=qT[b][r0:r0 + 64, qc0:qc0 + 128], start=True, stop=True)
                        P.op("pe", mmS, reads=[("kT", b, j) for j in js] + [("kT", b, 16), ("kT", b, 0), ("qT", b, i)],
                             writes=[("psS", hl)])
                        P.op("dve", (lambda hl, pS: lambda e: e.scalar_tensor_tensor(
                            out=S_sb[hl][:, 0:ncol], in0=pS[:, 0:ncol], scalar=0.125, in1=tab[b][:, hl, tcol:tcol + ncol],
                            op0=ALU.mult, op1=ALU.add))(hl, pS),
                            reads=[("psS", hl), ("tab", b)], writes=[("S_sb", hl)])
                        P.op("act", (lambda hl: lambda e: e.activation(out=PT[hl][par][:, 0:ncol], in_=S_sb[hl][:, 0:ncol],
                                                                      func=AF.Exp))(hl),
                             reads=[("S_sb", hl)], writes=[("PT", hl, par)])
                        P.op("act", (lambda hl, pS: lambda e: e.activation(out=PTm[hl][par][0:16, :], in_=pS[0:16, 768:896],
                                                                          func=AF.Exp, scale=0.125))(hl, pS),
                             reads=[("psS", hl), ("PTm0",)], writes=[("PTm", hl, par)])

                def stageB(t):
                    i, js, tcol = its[t]
                    par = t % 2
                    for hl in range(2):
                        pO = psO[hl]
                        head = 2 * hp + hl
                        if i == 16:
                            P.op("pe", (lambda hl, pO: lambda e: e.matmul(pO[0:16, 0:65], lhsT=PTq[hl][0:16, 0:16],
                                                                          rhs=vaug[b][0:16, 16, hl, :], start=True, stop=True))(hl, pO),
                                 reads=[("PTq", hl), ("vaug", b, 16)], writes=[("psO", hl)])
                            rows = 16
                        else:
                            def mmO(e, hl=hl, pO=pO):
                                for c, j in enumerate(js):
                                    e.matmul(pO[:, 0:65], lhsT=PT[hl][par][:, c * 128:(c + 1) * 128], rhs=vaug[b][:, j, hl, :],
                                             start=(c == 0), stop=False)
                                return e.matmul(pO[:, 0:65], lhsT=PTm[hl][par][:, :], rhs=vaug[b][:, 16, hl, :],
                                                start=False, stop=True)
                            P.op("pe", mmO, reads=[("PT", hl, par), ("PTm", hl, par), ("vz", b)] + [("vaug", b, j) for j in js] + [("vaug", b, 16)],
                                 writes=[("psO", hl)])
                            rows = 128
                        P.op("dve", (lambda hl, pO, rows: lambda e: e.reciprocal(out=rec[0:rows, hl:hl + 1], in_=pO[0:rows, 64:65]))(hl, pO, rows),
                             reads=[("psO", hl)], writes=[("rec", hl)])
                        P.op("act", (lambda hl, pO, rows, head: lambda e: e.activation(
                            out=ona[0:rows, i, head * 64:(head + 1) * 64], in_=pO[0:rows, 0:64], func=AF.Copy,
                            scale=rec[0:rows, hl:hl + 1]))(hl, pO, rows, head),
                            reads=[("psO", hl), ("rec", hl)], writes=[("ona", i, head)])

                n = len(its)
                for t in range(n + 1):
                    if t < n:
                        stageA(t)
                    if t >= 1:
                        stageB(t - 1)
                    if hp == 3 and t >= 2:
                        na_final(its[t - 2][0])
                    yield
                    yield
                if hp == 3:
                    na_final(its[n - 1][0])

            def na_final(i):
                c0, rows = tile_cols(i)
                ss = stat[:rows, 3, i:i + 1]
                ln = stat[:rows, 4, i:i + 1]
                rs = stat[:rows, 5, i:i + 1]
                P.op("act", (lambda rows, i, ss: lambda e: e.activation(out=junk[:rows, 0:512], in_=ona[:rows, i, :],
                                                                         func=AF.Square, accum_out=ss))(rows, i, ss),
                     reads=[("ona", i, h) for h in range(8)], writes=[("junk",), ("N", i, "ss")])
                rstd_ops(ss, ln, rs, 512, ("N", i))
                P.op("dve", (lambda rows, i, rs: lambda e: e.tensor_scalar(out=mix[:rows, i, 0:512], in0=ona[:rows, i, :],
                                                                            scalar1=rs, scalar2=None, op0=ALU.mult))(rows, i, rs),
                     reads=[("ona", i, h) for h in range(8)] + [("N", i, "rstd")], writes=[("mix", i, "na")])
            def precast_gen():
                blocks = [(w_f2_d[k_ * 128:(k_ + 1) * 128, :], w2bf_d[k_], ("w2bf", k_)) for k_ in range(NCH)]
                pin = [xin[k_][:, 0, :] for k_ in range(2)]
                pout = []
                for k_ in range(2):
                    v_ = xin[k_][:, 1, :].bitcast(BF16)
                    pout += [v_[:, 0:1024], v_[:, 1024:2048]]
                nb = len(blocks)

                def dma_in(b_):
                    k_ = b_ % 2
                    P.dma("sp", lambda e: e.dma_start(out=pin[k_], in_=blocks[b_][0]),
                          writes=[("pin", k_)] + ([("xin", k_)] if b_ < 2 else []))

                def dma_out(b_):
                    o_ = b_ % 4
                    P.dma("sp", lambda e: e.dma_start(out=blocks[b_][1], in_=pout[o_]),
                          reads=[("pout", o_)], writes=[blocks[b_][2]])
                dma_in(0)
                for b_ in range(nb):
                    if b_ + 1 < nb:
                        dma_in(b_ + 1)
                    P.op("pool", (lambda b_: lambda e: e.tensor_copy(out=pout[b_ % 4], in_=pin[b_ % 2]))(b_),
                         reads=[("pin", b_ % 2)], writes=[("pout", b_ % 4)] + ([("xin", b_ % 2)] if b_ < 2 else []))
                    if b_ >= 2:
                        dma_out(b_ - 2)
                    for _ in range(6):
                        yield
                dma_out(nb - 2)
                dma_out(nb - 1)
                yield

            pre_g = precast_gen()

            def take(g_, n_):
                for _ in range(n_):
                    try:
                        next(g_)
                    except StopIteration:
                        return
                    yield

            run(na_proj(0))
            for hp in range(4):
                if hp < 3:
                    interleave(na_attn(hp), gen_interleave(na_proj(hp + 1), take(pre_g, 36)), ratio=1)
                else:
                    interleave(na_attn(hp), gen_interleave(proj_fm(3072, 32, zT, ("zT",)), take(pre_g, 36)), ratio=1)
            run(pre_g)

            A.reset(mNA)
            P.barrier()
            A.hi = hiA
            if stage <= 2:
                raise _Stop()

            mG = A.mark()
            maskfb = A.alloc(F32, (4, 128))
            qg = A.alloc(F32, (UTW,))
            kg = A.alloc(F32, (UTW,))
            vg2 = [A.alloc(BF16, (17, 256)) for _ in range(2)]
            sg2 = [A.alloc(BF16, (17, 256)) for _ in range(2)]
            qdec = A.alloc(BF16, (2, UTW))
            kinvT = A.alloc(BF16, (2, UTW))
            kinv_tm = A.alloc(BF16, (17, 256))
            Sst = [A.alloc(BF16, (34, 128)) for _ in range(2)]
            dec = A.alloc(F32, (2, 34))
            Tst = [[A.alloc(F32, (128,)) for _ in range(2)] for _d in range(2)]
            lae = [A.alloc(F32, (256,)) for _ in range(3)]
            lap = [A.alloc(F32, (256,)) for _ in range(3)]
            e1 = [A.alloc(F32, (2, 128)) for _ in range(3)]
            e2 = [A.alloc(F32, (2, 128)) for _ in range(3)]
            ATb = [A.alloc(BF16, (4, 128)) for _ in range(2)]
            gst = A.alloc(F32, (3, 4))
            P.dma("sp", lambda e: e.dma_start(out=maskfb, in_=mask_d.rearrange("p (a b) -> p a b", a=4)), writes=[("maskfb",)])

            psA_ = [bank(2), bank(3)]
            psOg_bank = [[0, 1], [4, 5]]
            psOg = [[bank(b_) for b_ in row] for row in psOg_bank]


            def chunk_loc(n):
                if n == 0:
                    return 16, 0, 16
                t = (n - 1) // 2
                a = (n - 1) % 2
                return t, a * 64, 64

            def gla_proj_fm(hp):
                yield from proj_fm(1536 + hp * 128, 128, qg, ("qg",))
                yield from proj_fm(1792 + hp * 128, 128, kg, ("kg",))

            def gla_proj_tm(hp):
                vg, sg = vg2[hp % 2], sg2[hp % 2]

                def evv(i, rows, ps, pkey):
                    evac_copy(vg[:rows, i, :], ps, [pkey], [("vg", hp % 2, i)])
                yield from proj_tm(2048 + hp * 256, 256, evv)

                def evg(i, rows, ps, pkey):
                    evac_copy(sg[:rows, i, :], ps, [pkey], [("sg", hp % 2, i)], func=AF.Silu)
                yield from proj_tm(2560 + hp * 256, 256, evg)

            def gla_prep(hp, tiles, par):
                bLA, bBT, bKT = ((0, 1, 0), (4, 5, 1), (2, 3, 0))[par]
                pLA, kLA = bank(bLA), ("pb", bLA)
                pBT, kBT = bank(bBT), ("pb", bBT)
                pKT, kKT = bank(6 + bKT), ("psP", bKT)
                sl = par
                for i in tiles:
                    c0, rows = tile_cols(i)
                    zreads = [("zT", i), ("zT1",), ("upsb",)]

                    def mmla(e, c0=c0, rows=rows):
                        e.matmul(pLA[0:rows, 0:128], lhsT=zT[0:33, c0:c0 + rows], rhs=upsb[0:33, hp * 128:(hp + 1) * 128],
                                 start=True, stop=True)
                        return e.matmul(pLA[0:rows, 128:256], lhsT=zT[0:33, c0:c0 + rows],
                                        rhs=upsb[0:33, 256 + hp * 128:256 + (hp + 1) * 128], start=True, stop=True)
                    P.op("pe", mmla, reads=zreads, writes=[kLA])
                    yield
                    P.op("act", (lambda rows: lambda e: e.activation(out=lae[sl][:rows, :], in_=pLA[0:rows, 0:256],
                                                                      func=AF.Exp, scale=-1.0))(rows),
                         reads=[kLA], writes=[("lae", sl)])
                    P.op("act", (lambda rows: lambda e: e.activation(out=lap[sl][:rows, :], in_=lae[sl][:rows, :],
                                                                      func=AF.Ln, bias=1.0))(rows),
                         reads=[("lae", sl)], writes=[("lap", sl)])
                    yield

                    def mmbt(e, rows=rows):
                        e.matmul(pBT[:, 0:rows], lhsT=lap[sl][:rows, 0:128], rhs=tri[:rows, 0:rows],
                                 start=True, stop=True)
                        return e.matmul(pBT[:, 128:128 + rows], lhsT=lap[sl][:rows, 128:256], rhs=tri[:rows, 128:128 + rows],
                                        start=True, stop=True)
                    P.op("pe", mmbt, reads=[("lap", sl), ("tri",)], writes=[kBT])
                    yield
                    bt = pBT[:, 0:256].rearrange("p (a b) -> p a b", a=2)[:, :, 0:rows]
                    P.op("act", (lambda rows, bt: lambda e: e.activation(out=e1[sl][:, :, 0:rows], in_=bt, func=AF.Exp))(rows, bt),
                         reads=[kBT], writes=[("e1", sl)])
                    P.op("act", (lambda rows, bt: lambda e: e.activation(out=e2[sl][:, :, 0:rows], in_=bt, func=AF.Exp,
                                                                          scale=-1.0))(rows, bt),
                         reads=[kBT], writes=[("e2", sl)])
                    yield
                    for d_ in range(2):
                        P.op("dve", (lambda rows, c0, d_: lambda e: e.scalar_tensor_tensor(
                            out=qdec[:, d_, c0:c0 + rows], in0=qg[:, c0:c0 + rows], scalar=0.125, in1=e1[sl][:, d_, 0:rows],
                            op0=ALU.mult, op1=ALU.mult))(rows, c0, d_),
                            reads=[("qg", i), ("e1", sl)], writes=[("qdec", i)])
                        P.op("pool", (lambda rows, c0, d_: lambda e: e.tensor_tensor(
                            out=kinvT[:, d_, c0:c0 + rows], in0=kg[:, c0:c0 + rows], in1=e2[sl][:, d_, 0:rows],
                            op=ALU.mult))(rows, c0, d_),
                            reads=[("kg", i), ("e2", sl)], writes=[("kinvT", i, d_)])
                    if i < 16:
                        n1 = 2 * i + 1
                        P.op("dve", (lambda n1: lambda e: e.tensor_copy(
                            out=dec[:, 0, n1:n1 + 2], in_=e1[sl][:, 0, :].rearrange("p (a b) -> p a b", a=2)[:, :, 63]))(n1),
                            reads=[("e1", sl)], writes=[("dec", i, 0)])
                        P.op("dve", (lambda n1: lambda e: e.tensor_copy(
                            out=dec[:, 1, n1:n1 + 2], in_=e1[sl][:, 1, :].rearrange("p (a b) -> p a b", a=2)[:, :, 0]))(n1),
                            reads=[("e1", sl)], writes=[("dec", i, 1)])
                    else:
                        P.op("dve", lambda e: e.tensor_copy(out=dec[:, 0, 0:1], in_=e1[sl][:, 0, 15:16]),
                             reads=[("e1", sl)], writes=[("dec", i, 0)])
                        P.op("dve", lambda e: e.tensor_copy(out=dec[:, 1, 0:1], in_=e1[sl][:, 1, 0:1]),
                             reads=[("e1", sl)], writes=[("dec", i, 1)])
                    yield
                    pkt = pKT.bitcast(BF16)

                    def trk(e, rows=rows, c0=c0, pkt=pkt):
                        e.transpose(out=pkt[0:rows, 0:128], in_=kinvT[:, 0, c0:c0 + rows], identity=ident[:, :])
                        return e.transpose(out=pkt[0:rows, 128:256], in_=kinvT[:, 1, c0:c0 + rows], identity=ident[:, :])
                    P.op("pe", trk, reads=[("kinvT", i, 0), ("kinvT", i, 1), ("ident",)], writes=[kKT])
                    P.op("dve", (lambda rows, i, pkt: lambda e: e.tensor_copy(out=kinv_tm[:rows, i, :], in_=pkt[0:rows, 0:256]))(rows, i, pkt),
                         reads=[kKT], writes=[("kinv_tm", i)])
                    yield

            def gla_scan(hp, d_):
                vg = vg2[hp % 2]
                order = list(range(33)) if d_ == 0 else list(range(32, -1, -1))
                banks_ = (4, 5) if d_ == 0 else (2, 3)
                prev = None
                for step, n in enumerate(order):
                    t, r0, rows = chunk_loc(n)
                    bk = banks_[step % 2]
                    pD = bank(bk)[:, 0:128]

                    def mmds(e, pD=pD, t=t, r0=r0, rows=rows):
                        ins = None
                        for hl in range(2):
                            ins = e.matmul(pD[hl * 64:(hl + 1) * 64, :],
                                           lhsT=kinv_tm[r0:r0 + rows, t, d_ * 128 + hl * 64:d_ * 128 + (hl + 1) * 64],
                                           rhs=vg[r0:r0 + rows, t, hl * 128:(hl + 1) * 128],
                                           start=True, stop=True, tile_position=(r0, hl * 64))
                        return ins
                    P.op("pe", mmds, reads=[("kinv_tm", t), ("vg", hp % 2, t)], writes=[("pb", bk)])
                    tn = step % 2
                    if prev is None:
                        P.op("dve", (lambda pD, tn: lambda e: e.tensor_copy(out=Tst[d_][tn], in_=pD))(pD, tn),
                             reads=[("pb", bk)], writes=[("T", d_, tn)])
                    else:
                        P.op("dve", (lambda pD, tn, prev: lambda e: e.scalar_tensor_tensor(
                            out=Tst[d_][tn], in0=Tst[d_][1 - tn], scalar=dec[:, d_, prev:prev + 1], in1=pD,
                            op0=ALU.mult, op1=ALU.add))(pD, tn, prev),
                            reads=[("pb", bk), ("T", d_, 1 - tn)] + [("dec", i, d_) for i in range(17)], writes=[("T", d_, tn)])
                    if step < 32:
                        P.op("act", (lambda tn, n: lambda e: e.activation(out=Sst[d_][:, n, :], in_=Tst[d_][tn], func=AF.Copy,
                                                                          scale=dec[:, d_, n:n + 1]))(tn, n),
                             reads=[("T", d_, tn)] + [("dec", i, d_) for i in range(17)], writes=[("S", d_, n)])
                    prev = n
                    yield

            def gla_out(hp):
                tiles = list(range(16)) + [16]
                vg, sg = vg2[hp % 2], sg2[hp % 2]

                def stageA(t):
                    i = tiles[t]
                    c0, rows = tile_cols(i)
                    sl = t % 2

                    for hl in range(2):
                        def mmA(e, hl=hl):
                            ins = None
                            for d_ in range(2):
                                ins = e.matmul(psA_[hl][0:rows, d_ * 128:d_ * 128 + rows],
                                               lhsT=kinvT[hl * 64:(hl + 1) * 64, d_, c0:c0 + rows],
                                               rhs=qdec[hl * 64:(hl + 1) * 64, d_, c0:c0 + rows], start=True, stop=True)
                            return ins
                        P.op("pe", mmA, reads=[("kinvT", i, 0), ("kinvT", i, 1), ("qdec", i)], writes=[("pb", 2 + hl)])
                        pav = psA_[hl][:, 0:256].rearrange("p (a b) -> p a b", a=2)
                        P.op("dve", (lambda pav, hl: lambda e: e.tensor_tensor(
                            out=ATb[sl][:rows, 2 * hl:2 * hl + 2, 0:rows], in0=pav[0:rows, :, 0:rows],
                            in1=maskfb[:rows, 0:2, 0:rows], op=ALU.mult))(pav, hl),
                            reads=[("pb", 2 + hl), ("maskfb",)], writes=[("AT", sl, hl)])

                def stageB(t):
                    i = tiles[t]
                    c0, rows = tile_cols(i)
                    sl = t % 2
                    if i < 16:
                        chunks = [(2 * i + 1, 0), (2 * i + 2, 64)]
                    else:
                        chunks = [(0, 0)]
                    sreads = []
                    for (n, _r) in chunks:
                        if n >= 1:
                            sreads.append(("S", 0, n - 1))
                        if n <= 31:
                            sreads.append(("S", 1, n + 1))
                    for hl in range(2):
                        pO = psOg[sl][hl]
                        pkey = ("pb", psOg_bank[sl][hl])
                        oc = slice(hl * 128, (hl + 1) * 128)

                        def mmO(e, pO=pO, hl=hl, oc=oc):
                            extra = []
                            for (n, r0) in chunks:
                                cr = 64 if i < 16 else 16
                                if n >= 1:
                                    extra.append((0, n - 1, r0, cr))
                                if n <= 31:
                                    extra.append((1, n + 1, r0, cr))
                            e.matmul(pO[0:rows, 0:128], lhsT=ATb[sl][:rows, hl * 2, 0:rows], rhs=vg[:rows, i, oc],
                                     start=True, stop=False)
                            ins = e.matmul(pO[0:rows, 0:128], lhsT=ATb[sl][:rows, hl * 2 + 1, 0:rows], rhs=vg[:rows, i, oc],
                                           start=False, stop=(len(extra) == 0))
                            for xi, (d_, sn, r0, cr) in enumerate(extra):
                                last_of_range = (xi == len(extra) - 1) or (extra[xi + 1][2] != r0)
                                ins = e.matmul(pO[r0:r0 + cr, 0:128], lhsT=qdec[hl * 64:(hl + 1) * 64, d_, c0 + r0:c0 + r0 + cr],
                                               rhs=Sst[d_][hl * 64:(hl + 1) * 64, sn, :], start=False,
                                               stop=last_of_range, tile_position=(hl * 64, r0))
                            return ins
                        P.op("pe", mmO, reads=[("AT", sl, hl), ("vg", hp % 2, i), ("qdec", i)] + sreads, writes=[pkey])
                        gi = sl * 2 + hl
                        P.op("act", (lambda pO, gi: lambda e: e.activation(
                            out=junk[:rows, 0:128], in_=pO[0:rows, 0:128], func=AF.Square,
                            accum_out=gst[:rows, 0, gi:gi + 1]))(pO, gi),
                            reads=[pkey], writes=[("junk",), ("G", gi, "ss")])
                        rstd_ops(gst[:rows, 0, gi:gi + 1], gst[:rows, 1, gi:gi + 1], gst[:rows, 2, gi:gi + 1], 128, ("G", gi))
                        head = 2 * hp + hl
                        P.op("dve", (lambda pO, hl, head, gi: lambda e: e.scalar_tensor_tensor(
                            out=mix[:rows, i, 512 + head * 128:512 + (head + 1) * 128], in0=pO[0:rows, 0:128],
                            scalar=gst[:rows, 2, gi:gi + 1], in1=sg[:rows, i, hl * 128:(hl + 1) * 128],
                            op0=ALU.mult, op1=ALU.mult))(pO, hl, head, gi),
                            reads=[pkey, ("G", gi, "rstd"), ("sg", hp % 2, i)], writes=[("mix", i, "g", head)])

                n = len(tiles)
                for t in range(n + 1):
                    if t < n:
                        stageA(t)
                    if t >= 1:
                        stageB(t - 1)
                    yield

            def prep_all(hp):
                return gen_interleave(gen_interleave(gla_prep(hp, [16, 0, 3, 6, 9, 12, 15], 0),
                                                     gla_prep(hp, [1, 4, 7, 10, 13], 1)),
                                      gla_prep(hp, [2, 5, 8, 11, 14], 2))

            run(gla_proj_fm(0))
            interleave(prep_all(0), gla_proj_tm(0), ratio=1)
            interleave(gen_chain(gen_interleave(gla_scan(0, 0), gla_scan(0, 1)), gla_out(0)),
                       gen_chain(gla_proj_fm(1), gla_proj_tm(1)), ratio=1)
            run(prep_all(1))
            run(gen_interleave(gla_scan(1, 0), gla_scan(1, 1)))
            run(gla_out(1))
            A.reset(mG)
            P.barrier()
            if stage <= 3:
                raise _Stop()

            A.reset(mB2)
            h1 = A.alloc(F32, (17, 1024), top=True)
            dump_reg["h1"] = (h1, F32)
            mC = A.mark()
            wo_sb = A.alloc(BF16, (8, 1024))
            wstg = [A.alloc(F32, (1024,)) for _ in range(2)]
            grepP = A.alloc(F32, (1024,))
            grepF = A.alloc(F32, (1024,))
            tmpE = [A.alloc(F32, (1024,)) for _ in range(2)]
            xs2 = [A.alloc(BF16, (1024,)) for _ in range(4)]
            P.dma("sp", lambda e: e.dma_start(out=grepP, in_=grep_d[1]), writes=[("grep", "P")])
            P.dma("sp", lambda e: e.dma_start(out=grepF, in_=grep_d[2]), writes=[("grep", "F")])
            for k in range(8):
                sl = k % 2
                P.dma("sp" if k % 2 == 0 else "act",
                      (lambda k, sl: lambda e: e.dma_start(out=wstg[sl], in_=w_o_d[k * 128:(k + 1) * 128, :]))(k, sl),
                      writes=[("wstg", sl)])
                P.op("act" if k % 2 == 0 else "dve",
                     (lambda k, sl: lambda e: e.activation(out=wo_sb[:, k, :], in_=wstg[sl], func=AF.Copy,
                                                           scale=gfm[:, k:k + 1])) (k, sl) if k % 2 == 0 else
                     (lambda k, sl: lambda e: e.tensor_scalar(out=wo_sb[:, k, :], in0=wstg[sl], scalar1=gfm[:, k:k + 1],
                                                              scalar2=None, op0=ALU.mult))(k, sl),
                     reads=[("wstg", sl), ("gfm",)], writes=[("wo_sb", k)])
            P.dma("sp", lambda e: e.dma_start(out=h1[:16, 16, :], in_=meta_d), writes=[("h1", 16)])
            for pr in range(8):
                srcd = x_d[pr * 256:(pr + 1) * 256, :].rearrange("(t p) d -> p t d", p=128)
                P.dma("sp" if pr % 2 == 0 else "act",
                      (lambda pr, srcd: lambda e: e.dma_start(out=h1[:, 2 * pr:2 * pr + 2, :], in_=srcd))(pr, srcd),
                      writes=[("h1", 2 * pr), ("h1", 2 * pr + 1)])
            for i in range(17):
                c0, rows = tile_cols(i)
                sl = i % 2
                psv = bank(sl).bitcast(BF16)

                def trm(e, i=i, rows=rows, psv=psv):
                    ins = None
                    for k in range(8):
                        ins = e.transpose(out=psv[:, k * 128:k * 128 + rows], in_=mix[:rows, i, k * 128:(k + 1) * 128],
                                          identity=ident[:rows, :rows])
                    return ins
                mreads = [("mix", i, "na")] + [("mix", i, "g", h) for h in range(4)] + [("ident",)]
                P.op("pe", trm, reads=mreads, writes=[("ps", sl)])
                src_v = psv.rearrange("p (k c) -> p k c", k=8)[:, :, 0:rows]
                evac_copy(uT[:, :, c0:c0 + rows], src_v, [("ps", sl)], [("uT", i)])
            psY = [bank(2, 2), bank(4, 2)]

            finC = [None]

            def u2_tile(j):
                _c0, rj = tile_cols(j)
                fin_ = norm_transpose(j, h1[:rj, j, :], ("h1", j), 16, xs2[j % 4], ("xs2", j % 4),
                                      bank(6 + j % 2), ("psT2", j % 2), 3, "F")
                if finC[0] is not None:
                    finC[0]()
                finC[0] = fin_
            for i in range(17):
                c0, rows = tile_cols(i)
                sl = i % 2
                pY = psY[sl]

                def mmY(e, c0=c0, rows=rows, pY=pY):
                    ins = None
                    for nh in range(2):
                        for k in range(8):
                            ins = e.matmul(pY[0:rows, nh * 512:(nh + 1) * 512], lhsT=uT[:, k, c0:c0 + rows],
                                           rhs=wo_sb[:, k, nh * 512:(nh + 1) * 512], start=(k == 0), stop=(k == 7))
                    return ins
                P.op("pe", mmY, reads=[("uT", i)] + [("wo_sb", k) for k in range(8)], writes=[("psY", sl)])
                if i >= 3:
                    u2_tile(i - 3)
                ss = stat[:rows, 0, i:i + 1]
                ln = stat[:rows, 1, i:i + 1]
                rs = stat[:rows, 2, i:i + 1]
                P.op("act", (lambda rows, pY, ss: lambda e: e.activation(out=junk[:rows, :], in_=pY[0:rows, :], func=AF.Square,
                                                                          accum_out=ss))(rows, pY, ss),
                     reads=[("psY", sl)], writes=[("junk",), ("C", i, "ss")])
                rstd_ops(ss, ln, rs, D, ("C", i))
                P.op("dve", (lambda sl, rows, pY, rs: lambda e: e.scalar_tensor_tensor(
                    out=tmpE[sl][:rows, :], in0=pY[0:rows, :], scalar=rs, in1=grepP[:rows, :], op0=ALU.mult, op1=ALU.mult))(sl, rows, pY, rs),
                    reads=[("psY", sl), ("C", i, "rstd"), ("grep", "P")], writes=[("tmpE", sl)])
                P.op("pool", (lambda sl, rows, i: lambda e: e.tensor_tensor(out=h1[:rows, i, :], in0=h1[:rows, i, :],
                                                                             in1=tmpE[sl][:rows, :], op=ALU.add))(sl, rows, i),
                     reads=[("tmpE", sl), ("h1", i)], writes=[("h1", i)])
            u2_tile(14)
            u2_tile(15)
            u2_tile(16)
            finC[0]()
            A.reset(mB)
            P.barrier()
            if stage <= 4:
                raise _Stop()

            hid = A.alloc(BF16, (NCH, 1024))
            cw = A.alloc(F32, (44, 4))
            grepO = A.alloc(F32, (1024,))
            wf = [A.alloc(BF16, (2, 8, 128)) for _ in range(3)]
            fstg = [A.alloc(F32, (8, 128)) for _ in range(3)]
            wo2 = [A.alloc(BF16, (1024,)) for _ in range(4)]
            accV = [A.alloc(F32, (344,)) for _ in range(3)]
            accG = [A.alloc(F32, (344,)) for _ in range(3)]
            gel = [A.alloc(F32, (344,)) for _ in range(2)]
            tmpO = [A.alloc(F32, (1024,)) for _ in range(2)]
            P.dma("sp", lambda e: e.dma_start(out=cw, in_=cw_d.rearrange("p (c j) -> p c j", j=4)), writes=[("cw",)])
            P.dma("sp", lambda e: e.dma_start(out=grepO, in_=grep_d[3]), writes=[("grep", "O")])

            pb_i = [0]
            cv_i = [0]
            wf_i = [0]
            wo_i = [0]
            ep_i = [0]
            NTS = [(0, 342), (342, 342), (684, 340)]
            for hf in range(2):
                T0 = hf * 1024
                def load_wf(c, hf=hf):
                    sl_ = c % 3
                    for vg_ in range(2):
                        blk_ = vg_ * NCH + c
                        scr = w1bf_d[blk_].rearrange("p (k c) -> p k c", k=8)
                        if hf == 0:
                            srcap = w_f1_d[blk_].rearrange("p (k c) -> p k c", k=8)
                            load_cast(srcap, wf[sl_][:, vg_, :, :], ("wf", sl_, vg_), fstg, eng=("act" if vg_ == 0 else "dve"))
                            P.dma("act" if vg_ == 0 else "sp",
                                  (lambda sl_, vg_, scr: lambda e: e.dma_start(out=scr, in_=wf[sl_][:, vg_, :, :]))(sl_, vg_, scr),
                                  reads=[("wf", sl_, vg_)], writes=[("w1bf", blk_)])
                        else:
                            P.dma("sp", (lambda sl_, vg_, scr: lambda e: e.dma_start(out=wf[sl_][:, vg_, :, :], in_=scr))(sl_, vg_, scr),
                                  reads=[("w1bf", blk_)], writes=[("wf", sl_, vg_)])
                def tail_ops(cs, n, c, o0):
                    P.op("act", lambda e: e.activation(out=gel[cs % 2][:, 0:n], in_=accG[cs][:, 0:n], func=AF.Gelu_apprx_tanh),
                         reads=[("acc", cs, 1)], writes=[("gel", cs % 2)])
                    P.op("pool", lambda e: e.tensor_tensor(out=hid[:, c, o0:o0 + n], in0=accV[cs][:, 0:n],
                                                           in1=gel[cs % 2][:, 0:n], op=ALU.mult),
                         reads=[("acc", cs, 0), ("gel", cs % 2)], writes=[("hid", c, o0)])
                pending = [None]
                load_wf(0)
                for c in range(NCH):
                    sl = c % 3
                    if c + 1 < NCH:
                        load_wf(c + 1)
                    for (o0, n) in NTS:
                        ci = 16 + T0 + o0 - 1
                        nin = n + 2
                        tl = tiles_overlapping(ci, min(nin, 2064 - ci))
                        pbs = []
                        for vg_ in range(2):
                            pb = pb_i[0] % 8
                            pb_i[0] += 1
                            pbs.append(pb)
                            ps = bank(pb)

                            def mm(e, sl=sl, vg_=vg_, ci=ci, nin=nin, ps=ps):
                                ins = None
                                for k in range(8):
                                    ins = e.matmul(ps[:, 0:nin], lhsT=wf[sl][:, vg_, k, :],
                                                   rhs=uT[:, k, ci:ci + nin], start=(k == 0), stop=(k == 7))
                                return ins
                            P.op("pe", mm, reads=[("wf", sl, vg_), ("uTpad",)] + [("uT", t) for t in tl],
                                 writes=[("psF", pb)])
                        cs = cv_i[0] % 3
                        cv_i[0] += 1
                        for vg_, acc in ((0, accV[cs]), (1, accG[cs])):
                            ps = bank(pbs[vg_])
                            cc = vg_ * NCH + c
                            P.op("act", (lambda ps, acc, cc, n: lambda e: e.activation(
                                out=acc[:, 0:n], in_=ps[:, 1:n + 1], func=AF.Identity, scale=cw[:, cc, 1:2], bias=cw[:, cc, 3:4]))(ps, acc, cc, n),
                                reads=[("psF", pbs[vg_]), ("cw",)], writes=[("acc", cs, vg_)])
                            P.op("dve", (lambda ps, acc, cc, n: lambda e: e.scalar_tensor_tensor(
                                out=acc[:, 0:n], in0=ps[:, 0:n], scalar=cw[:, cc, 0:1], in1=acc[:, 0:n],
                                op0=ALU.mult, op1=ALU.add))(ps, acc, cc, n),
                                reads=[("psF", pbs[vg_]), ("cw",), ("acc", cs, vg_)], writes=[("acc", cs, vg_)])
                            P.op("dve", (lambda ps, acc, cc, n: lambda e: e.scalar_tensor_tensor(
                                out=acc[:, 0:n], in0=ps[:, 2:n + 2], scalar=cw[:, cc, 2:3], in1=acc[:, 0:n],
                                op0=ALU.mult, op1=ALU.add))(ps, acc, cc, n),
                                reads=[("psF", pbs[vg_]), ("cw",), ("acc", cs, vg_)], writes=[("acc", cs, vg_)])
                        if pending[0] is not None:
                            pending[0]()
                        pending[0] = (lambda cs, n, c, o0: lambda: tail_ops(cs, n, c, o0))(cs, n, c, o0)
                if pending[0] is not None:
                    pending[0]()
                    pending[0] = None
                def load_wo(q):
                    k_ = q % NCH
                    P.dma("sp", (lambda q, k_: lambda e: e.dma_start(out=wo2[q % 4], in_=w2bf_d[k_]))(q, k_),
                          reads=[("w2bf", k_)], writes=[("wo2", q % 4)])

                qbase = wo_i[0]
                nq_ = 2 * NCH
                load_wo(qbase)
                load_wo(qbase + 1)
                load_wo(qbase + 2)
                for blk in range(2):
                    for k in range(NCH):
                        q = wo_i[0]
                        ws = q % 4
                        wo_i[0] += 1
                        if q + 3 < qbase + nq_:
                            load_wo(q + 3)

                        for tl_ in range(4):
                            tc0 = (blk * 4 + tl_) * 128

                            def mm2(e, ws=ws, k=k, tl_=tl_, tc0=tc0):
                                ins = None
                                for nh in range(2):
                                    ins = e.matmul(bank(tl_ * 2 + nh), lhsT=hid[:, k, tc0:tc0 + 128],
                                                   rhs=wo2[ws][:, nh * 512:(nh + 1) * 512], start=(k == 0), stop=(k == NCH - 1))
                                return ins
                            P.op("pe", mm2, reads=[("wo2", ws)] + [("hid", k, o0) for (o0, _n) in NTS],
                                 writes=[("psF", tl_ * 2), ("psF", tl_ * 2 + 1)])
                    for tl_ in range(4):
                        i = hf * 8 + blk * 4 + tl_
                        pY = bank(tl_ * 2, 2)
                        es_ = ep_i[0] % 2
                        ep_i[0] += 1
                        ss = stat[:, 0, i:i + 1]
                        ln = stat[:, 1, i:i + 1]
                        rs = stat[:, 2, i:i + 1]
                        pk = [("psF", tl_ * 2), ("psF", tl_ * 2 + 1)]
                        P.op("act", (lambda pY, ss: lambda e: e.activation(out=junk[:, :], in_=pY, func=AF.Square, accum_out=ss))(pY, ss),
                             reads=pk, writes=[("junk",), ("O", i, "ss")])
                        rstd_ops(ss, ln, rs, D, ("O", i))
                        P.op("dve", (lambda es_, pY, rs: lambda e: e.scalar_tensor_tensor(
                            out=tmpO[es_], in0=pY, scalar=rs, in1=grepO, op0=ALU.mult, op1=ALU.mult))(es_, pY, rs),
                            reads=pk + [("O", i, "rstd"), ("grep", "O")], writes=[("tmpO", es_)])
                        P.op("dve" if (hf == 1 and blk == 1 and tl_ % 2 == 1) else "pool",
                             (lambda es_, i: lambda e: e.tensor_tensor(out=h1[:, i, :], in0=h1[:, i, :], in1=tmpO[es_],
                                                                       op=ALU.add))(es_, i),
                             reads=[("tmpO", es_), ("h1", i)], writes=[("h1", i)])
                        P.dma("sp", (lambda i: lambda e: e.dma_start(out=y_d[i * 128:(i + 1) * 128, :], in_=h1[:, i, :]))(i),
                              reads=[("h1", i)])
        except _Stop:
            pass
        dump_all()
        P.finish()
        print("SBUF peak bytes/partition:", A.peak, " ops:", {e: len(P.streams[e]) for e in ENGS})

        @block.tensor
        def _(e):
            P.emit("pe", e, sems)

        @block.scalar
        def _(e):
            P.emit("act", e, sems)

        @block.vector
        def _(e):
            P.emit("dve", e, sems)

        @block.gpsimd
        def _(e):
            P.emit("pool", e, sems)

        @block.sync
        def _(e):
            P.emit("sp", e, sems)
    return nc


def _na_tables(rpb):
    H = rpb.shape[0]
    cq = np.arange(64)
    cs = np.clip(cq - 8, 0, 48)
    ck = np.arange(64)
    in_win = (ck[None, :] >= cs[:, None]) & (ck[None, :] < cs[:, None] + 16)
    dc = np.clip(ck[:, None] - cq[None, :], -15, 15) + 15
    G = rpb[:, :, dc]
    G = np.where(in_win.T[None, None], G, np.float32(NEG)).astype(np.float32)
    tabs = np.full((H, 128, 1664), NEG, np.float32)
    for a in range(2):
        for u in range(16):
            dr = a + 14 - u
            if 0 <= dr <= 14:
                tabs[:, a * 64:(a + 1) * 64, u * 64:(u + 1) * 64] = G[:, dr]
    for c in range(5):
        for b in range(2):
            u = 3 + 2 * c + b
            for a in range(2):
                dr = a + 14 - u
                if 3 <= dr <= 10:
                    tabs[:, a * 64:(a + 1) * 64, 1024 + (2 * c + b) * 64:1024 + (2 * c + b + 1) * 64] = G[:, dr]
    return tabs


def _consts():
    s = np.arange(128)
    same = (s[:, None] // 64) == (s[None, :] // 64)
    trif = np.where(same & (s[:, None] <= s[None, :]), -1.0 / 16.0, 0.0)
    trib = np.where(same & (s[:, None] >= s[None, :]), -1.0 / 16.0, 0.0)
    tri = np.concatenate([trif, trib], axis=1).astype(np.float32)
    mf = (same & (s[:, None] <= s[None, :])).astype(np.float32)
    mb = (same & (s[:, None] > s[None, :])).astype(np.float32)
    maskfb = np.concatenate([mf, mb, mf, mb], axis=1).astype(np.float32)
    ident = np.eye(128, dtype=np.float32)
    return ident, tri, maskfb


def _tile_weights(w_in, w_f1):
    wp = np.zeros((1024, 3200), np.float32)
    wp[:, :3104] = w_in
    w_in_t = np.ascontiguousarray(wp.reshape(8, 128, 25, 128).transpose(2, 1, 0, 3)).reshape(25, 128, 1024)
    w_f1_t = np.ascontiguousarray(w_f1.reshape(8, 128, 44, 128).transpose(2, 1, 0, 3)).reshape(44, 128, 1024)
    return w_in_t, w_f1_t


_NC_CACHE = {}


def kernel(x, meta_tokens, norm_mix_pre, w_in, na_rel_bias, na_out_gain, gla_gate_up_fwd,
           gla_gate_bias_fwd, gla_gate_up_bwd, gla_gate_bias_bwd, gla_out_gain, w_o, norm_mix_post,
           norm_ffn_pre, w_ffn_in, ffn_conv_w, ffn_conv_b, w_ffn_out, norm_ffn_post):
    f = lambda a: np.ascontiguousarray(np.asarray(a, dtype=np.float32))
    x = f(x)
    B = x.shape[0]
    up33 = np.zeros((33, 512), np.float32)
    up33[0:16, 0:256] = f(gla_gate_up_fwd)[0]
    up33[16:32, 256:512] = f(gla_gate_up_bwd)[0]
    up33[32, 0:256] = f(gla_gate_bias_fwd)[0]
    up33[32, 256:512] = f(gla_gate_bias_bwd)[0]
    tabs = _na_tables(f(na_rel_bias)[0])
    gvec_o = np.concatenate([f(na_out_gain)[0], np.tile(f(gla_out_gain)[0], 4)])
    gfm = f(np.concatenate([gvec_o.reshape(8, 128).T, f(norm_mix_pre)[0].reshape(8, 128).T,
                            f(norm_ffn_pre)[0].reshape(8, 128).T], axis=1))
    grep = np.stack([np.broadcast_to(f(v)[0][None, :], (128, D)) for v in
                     (norm_mix_pre, norm_mix_post, norm_ffn_pre, norm_ffn_post)])
    grep = f(grep)
    cwj = f(ffn_conv_w)[0].reshape(3, 44, 128)
    cbj = f(ffn_conv_b)[0].reshape(1, 44, 128)
    cw = f(np.concatenate([cwj, cbj], axis=0).transpose(2, 1, 0).reshape(128, 44 * 4))
    ident, tri, maskfb = _consts()
    w_in_t, w_f1_t = _tile_weights(f(w_in)[0], f(w_ffn_in)[0])
    shared = {
        "meta": f(meta_tokens), "w_in_t": w_in_t, "up33": up33, "tabs": tabs, "w_o": f(w_o)[0],
        "w_f1_t": w_f1_t, "w_f2": f(w_ffn_out)[0], "cw": cw, "gfm": gfm, "grep": grep,
        "identf": ident, "tri": tri, "maskfb": maskfb,
    }
    if "nc" not in _NC_CACHE:
        _NC_CACHE["nc"] = build_program()
    nc = _NC_CACHE["nc"]
    in_maps = []
    for b in range(B):
        m = dict(shared)
        m["x"] = np.ascontiguousarray(x[b])
        in_maps.append(m)
    res = run_bass_kernel_spmd(nc, in_maps, core_ids=list(range(B)))
    out = np.stack([np.asarray(r["y"], dtype=np.float32) for r in res.results], axis=0)
    return out
```

```python
import numpy as np
import concourse.bass as bass
import concourse.mybir as mybir
from concourse.bass_utils import run_bass_kernel_spmd

F32 = mybir.dt.float32
BF16 = mybir.dt.bfloat16
AF = mybir.ActivationFunctionType
ALU = mybir.AluOpType
AX = mybir.AxisListType

D = 1024
SEQ = 2048
NMETA = 16
NT_X = 16
UTW = 2072
DFF = 2816
NCH = 22
EPS = 1e-6
NEG = -30000.0
import os
NDS = int(os.environ.get("K_NDS", "24"))
NQS = 8
ENGS = ("pe", "act", "dve", "pool", "sp")
DEBUG = False


class _Stop(Exception):
    pass


class Prog:
    def __init__(self):
        self.streams = {e: [] for e in ENGS}
        self.cnt = {e: 0 for e in ENGS}
        self.seen = {e: {} for e in ENGS}
        self.lastw = {}
        self.readers = {}
        self.ndma = 0
        self.nq = 0
        self.dma_last = {}

    def _deps(self, reads, writes):
        deps = {}

        def add(tok):
            k, v = tok
            if deps.get(k, 0) < v:
                deps[k] = v
        for r in reads:
            if r in self.lastw:
                add(self.lastw[r])
        for w in writes:
            if w in self.lastw:
                add(self.lastw[w])
            for k, v in self.readers.get(w, {}).items():
                add((k, v))
        return deps

    def _commit(self, tok, reads, writes):
        k, v = tok
        for r in reads:
            d = self.readers.setdefault(r, {})
            if d.get(k, 0) < v:
                d[k] = v
        for w in writes:
            self.lastw[w] = tok
            self.readers[w] = {}

    def _waits(self, eng, deps):
        snaps = getattr(self, "snaps", None)
        if snaps is None:
            snaps = self.snaps = {}
        seen = self.seen[eng]
        items = sorted(deps.items(), key=lambda kv: -len(snaps.get(kv, ())))
        waits = []
        for k, v in items:
            if seen.get(k, 0) >= v:
                continue
            waits.append((k, v))
            seen[k] = v
            for k2, v2 in snaps.get((k, v), {}).items():
                if seen.get(k2, 0) < v2:
                    seen[k2] = v2
        return waits

    def op(self, eng, fn, reads=(), writes=()):
        deps = self._deps(reads, writes)
        waits = self._waits(eng, deps)
        self.cnt[eng] += 1
        tok = (eng, self.cnt[eng])
        sn = dict(self.seen[eng])
        sn[eng] = max(sn.get(eng, 0), self.cnt[eng] - 1)
        self.snaps[tok] = sn
        self.streams[eng].append((waits, fn, None))
        self._commit(tok, reads, writes)
        return tok

    def dma(self, eng, fn, reads=(), writes=()):
        deps = self._deps(reads, writes)
        if eng == "pool":
            idx = ("q", self.nq)
            self.nq += 1
            assert self.nq <= NQS
            val = 16
        else:
            idx = ("d", self.ndma % NDS)
            val = 16 * (self.ndma // NDS + 1)
            self.ndma += 1
            if val > 16:
                deps[idx] = max(deps.get(idx, 0), val - 16)
        waits = self._waits(eng, deps)
        self.streams[eng].append((waits, fn, idx))
        tok = (idx, val)
        self.snaps[tok] = dict(self.seen[eng])
        self.dma_last[idx] = val
        self._commit(tok, reads, writes)
        return tok

    def barrier(self):
        snap = dict(self.cnt)
        dsnap = dict(self.dma_last)
        for e in ENGS:
            deps = {o: snap[o] for o in ENGS if o != e and snap[o] > 0}
            for idx, v in dsnap.items():
                deps[idx] = v
            waits = self._waits(e, deps)
            if waits:
                self.streams[e].append((waits, None, None))

    def finish(self):
        deps = {idx: v for idx, v in self.dma_last.items()}
        waits = self._waits("sp", deps)
        if waits:
            self.streams["sp"].append((waits, None, None))

    def emit(self, eng, handle, sems):
        for waits, fn, dmaidx in self.streams[eng]:
            for k, v in waits:
                handle.wait_ge(sems[k], v)
            if fn is None:
                continue
            ins = fn(handle)
            if dmaidx is None:
                ins.then_inc(sems[eng], 1)
            else:
                ins.then_inc(sems[dmaidx], 16)


class Arena:
    def __init__(self, tensor, nbytes):
        self.t = tensor
        self.n = nbytes
        self.top = 0
        self.hi = nbytes
        self.peak = 0

    def mark(self):
        return self.top

    def reset(self, m):
        self.top = m

    def alloc(self, dtype, shape, top=False):
        esz = 4 if dtype == F32 else 2
        nel = int(np.prod(shape))
        nb = (nel * esz + 15) // 16 * 16
        if top:
            self.hi -= nb
            off = self.hi
        else:
            off = self.top
            self.top += nb
        self.peak = max(self.peak, self.top + (self.n - self.hi))
        assert self.top <= self.hi, f"SBUF arena overflow {self.top} > {self.hi}"
        ap = self.t[:, off // 4:(off + nb) // 4]
        if dtype == BF16:
            ap = ap.bitcast(BF16)
        ap = ap[:, 0:nel]
        if len(shape) == 2:
            ap = ap.rearrange("p (a b) -> p a b", a=shape[0])
        elif len(shape) == 3:
            ap = ap.rearrange("p (a b c) -> p a b c", a=shape[0], b=shape[1])
        return ap


def tile_cols(i):
    if i == 16:
        return 0, 16
    return 16 + 128 * i, 128


def tiles_overlapping(c0, n):
    res = []
    for i in range(17):
        a, w = tile_cols(i)
        if a < c0 + n and c0 < a + w:
            res.append(i)
    return res


FM_NT = [(0, 512), (512, 512), (1024, 512), (1536, 512), (2048, 16)]


def build_program(stage=9, dumps=()):
    nc = bass.Bass("TRN2", target_bir_lowering=False)

    def din(name, shape):
        return nc.dram_tensor(name, list(shape), F32, kind="ExternalInput").ap()

    x_d = din("x", (SEQ, D))
    meta_d = din("meta", (NMETA, D))
    w_in_d = din("w_in_t", (25, 128, 1024))
    up_d = din("up33", (33, 512))
    tabs_d = din("tabs", (8, 128, 1664))
    w_o_d = din("w_o", (D, D))
    w_f1_d = din("w_f1_t", (44, 128, 1024))
    w_f2_d = din("w_f2", (DFF, D))
    cw_d = din("cw", (128, 44 * 4))
    gfm_d = din("gfm", (128, 24))
    grep_d = din("grep", (4, 128, D))
    ident_d = din("identf", (128, 128))
    tri_d = din("tri", (128, 256))
    mask_d = din("maskfb", (128, 512))
    y_d = nc.dram_tensor("y", [SEQ, D], F32, kind="ExternalOutput").ap()
    dump_reg = {}
    w1bf_d = nc.dram_tensor("w1bf", [44, 128, 1024], BF16, kind="Internal").ap()
    w2bf_d = nc.dram_tensor("w2bf", [NCH, 128, 1024], BF16, kind="Internal").ap()
    dump_shapes = {"uT": (8 * UTW, BF16), "mix": (17 * 1024, BF16), "h1": (17 * 1024, F32), "ona": (17 * 512, F32)}
    dump_dram = {name: nc.dram_tensor("dbg_" + name, [128, dump_shapes[name][0]], dump_shapes[name][1],
                                      kind="ExternalOutput").ap() for name in dumps}

    P = Prog()
    SB_BYTES = 212000

    from contextlib import ExitStack
    with ExitStack() as es:
        sb_t = es.enter_context(nc.sbuf_tensor("arena", [128, SB_BYTES // 4], F32))
        ps_t = es.enter_context(nc.psum_tensor("psum", [128, 4096], F32))
        sems = {}
        for e in ENGS:
            sems[e] = es.enter_context(nc.semaphore("s_" + e))
        for i in range(NDS):
            sems[("d", i)] = es.enter_context(nc.semaphore("d%d" % i))
        for i in range(NQS):
            sems[("q", i)] = es.enter_context(nc.semaphore("q%d" % i))
        block = es.enter_context(nc.Block())

        A = Arena(sb_t, SB_BYTES)

        def bank(b, n=1):
            return ps_t[:, b * 512:(b + n) * 512]

        ident = A.alloc(BF16, (128,))
        tri = A.alloc(F32, (256,))
        gfm = A.alloc(F32, (24,))
        uT = A.alloc(BF16, (8, UTW))
        stat = A.alloc(F32, (8, 17))
        junk = A.alloc(BF16, (1024,))

        P.dma("pool", lambda e: e.dma_start(out=ident, in_=ident_d), writes=[("ident",)])
        P.dma("sp", lambda e: e.dma_start(out=tri, in_=tri_d), writes=[("tri",)])
        P.dma("sp", lambda e: e.dma_start(out=gfm, in_=gfm_d), writes=[("gfm",)])
        P.op("pool", lambda e: e.memset(uT[:, :, 2064:UTW], 0.0), writes=[("uTpad",)])

        def rstd_ops(ss_ap, tmp_ap, out_ap, n, rkey):
            P.op("act", lambda e: e.activation(out=tmp_ap, in_=ss_ap, func=AF.Ln, scale=1.0 / n, bias=EPS),
                 reads=[rkey + ("ss",)], writes=[rkey + ("ln",)])
            P.op("act", lambda e: e.activation(out=out_ap, in_=tmp_ap, func=AF.Exp, scale=-0.5),
                 reads=[rkey + ("ln",)], writes=[rkey + ("rstd",)])

        def norm_transpose(i, src, src_key, gcol, xs_slot, xs_key, psT, psT_key, stat_base, tag):
            c0, rows = tile_cols(i)
            ss = stat[:rows, stat_base, i:i + 1]
            ln = stat[:rows, stat_base + 1, i:i + 1]
            rs = stat[:rows, stat_base + 2, i:i + 1]
            rkey = (tag, i)
            P.op("act", lambda e: e.activation(out=junk[:rows, :], in_=src, func=AF.Square, accum_out=ss),
                 reads=[src_key], writes=[("junk",), rkey + ("ss",)])
            rstd_ops(ss, ln, rs, D, rkey)
            if tag == "A":
                P.op("dve", lambda e: e.tensor_scalar(out=xs_slot[:rows, :], in0=src, scalar1=rs, scalar2=None, op0=ALU.mult),
                     reads=[src_key, rkey + ("rstd",)], writes=[xs_key])
            else:
                P.op("act", lambda e: e.activation(out=xs_slot[:rows, :], in_=src, func=AF.Copy, scale=rs),
                     reads=[src_key, rkey + ("rstd",)], writes=[xs_key])
            psv = psT.bitcast(BF16)

            def tr(e):
                ins = None
                for k in range(8):
                    ins = e.transpose(out=psv[:, k * 128:k * 128 + rows], in_=xs_slot[:rows, k * 128:(k + 1) * 128],
                                      identity=ident[:rows, :rows])
                return ins
            P.op("pe", tr, reads=[xs_key, ("ident",)], writes=[psT_key])
            src_v = psv.rearrange("p (k c) -> p k c", k=8)[:, :, 0:rows]
            gb = gfm[:, gcol:gcol + 8].to_broadcast([128, 8, rows])

            def fin():
                P.op("dve", lambda e: e.tensor_tensor(out=uT[:, :, c0:c0 + rows], in0=src_v, in1=gb, op=ALU.mult),
                     reads=[psT_key, ("gfm",)], writes=[("uT", i)])
            return fin

        def dump_all():
            for name in dumps:
                ap, dt = dump_reg[name]
                shp = list(ap.shape)
                dd = dump_dram[name]
                src = ap
                if len(shp) == 3:
                    src = ap.rearrange("p a b -> p (a b)")
                elif len(shp) == 4:
                    src = ap.rearrange("p a b c -> p (a b c)")
                P.dma("sp", (lambda dd, src: lambda e: e.dma_start(out=dd, in_=src))(dd, src))

        dump_reg["uT"] = (uT, BF16)
        try:
            hiA = A.hi
            grepA = A.alloc(F32, (1024,), top=True)
            xin = [A.alloc(F32, (2, 1024), top=True) for _ in range(2)]
            xmeta = A.alloc(F32, (1024,), top=True)
            xs = [A.alloc(BF16, (1024,), top=True) for _ in range(2)]
            P.dma("sp", lambda e: e.dma_start(out=grepA, in_=grep_d[0]), writes=[("grep", "A")])
            P.dma("sp", lambda e: e.dma_start(out=xmeta[:16, :], in_=meta_d), writes=[("xmeta",)])
            for pr in range(2):
                srcd = x_d[pr * 256:(pr + 1) * 256, :].rearrange("(t p) d -> p t d", p=128)
                P.dma("act" if pr == 1 else "sp", (lambda pr, srcd: lambda e: e.dma_start(out=xin[pr % 2], in_=srcd))(pr, srcd),
                      writes=[("xin", pr % 2)])
            finA = [None]
            for idxA, i in enumerate([16] + list(range(16))):
                sl = idxA % 2
                c0, rows = tile_cols(i)
                if i == 16:
                    src_ap, src_key = xmeta[:16, :], ("xmeta",)
                else:
                    pr = i // 2
                    if i % 2 == 0 and pr >= 2:
                        srcd = x_d[pr * 256:(pr + 1) * 256, :].rearrange("(t p) d -> p t d", p=128)
                        P.dma("sp" if pr % 2 == 0 else "act", (lambda pr, srcd: lambda e: e.dma_start(out=xin[pr % 2], in_=srcd))(pr, srcd),
                              writes=[("xin", pr % 2)])
                    src_ap, src_key = xin[pr % 2][:, i % 2, :], ("xin", pr % 2)
                fin_ = norm_transpose(i, src_ap, src_key, 8, xs[sl], ("xs", sl),
                                      bank(6 + sl), ("psP", sl), 0, "A")
                if finA[0] is not None:
                    finA[0]()
                finA[0] = fin_
            finA[0]()
            if stage <= 1:
                raise _Stop()

            mB = A.mark()
            mix = A.alloc(BF16, (17, 1024))
            mB2 = A.mark()
            dump_reg["mix"] = (mix, BF16)
            P.op("pool", lambda e: e.memset(mix[:, 16, :], 0.0), writes=[("mix", 16, "na")] + [("mix", 16, "g", h) for h in range(4)])
            zT = A.alloc(BF16, (UTW,))
            upsb = A.alloc(BF16, (512,))
            wb = [A.alloc(BF16, (2, 8, 128)) for _ in range(3)]
            wb_i = [0]
            psP = [bank(6), bank(7)]
            psP_i = [0]
            ev_i = [0]

            P.dma("pool", lambda e: e.dma_start(out=upsb[0:33, :], in_=up_d), writes=[("upsb",)])
            P.op("pool", lambda e: e.memset(zT[32:33, :], 1.0), writes=[("zT1",)])

            def evac_copy(dst, src, reads, writes, func=None):
                if func is not None:
                    P.op("act", lambda e: e.activation(out=dst, in_=src, func=func), reads=reads, writes=writes)
                    return
                ev_i[0] += 1
                if ev_i[0] % 2 == 0:
                    P.op("act", lambda e: e.copy(out=dst, in_=src), reads=reads, writes=writes)
                else:
                    P.op("dve", lambda e: e.tensor_copy(out=dst, in_=src), reads=reads, writes=writes)

            wstg = [A.alloc(F32, (8, 128)) for _ in range(2)]
            stg_i = [0]

            def load_cast(src_ap, dst_ap, dst_key, stg, eng="pool"):
                ss_ = stg_i[0] % len(stg)
                stg_i[0] += 1
                P.dma("sp", lambda e: e.dma_start(out=stg[ss_], in_=src_ap), writes=[("stg", id(stg), ss_)])
                if eng == "act":
                    P.op("act", lambda e: e.copy(out=dst_ap, in_=stg[ss_]), reads=[("stg", id(stg), ss_)], writes=[dst_key])
                else:
                    P.op(eng, lambda e: e.tensor_copy(out=dst_ap, in_=stg[ss_]), reads=[("stg", id(stg), ss_)], writes=[dst_key])

            def load_wblk(col0, ncols):
                sl = wb_i[0] % 3
                wb_i[0] += 1
                assert col0 % 128 == 0
                nblk = (ncols + 127) // 128
                for j in range(nblk):
                    srcap = w_in_d[col0 // 128 + j].rearrange("p (k c) -> p k c", k=8)
                    load_cast(srcap, wb[sl][:, j, :, :], ("wb", sl, j), wstg)
                return sl, nblk

            def proj_fm(col0, ncols, dst, dst_key):
                sl, _nb = load_wblk(col0, ncols)
                for (c0, n) in FM_NT:
                    pb = psP_i[0] % 2
                    psP_i[0] += 1
                    ps = psP[pb]

                    def mm(e, sl=sl, c0=c0, n=n, ps=ps):
                        ins = None
                        for k in range(8):
                            ins = e.matmul(ps[0:ncols, 0:n], lhsT=wb[sl][:, 0, k, 0:ncols], rhs=uT[:, k, c0:c0 + n],
                                           start=(k == 0), stop=(k == 7))
                        return ins
                    tl = tiles_overlapping(c0, n)
                    P.op("pe", mm, reads=[("wb", sl, 0)] + [("uT", t) for t in tl], writes=[("psP", pb)])
                    evac_copy(dst[0:ncols, c0:c0 + n], ps[0:ncols, 0:n], [("psP", pb)],
                              [dst_key + (t,) for t in tl])
                    yield

            def proj_tm(col0, ncols, evac):
                sl, nblk = load_wblk(col0, ncols)
                for i in range(17):
                    c0, rows = tile_cols(i)
                    pb = psP_i[0] % 2
                    psP_i[0] += 1
                    ps = psP[pb]

                    def mm(e, sl=sl, c0=c0, rows=rows, ps=ps):
                        ins = None
                        for k in range(8):
                            ins = e.matmul(ps[0:rows, 0:ncols], lhsT=uT[:, k, c0:c0 + rows], rhs=wb[sl][:, 0:nblk, k, :],
                                           start=(k == 0), stop=(k == 7))
                        return ins
                    P.op("pe", mm, reads=[("wb", sl, j) for j in range(nblk)] + [("uT", i)], writes=[("psP", pb)])
                    evac(i, rows, ps[0:rows, 0:ncols], ("psP", pb))
                    yield

            def run(gen):
                for _ in gen:
                    pass

            def interleave(main, side, ratio=1):
                side_done = False
                for _ in main:
                    for _r in range(ratio):
                        if not side_done:
                            try:
                                next(side)
                            except StopIteration:
                                side_done = True
                if not side_done:
                    for _ in side:
                        pass

            def gen_interleave(ga, gb):
                da = db = False
                while not (da and db):
                    if not da:
                        try:
                            next(ga)
                        except StopIteration:
                            da = True
                    if not db:
                        try:
                            next(gb)
                        except StopIteration:
                            db = True
                    yield

            def gen_chain(*gens):
                for g_ in gens:
                    yield from g_

            mNA = A.mark()
            ona = A.alloc(F32, (17, 512))
            dump_reg["ona"] = (ona, F32)
            qT = [A.alloc(BF16, (UTW,)) for _ in range(2)]
            kT = [A.alloc(BF16, (UTW,)) for _ in range(2)]
            vaug = [A.alloc(BF16, (17, 2, 65)) for _ in range(2)]
            tab = [A.alloc(BF16, (2, 1664)) for _ in range(2)]
            S_sb = [A.alloc(F32, (640,)) for _ in range(2)]
            PT = [[A.alloc(BF16, (640,)) for _ in range(2)] for _h in range(2)]
            PTm = [[A.alloc(BF16, (128,)) for _ in range(2)] for _h in range(2)]
            PTq = [A.alloc(BF16, (16,)) for _ in range(2)]
            rec = A.alloc(F32, (4,))
            for b in range(2):
                P.op("pool", (lambda b: lambda e: e.memset(vaug[b][:, 16, :, :], 0.0))(b), writes=[("vz", b)])
                P.op("pool", (lambda b: lambda e: e.memset(vaug[b][:, :, :, 64:65], 1.0))(b), reads=[("vz", b)], writes=[("vones", b)])
            for hl_ in range(2):
                for par_ in range(2):
                    P.op("pool", (lambda hl_, par_: lambda e: e.memset(PTm[hl_][par_], 0.0))(hl_, par_),
                         writes=[("PTm0",), ("PTm", hl_, par_)])

            psS = [bank(0, 2), bank(2, 2)]
            psO = [bank(4), bank(5)]

            def na_proj(hp):
                b = hp % 2
                srcap = tabs_d[2 * hp:2 * hp + 2].rearrange("h p c -> p h c")
                P.dma("pool", lambda e: e.dma_start(out=tab[b], in_=srcap), writes=[("tab", b)])
                yield from proj_fm(hp * 128, 128, qT[b], ("qT", b))
                yield from proj_fm(512 + hp * 128, 128, kT[b], ("kT", b))

                def ev(i, rows, ps, pkey):
                    dst = vaug[b][:rows, i, :, 0:64]
                    evac_copy(dst, ps.rearrange("p (h d) -> p h d", h=2), [pkey, ("vones", b)], [("vaug", b, i)])
                yield from proj_tm(1024 + hp * 128, 128, ev)

            na_it = [0]

            def na_attn(hp):
                b = hp % 2
                its = []
                for i in range(16):
                    if 2 <= i <= 13:
                        js = [i + 2 - c for c in range(5)]
                        tcol = 1024
                    else:
                        lo = 0 if i < 2 else 12
                        js = [lo + 3 - c for c in range(4)]
                        tcol = (7 - 2 * (js[0] - i)) * 64
                    its.append((i, js, tcol))
                its.append((16, [], 0))

                def stageA(t):
                    i, js, tcol = its[t]
                    par = t % 2
                    if i == 16:
                        def mmM(e):
                            ins = None
                            for hl in range(2):
                                r0 = hl * 64
                                ins = e.matmul(psS[hl][0:16, 768:784], lhsT=kT[b][r0:r0 + 64, 0:16],
                                               rhs=qT[b][r0:r0 + 64, 0:16], start=True, stop=True)
                            return ins
                        P.op("pe", mmM, reads=[("kT", b, 16), ("qT", b, 16)], writes=[("psS", 0), ("psS", 1)])
                        for hl in range(2):
                            P.op("act", (lambda hl: lambda e: e.activation(out=PTq[hl][0:16, 0:16], in_=psS[hl][0:16, 768:784],
                                                                          func=AF.Exp, scale=0.125))(hl),
                                 reads=[("psS", hl)], writes=[("PTq", hl)])
                        return
                    qc0 = 16 + 128 * i
                    ncol = 128 * len(js)

                    for hl in range(2):
                        pS = psS[hl]
                        r0 = hl * 64

                        def mmS(e, pS=pS, r0=r0):
                            for c, j in enumerate(js):
                                kc0 = 16 + 128 * j
                                e.matmul(pS[:, c * 128:(c + 1) * 128], lhsT=kT[b][r0:r0 + 64, kc0:kc0 + 128],
                                         rhs=qT[b][r0:r0 + 64, qc0:qc0 + 128], start=True, stop=True)
                            return e.matmul(pS[:, 768:896], lhsT=kT[b][r0:r0 + 64, 0:128],
                                            rhs=qT[b][r0:r0 + 64, qc0:qc0 + 128], start=True, stop=True)
                        P.op("pe", mmS, reads=[("kT", b, j) for j in js] + [("kT", b, 16), ("kT", b, 0), ("qT", b, i)],
                             writes=[("psS", hl)])
                        P.op("dve", (lambda hl, pS: lambda e: e.scalar_tensor_tensor(
                            out=S_sb[hl][:, 0:ncol], in0=pS[:, 0:ncol], scalar=0.125, in1=tab[b][:, hl, tcol:tcol + ncol],
                            op0=ALU.mult, op1=ALU.add))(hl, pS),
                            reads=[("psS", hl), ("tab", b)], writes=[("S_sb", hl)])
                        P.op("act", (lambda hl: lambda e: e.activation(out=PT[hl][par][:, 0:ncol], in_=S_sb[hl][:, 0:ncol],
                                                                      func=AF.Exp))(hl),
                             reads=[("S_sb", hl)], writes=[("PT", hl, par)])
                        P.op("act", (lambda hl, pS: lambda e: e.activation(out=PTm[hl][par][0:16, :], in_=pS[0:16, 768:896],
                                                                          func=AF.Exp, scale=0.125))(hl, pS),
                             reads=[("psS", hl), ("PTm0",)], writes=[("PTm", hl, par)])

                def stageB(t):
                    i, js, tcol = its[t]
                    par = t % 2
                    for hl in range(2):
                        pO = psO[hl]
                        head = 2 * hp + hl
                        if i == 16:
                            P.op("pe", (lambda hl, pO: lambda e: e.matmul(pO[0:16, 0:65], lhsT=PTq[hl][0:16, 0:16],
                                                                          rhs=vaug[b][0:16, 16, hl, :], start=True, stop=True))(hl, pO),
                                 reads=[("PTq", hl), ("vaug", b, 16)], writes=[("psO", hl)])
                            rows = 16
                        else:
                            def mmO(e, hl=hl, pO=pO):
                                for c, j in enumerate(js):
                                    e.matmul(pO[:, 0:65], lhsT=PT[hl][par][:, c * 128:(c + 1) * 128], rhs=vaug[b][:, j, hl, :],
                                             start=(c == 0), stop=False)
                                return e.matmul(pO[:, 0:65], lhsT=PTm[hl][par][:, :], rhs=vaug[b][:, 16, hl, :],
                                                start=False, stop=True)
                            P.op("pe", mmO, reads=[("PT", hl, par), ("PTm", hl, par), ("vz", b)] + [("vaug", b, j) for j in js] + [("vaug", b, 16)],
                                 writes=[("psO", hl)])
                            rows = 128
                        P.op("dve", (lambda hl, pO, rows: lambda e: e.reciprocal(out=rec[0:rows, hl:hl + 1], in_=pO[0:rows, 64:65]))(hl, pO, rows),
                             reads=[("psO", hl)], writes=[("rec", hl)])
                        P.op("act", (lambda hl, pO, rows, head: lambda e: e.activation(
                            out=ona[0:rows, i, head * 64:(head + 1) * 64], in_=pO[0:rows, 0:64], func=AF.Copy,
                            scale=rec[0:rows, hl:hl + 1]))(hl, pO, rows, head),
                            reads=[("psO", hl), ("rec", hl)], writes=[("ona", i, head)])

                n = len(its)
                for t in range(n + 1):
                    if t < n:
                        stageA(t)
                    if t >= 1:
                        stageB(t - 1)
                    if hp == 3 and t >= 2:
                        na_final(its[t - 2][0])
                    yield
                    yield
                if hp == 3:
                    na_final(its[n - 1][0])

            def na_final(i):
                c0, rows = tile_cols(i)
                ss = stat[:rows, 3, i:i + 1]
                ln = stat[:rows, 4, i:i + 1]
                rs = stat[:rows, 5, i:i + 1]
                P.op("act", (lambda rows, i, ss: lambda e: e.activation(out=junk[:rows, 0:512], in_=ona[:rows, i, :],
                                                                         func=AF.Square, accum_out=ss))(rows, i, ss),
                     reads=[("ona", i, h) for h in range(8)], writes=[("junk",), ("N", i, "ss")])
                rstd_ops(ss, ln, rs, 512, ("N", i))
                P.op("dve", (lambda rows, i, rs: lambda e: e.tensor_scalar(out=mix[:rows, i, 0:512], in0=ona[:rows, i, :],
                                                                            scalar1=rs, scalar2=None, op0=ALU.mult))(rows, i, rs),
                     reads=[("ona", i, h) for h in range(8)] + [("N", i, "rstd")], writes=[("mix", i, "na")])
            def precast_gen():
                blocks = [(w_f2_d[k_ * 128:(k_ + 1) * 128, :], w2bf_d[k_], ("w2bf", k_)) for k_ in range(NCH)]
                pin = [xin[k_][:, 0, :] for k_ in range(2)]
                pout = []
                for k_ in range(2):
                    v_ = xin[k_][:, 1, :].bitcast(BF16)
                    pout += [v_[:, 0:1024], v_[:, 1024:2048]]
                nb = len(blocks)

                def dma_in(b_):
                    k_ = b_ % 2
                    P.dma("sp", lambda e: e.dma_start(out=pin[k_], in_=blocks[b_][0]),
                          writes=[("pin", k_)] + ([("xin", k_)] if b_ < 2 else []))

                def dma_out(b_):
                    o_ = b_ % 4
                    P.dma("sp", lambda e: e.dma_start(out=blocks[b_][1], in_=pout[o_]),
                          reads=[("pout", o_)], writes=[blocks[b_][2]])
                dma_in(0)
                for b_ in range(nb):
                    if b_ + 1 < nb:
                        dma_in(b_ + 1)
                    P.op("pool", (lambda b_: lambda e: e.tensor_copy(out=pout[b_ % 4], in_=pin[b_ % 2]))(b_),
                         reads=[("pin", b_ % 2)], writes=[("pout", b_ % 4)] + ([("xin", b_ % 2)] if b_ < 2 else []))
                    if b_ >= 2:
                        dma_out(b_ - 2)
                    for _ in range(6):
                        yield
                dma_out(nb - 2)
                dma_out(nb - 1)
                yield

            pre_g = precast_gen()

            def take(g_, n_):
                for _ in range(n_):
                    try:
                        next(g_)
                    except StopIteration:
                        return
                    yield

            run(na_proj(0))
            for hp in range(4):
                if hp < 3:
                    interleave(na_attn(hp), gen_interleave(na_proj(hp + 1), take(pre_g, 36)), ratio=1)
                else:
                    interleave(na_attn(hp), gen_interleave(proj_fm(3072, 32, zT, ("zT",)), take(pre_g, 36)), ratio=1)
            run(pre_g)

            A.reset(mNA)
            P.barrier()
            A.hi = hiA
            if stage <= 2:
                raise _Stop()

            mG = A.mark()
            maskfb = A.alloc(F32, (4, 128))
            qg = A.alloc(F32, (UTW,))
            kg = A.alloc(F32, (UTW,))
            vg2 = [A.alloc(BF16, (17, 256)) for _ in range(2)]
            sg2 = [A.alloc(BF16, (17, 256)) for _ in range(2)]
            qdec = A.alloc(BF16, (2, UTW))
            kinvT = A.alloc(BF16, (2, UTW))
            kinv_tm = A.alloc(BF16, (17, 256))
            Sst = [A.alloc(BF16, (34, 128)) for _ in range(2)]
            dec = A.alloc(F32, (2, 34))
            Tst = [[A.alloc(F32, (128,)) for _ in range(2)] for _d in range(2)]
            lae = [A.alloc(F32, (256,)) for _ in range(3)]
            lap = [A.alloc(F32, (256,)) for _ in range(3)]
            e1 = [A.alloc(F32, (2, 128)) for _ in range(3)]
            e2 = [A.alloc(F32, (2, 128)) for _ in range(3)]
            ATb = [A.alloc(BF16, (4, 128)) for _ in range(2)]
            gst = A.alloc(F32, (3, 4))
            P.dma("sp", lambda e: e.dma_start(out=maskfb, in_=mask_d.rearrange("p (a b) -> p a b", a=4)), writes=[("maskfb",)])

            psA_ = [bank(2), bank(3)]
            psOg_bank = [[0, 1], [4, 5]]
            psOg = [[bank(b_) for b_ in row] for row in psOg_bank]


            def chunk_loc(n):
                if n == 0:
                    return 16, 0, 16
                t = (n - 1) // 2
                a = (n - 1) % 2
                return t, a * 64, 64

            def gla_proj_fm(hp):
                yield from proj_fm(1536 + hp * 128, 128, qg, ("qg",))
                yield from proj_fm(1792 + hp * 128, 128, kg, ("kg",))

            def gla_proj_tm(hp):
                vg, sg = vg2[hp % 2], sg2[hp % 2]

                def evv(i, rows, ps, pkey):
                    evac_copy(vg[:rows, i, :], ps, [pkey], [("vg", hp % 2, i)])
                yield from proj_tm(2048 + hp * 256, 256, evv)

                def evg(i, rows, ps, pkey):
                    evac_copy(sg[:rows, i, :], ps, [pkey], [("sg", hp % 2, i)], func=AF.Silu)
                yield from proj_tm(2560 + hp * 256, 256, evg)

            def gla_prep(hp, tiles, par):
                bLA, bBT, bKT = ((0, 1, 0), (4, 5, 1), (2, 3, 0))[par]
                pLA, kLA = bank(bLA), ("pb", bLA)
                pBT, kBT = bank(bBT), ("pb", bBT)
                pKT, kKT = bank(6 + bKT), ("psP", bKT)
                sl = par
                for i in tiles:
                    c0, rows = tile_cols(i)
                    zreads = [("zT", i), ("zT1",), ("upsb",)]

                    def mmla(e, c0=c0, rows=rows):
                        e.matmul(pLA[0:rows, 0:128], lhsT=zT[0:33, c0:c0 + rows], rhs=upsb[0:33, hp * 128:(hp + 1) * 128],
                                 start=True, stop=True)
                        return e.matmul(pLA[0:rows, 128:256], lhsT=zT[0:33, c0:c0 + rows],
                                        rhs=upsb[0:33, 256 + hp * 128:256 + (hp + 1) * 128], start=True, stop=True)
                    P.op("pe", mmla, reads=zreads, writes=[kLA])
                    yield
                    P.op("act", (lambda rows: lambda e: e.activation(out=lae[sl][:rows, :], in_=pLA[0:rows, 0:256],
                                                                      func=AF.Exp, scale=-1.0))(rows),
                         reads=[kLA], writes=[("lae", sl)])
                    P.op("act", (lambda rows: lambda e: e.activation(out=lap[sl][:rows, :], in_=lae[sl][:rows, :],
                                                                      func=AF.Ln, bias=1.0))(rows),
                         reads=[("lae", sl)], writes=[("lap", sl)])
                    yield

                    def mmbt(e, rows=rows):
                        e.matmul(pBT[:, 0:rows], lhsT=lap[sl][:rows, 0:128], rhs=tri[:rows, 0:rows],
                                 start=True, stop=True)
                        return e.matmul(pBT[:, 128:128 + rows], lhsT=lap[sl][:rows, 128:256], rhs=tri[:rows, 128:128 + rows],
                                        start=True, stop=True)
                    P.op("pe", mmbt, reads=[("lap", sl), ("tri",)], writes=[kBT])
                    yield
                    bt = pBT[:, 0:256].rearrange("p (a b) -> p a b", a=2)[:, :, 0:rows]
                    P.op("act", (lambda rows, bt: lambda e: e.activation(out=e1[sl][:, :, 0:rows], in_=bt, func=AF.Exp))(rows, bt),
                         reads=[kBT], writes=[("e1", sl)])
                    P.op("act", (lambda rows, bt: lambda e: e.activation(out=e2[sl][:, :, 0:rows], in_=bt, func=AF.Exp,
                                                                          scale=-1.0))(rows, bt),
                         reads=[kBT], writes=[("e2", sl)])
                    yield
                    for d_ in range(2):
                        P.op("dve", (lambda rows, c0, d_: lambda e: e.scalar_tensor_tensor(
                            out=qdec[:, d_, c0:c0 + rows], in0=qg[:, c0:c0 + rows], scalar=0.125, in1=e1[sl][:, d_, 0:rows],
                            op0=ALU.mult, op1=ALU.mult))(rows, c0, d_),
                            reads=[("qg", i), ("e1", sl)], writes=[("qdec", i)])
                        P.op("pool", (lambda rows, c0, d_: lambda e: e.tensor_tensor(
                            out=kinvT[:, d_, c0:c0 + rows], in0=kg[:, c0:c0 + rows], in1=e2[sl][:, d_, 0:rows],
                            op=ALU.mult))(rows, c0, d_),
                            reads=[("kg", i), ("e2", sl)], writes=[("kinvT", i, d_)])
                    if i < 16:
                        n1 = 2 * i + 1
                        P.op("dve", (lambda n1: lambda e: e.tensor_copy(
                            out=dec[:, 0, n1:n1 + 2], in_=e1[sl][:, 0, :].rearrange("p (a b) -> p a b", a=2)[:, :, 63]))(n1),
                            reads=[("e1", sl)], writes=[("dec", i, 0)])
                        P.op("dve", (lambda n1: lambda e: e.tensor_copy(
                            out=dec[:, 1, n1:n1 + 2], in_=e1[sl][:, 1, :].rearrange("p (a b) -> p a b", a=2)[:, :, 0]))(n1),
                            reads=[("e1", sl)], writes=[("dec", i, 1)])
                    else:
                        P.op("dve", lambda e: e.tensor_copy(out=dec[:, 0, 0:1], in_=e1[sl][:, 0, 15:16]),
                             reads=[("e1", sl)], writes=[("dec", i, 0)])
                        P.op("dve", lambda e: e.tensor_copy(out=dec[:, 1, 0:1], in_=e1[sl][:, 1, 0:1]),
                             reads=[("e1", sl)], writes=[("dec", i, 1)])
                    yield
                    pkt = pKT.bitcast(BF16)

                    def trk(e, rows=rows, c0=c0, pkt=pkt):
                        e.transpose(out=pkt[0:rows, 0:128], in_=kinvT[:, 0, c0:c0 + rows], identity=ident[:, :])
                        return e.transpose(out=pkt[0:rows, 128:256], in_=kinvT[:, 1, c0:c0 + rows], identity=ident[:, :])
                    P.op("pe", trk, reads=[("kinvT", i, 0), ("kinvT", i, 1), ("ident",)], writes=[kKT])
                    P.op("dve", (lambda rows, i, pkt: lambda e: e.tensor_copy(out=kinv_tm[:rows, i, :], in_=pkt[0:rows, 0:256]))(rows, i, pkt),
                         reads=[kKT], writes=[("kinv_tm", i)])
                    yield

            def gla_scan(hp, d_):
                vg = vg2[hp % 2]
                order = list(range(33)) if d_ == 0 else list(range(32, -1, -1))
                banks_ = (4, 5) if d_ == 0 else (2, 3)
                prev = None
                for step, n in enumerate(order):
                    t, r0, rows = chunk_loc(n)
                    bk = banks_[step % 2]
                    pD = bank(bk)[:, 0:128]

                    def mmds(e, pD=pD, t=t, r0=r0, rows=rows):
                        ins = None
                        for hl in range(2):
                            ins = e.matmul(pD[hl * 64:(hl + 1) * 64, :],
                                           lhsT=kinv_tm[r0:r0 + rows, t, d_ * 128 + hl * 64:d_ * 128 + (hl + 1) * 64],
                                           rhs=vg[r0:r0 + rows, t, hl * 128:(hl + 1) * 128],
                                           start=True, stop=True, tile_position=(r0, hl * 64))
                        return ins
                    P.op("pe", mmds, reads=[("kinv_tm", t), ("vg", hp % 2, t)], writes=[("pb", bk)])
                    tn = step % 2
                    if prev is None:
                        P.op("dve", (lambda pD, tn: lambda e: e.tensor_copy(out=Tst[d_][tn], in_=pD))(pD, tn),
                             reads=[("pb", bk)], writes=[("T", d_, tn)])
                    else:
                        P.op("dve", (lambda pD, tn, prev: lambda e: e.scalar_tensor_tensor(
                            out=Tst[d_][tn], in0=Tst[d_][1 - tn], scalar=dec[:, d_, prev:prev + 1], in1=pD,
                            op0=ALU.mult, op1=ALU.add))(pD, tn, prev),
                            reads=[("pb", bk), ("T", d_, 1 - tn)] + [("dec", i, d_) for i in range(17)], writes=[("T", d_, tn)])
                    if step < 32:
                        P.op("act", (lambda tn, n: lambda e: e.activation(out=Sst[d_][:, n, :], in_=Tst[d_][tn], func=AF.Copy,
                                                                          scale=dec[:, d_, n:n + 1]))(tn, n),
                             reads=[("T", d_, tn)] + [("dec", i, d_) for i in range(17)], writes=[("S", d_, n)])
                    prev = n
                    yield

            def gla_out(hp):
                tiles = list(range(16)) + [16]
                vg, sg = vg2[hp % 2], sg2[hp % 2]

                def stageA(t):
                    i = tiles[t]
                    c0, rows = tile_cols(i)
                    sl = t % 2

                    for hl in range(2):
                        def mmA(e, hl=hl):
                            ins = None
                            for d_ in range(2):
                                ins = e.matmul(psA_[hl][0:rows, d_ * 128:d_ * 128 + rows],
                                               lhsT=kinvT[hl * 64:(hl + 1) * 64, d_, c0:c0 + rows],
                                               rhs=qdec[hl * 64:(hl + 1) * 64, d_, c0:c0 + rows], start=True, stop=True)
                            return ins
                        P.op("pe", mmA, reads=[("kinvT", i, 0), ("kinvT", i, 1), ("qdec", i)], writes=[("pb", 2 + hl)])
                        pav = psA_[hl][:, 0:256].rearrange("p (a b) -> p a b", a=2)
                        P.op("dve", (lambda pav, hl: lambda e: e.tensor_tensor(
                            out=ATb[sl][:rows, 2 * hl:2 * hl + 2, 0:rows], in0=pav[0:rows, :, 0:rows],
                            in1=maskfb[:rows, 0:2, 0:rows], op=ALU.mult))(pav, hl),
                            reads=[("pb", 2 + hl), ("maskfb",)], writes=[("AT", sl, hl)])

                def stageB(t):
                    i = tiles[t]
                    c0, rows = tile_cols(i)
                    sl = t % 2
                    if i < 16:
                        chunks = [(2 * i + 1, 0), (2 * i + 2, 64)]
                    else:
                        chunks = [(0, 0)]
                    sreads = []
                    for (n, _r) in chunks:
                        if n >= 1:
                            sreads.append(("S", 0, n - 1))
                        if n <= 31:
                            sreads.append(("S", 1, n + 1))
                    for hl in range(2):
                        pO = psOg[sl][hl]
                        pkey = ("pb", psOg_bank[sl][hl])
                        oc = slice(hl * 128, (hl + 1) * 128)

                        def mmO(e, pO=pO, hl=hl, oc=oc):
                            extra = []
                            for (n, r0) in chunks:
                                cr = 64 if i < 16 else 16
                                if n >= 1:
                                    extra.append((0, n - 1, r0, cr))
                                if n <= 31:
                                    extra.append((1, n + 1, r0, cr))
                            e.matmul(pO[0:rows, 0:128], lhsT=ATb[sl][:rows, hl * 2, 0:rows], rhs=vg[:rows, i, oc],
                                     start=True, stop=False)
                            ins = e.matmul(pO[0:rows, 0:128], lhsT=ATb[sl][:rows, hl * 2 + 1, 0:rows], rhs=vg[:rows, i, oc],
                                           start=False, stop=(len(extra) == 0))
                            for xi, (d_, sn, r0, cr) in enumerate(extra):
                                last_of_range = (xi == len(extra) - 1) or (extra[xi + 1][2] != r0)
                                ins = e.matmul(pO[r0:r0 + cr, 0:128], lhsT=qdec[hl * 64:(hl + 1) * 64, d_, c0 + r0:c0 + r0 + cr],
                                               rhs=Sst[d_][hl * 64:(hl + 1) * 64, sn, :], start=False,
                                               stop=last_of_range, tile_position=(hl * 64, r0))
                            return ins
                        P.op("pe", mmO, reads=[("AT", sl, hl), ("vg", hp % 2, i), ("qdec", i)] + sreads, writes=[pkey])
                        gi = sl * 2 + hl
                        P.op("act", (lambda pO, gi: lambda e: e.activation(
                            out=junk[:rows, 0:128], in_=pO[0:rows, 0:128], func=AF.Square,
                            accum_out=gst[:rows, 0, gi:gi + 1]))(pO, gi),
                            reads=[pkey], writes=[("junk",), ("G", gi, "ss")])
                        rstd_ops(gst[:rows, 0, gi:gi + 1], gst[:rows, 1, gi:gi + 1], gst[:rows, 2, gi:gi + 1], 128, ("G", gi))
                        head = 2 * hp + hl
                        P.op("dve", (lambda pO, hl, head, gi: lambda e: e.scalar_tensor_tensor(
                            out=mix[:rows, i, 512 + head * 128:512 + (head + 1) * 128], in0=pO[0:rows, 0:128],
                            scalar=gst[:rows, 2, gi:gi + 1], in1=sg[:rows, i, hl * 128:(hl + 1) * 128],
                            op0=ALU.mult, op1=ALU.mult))(pO, hl, head, gi),
                            reads=[pkey, ("G", gi, "rstd"), ("sg", hp % 2, i)], writes=[("mix", i, "g", head)])

                n = len(tiles)
                for t in range(n + 1):
                    if t < n:
                        stageA(t)
                    if t >= 1:
                        stageB(t - 1)
                    yield

            def prep_all(hp):
                return gen_interleave(gen_interleave(gla_prep(hp, [16, 0, 3, 6, 9, 12, 15], 0),
                                                     gla_prep(hp, [1, 4, 7, 10, 13], 1)),
                                      gla_prep(hp, [2, 5, 8, 11, 14], 2))

            run(gla_proj_fm(0))
            interleave(prep_all(0), gla_proj_tm(0), ratio=1)
            interleave(gen_chain(gen_interleave(gla_scan(0, 0), gla_scan(0, 1)), gla_out(0)),
                       gen_chain(gla_proj_fm(1), gla_proj_tm(1)), ratio=1)
            run(prep_all(1))
            run(gen_interleave(gla_scan(1, 0), gla_scan(1, 1)))
            run(gla_out(1))
            A.reset(mG)
            P.barrier()
            if stage <= 3:
                raise _Stop()

            A.reset(mB2)
            h1 = A.alloc(F32, (17, 1024), top=True)
            dump_reg["h1"] = (h1, F32)
            mC = A.mark()
            wo_sb = A.alloc(BF16, (8, 1024))
            wstg = [A.alloc(F32, (1024,)) for _ in range(2)]
            grepP = A.alloc(F32, (1024,))
            grepF = A.alloc(F32, (1024,))
            tmpE = [A.alloc(F32, (1024,)) for _ in range(2)]
            xs2 = [A.alloc(BF16, (1024,)) for _ in range(3)]
            P.dma("sp", lambda e: e.dma_start(out=grepP, in_=grep_d[1]), writes=[("grep", "P")])
            P.dma("sp", lambda e: e.dma_start(out=grepF, in_=grep_d[2]), writes=[("grep", "F")])
            for k in range(8):
                sl = k % 2
                P.dma("sp" if k % 2 == 0 else "act",
                      (lambda k, sl: lambda e: e.dma_start(out=wstg[sl], in_=w_o_d[k * 128:(k + 1) * 128, :]))(k, sl),
                      writes=[("wstg", sl)])
                P.op("act" if k % 2 == 0 else "dve",
                     (lambda k, sl: lambda e: e.activation(out=wo_sb[:, k, :], in_=wstg[sl], func=AF.Copy,
                                                           scale=gfm[:, k:k + 1])) (k, sl) if k % 2 == 0 else
                     (lambda k, sl: lambda e: e.tensor_scalar(out=wo_sb[:, k, :], in0=wstg[sl], scalar1=gfm[:, k:k + 1],
                                                              scalar2=None, op0=ALU.mult))(k, sl),
                     reads=[("wstg", sl), ("gfm",)], writes=[("wo_sb", k)])
            P.dma("sp", lambda e: e.dma_start(out=h1[:16, 16, :], in_=meta_d), writes=[("h1", 16)])
            for pr in range(8):
                srcd = x_d[pr * 256:(pr + 1) * 256, :].rearrange("(t p) d -> p t d", p=128)
                P.dma("sp" if pr % 2 == 0 else "act",
                      (lambda pr, srcd: lambda e: e.dma_start(out=h1[:, 2 * pr:2 * pr + 2, :], in_=srcd))(pr, srcd),
                      writes=[("h1", 2 * pr), ("h1", 2 * pr + 1)])
            for i in range(17):
                c0, rows = tile_cols(i)
                sl = i % 2
                psv = bank(sl).bitcast(BF16)

                def trm(e, i=i, rows=rows, psv=psv):
                    ins = None
                    for k in range(8):
                        ins = e.transpose(out=psv[:, k * 128:k * 128 + rows], in_=mix[:rows, i, k * 128:(k + 1) * 128],
                                          identity=ident[:rows, :rows])
                    return ins
                mreads = [("mix", i, "na")] + [("mix", i, "g", h) for h in range(4)] + [("ident",)]
                P.op("pe", trm, reads=mreads, writes=[("ps", sl)])
                src_v = psv.rearrange("p (k c) -> p k c", k=8)[:, :, 0:rows]
                evac_copy(uT[:, :, c0:c0 + rows], src_v, [("ps", sl)], [("uT", i)])
            psY = [bank(2, 2), bank(4, 2)]

            finC = [None]

            def u2_tile(j):
                _c0, rj = tile_cols(j)
                fin_ = norm_transpose(j, h1[:rj, j, :], ("h1", j), 16, xs2[j % 3], ("xs2", j % 3),
                                      bank(6 + j % 2), ("psT2", j % 2), 3, "F")
                if finC[0] is not None:
                    finC[0]()
                finC[0] = fin_
            for i in range(17):
                c0, rows = tile_cols(i)
                sl = i % 2
                pY = psY[sl]

                def mmY(e, c0=c0, rows=rows, pY=pY):
                    ins = None
                    for nh in range(2):
                        for k in range(8):
                            ins = e.matmul(pY[0:rows, nh * 512:(nh + 1) * 512], lhsT=uT[:, k, c0:c0 + rows],
                                           rhs=wo_sb[:, k, nh * 512:(nh + 1) * 512], start=(k == 0), stop=(k == 7))
                    return ins
                P.op("pe", mmY, reads=[("uT", i)] + [("wo_sb", k) for k in range(8)], writes=[("psY", sl)])
                if i >= 2:
                    u2_tile(i - 2)
                ss = stat[:rows, 0, i:i + 1]
                ln = stat[:rows, 1, i:i + 1]
                rs = stat[:rows, 2, i:i + 1]
                P.op("act", (lambda rows, pY, ss: lambda e: e.activation(out=junk[:rows, :], in_=pY[0:rows, :], func=AF.Square,
                                                                          accum_out=ss))(rows, pY, ss),
                     reads=[("psY", sl)], writes=[("junk",), ("C", i, "ss")])
                rstd_ops(ss, ln, rs, D, ("C", i))
                P.op("dve", (lambda sl, rows, pY, rs: lambda e: e.scalar_tensor_tensor(
                    out=tmpE[sl][:rows, :], in0=pY[0:rows, :], scalar=rs, in1=grepP[:rows, :], op0=ALU.mult, op1=ALU.mult))(sl, rows, pY, rs),
                    reads=[("psY", sl), ("C", i, "rstd"), ("grep", "P")], writes=[("tmpE", sl)])
                P.op("pool", (lambda sl, rows, i: lambda e: e.tensor_tensor(out=h1[:rows, i, :], in0=h1[:rows, i, :],
                                                                             in1=tmpE[sl][:rows, :], op=ALU.add))(sl, rows, i),
                     reads=[("tmpE", sl), ("h1", i)], writes=[("h1", i)])
            u2_tile(15)
            u2_tile(16)
            finC[0]()
            A.reset(mB)
            P.barrier()
            if stage <= 4:
                raise _Stop()

            hid = A.alloc(BF16, (NCH, 1024))
            cw = A.alloc(F32, (44, 4))
            grepO = A.alloc(F32, (1024,))
            wf = [A.alloc(BF16, (2, 8, 128)) for _ in range(3)]
            fstg = [A.alloc(F32, (8, 128)) for _ in range(3)]
            wo2 = [A.alloc(BF16, (1024,)) for _ in range(4)]
            accV = [A.alloc(F32, (344,)) for _ in range(3)]
            accG = [A.alloc(F32, (344,)) for _ in range(3)]
            gel = [A.alloc(F32, (344,)) for _ in range(2)]
            tmpO = [A.alloc(F32, (1024,)) for _ in range(2)]
            P.dma("sp", lambda e: e.dma_start(out=cw, in_=cw_d.rearrange("p (c j) -> p c j", j=4)), writes=[("cw",)])
            P.dma("sp", lambda e: e.dma_start(out=grepO, in_=grep_d[3]), writes=[("grep", "O")])

            pb_i = [0]
            cv_i = [0]
            wf_i = [0]
            wo_i = [0]
            ep_i = [0]
            NTS = [(0, 342), (342, 342), (684, 340)]
            for hf in range(2):
                T0 = hf * 1024
                def load_wf(c, hf=hf):
                    sl_ = c % 3
                    for vg_ in range(2):
                        blk_ = vg_ * NCH + c
                        scr = w1bf_d[blk_].rearrange("p (k c) -> p k c", k=8)
                        if hf == 0:
                            srcap = w_f1_d[blk_].rearrange("p (k c) -> p k c", k=8)
                            load_cast(srcap, wf[sl_][:, vg_, :, :], ("wf", sl_, vg_), fstg, eng=("act" if vg_ == 0 else "dve"))
                            P.dma("act" if vg_ == 0 else "sp",
                                  (lambda sl_, vg_, scr: lambda e: e.dma_start(out=scr, in_=wf[sl_][:, vg_, :, :]))(sl_, vg_, scr),
                                  reads=[("wf", sl_, vg_)], writes=[("w1bf", blk_)])
                        else:
                            P.dma("sp", (lambda sl_, vg_, scr: lambda e: e.dma_start(out=wf[sl_][:, vg_, :, :], in_=scr))(sl_, vg_, scr),
                                  reads=[("w1bf", blk_)], writes=[("wf", sl_, vg_)])
                def tail_ops(cs, n, c, o0):
                    P.op("act", lambda e: e.activation(out=gel[cs % 2][:, 0:n], in_=accG[cs][:, 0:n], func=AF.Gelu_apprx_tanh),
                         reads=[("acc", cs, 1)], writes=[("gel", cs % 2)])
                    P.op("pool", lambda e: e.tensor_tensor(out=hid[:, c, o0:o0 + n], in0=accV[cs][:, 0:n],
                                                           in1=gel[cs % 2][:, 0:n], op=ALU.mult),
                         reads=[("acc", cs, 0), ("gel", cs % 2)], writes=[("hid", c, o0)])
                pending = [None]
                load_wf(0)
                for c in range(NCH):
                    sl = c % 3
                    if c + 1 < NCH:
                        load_wf(c + 1)
                    for (o0, n) in NTS:
                        ci = 16 + T0 + o0 - 1
                        nin = n + 2
                        tl = tiles_overlapping(ci, min(nin, 2064 - ci))
                        pbs = []
                        for vg_ in range(2):
                            pb = pb_i[0] % 8
                            pb_i[0] += 1
                            pbs.append(pb)
                            ps = bank(pb)

                            def mm(e, sl=sl, vg_=vg_, ci=ci, nin=nin, ps=ps):
                                ins = None
                                for k in range(8):
                                    ins = e.matmul(ps[:, 0:nin], lhsT=wf[sl][:, vg_, k, :],
                                                   rhs=uT[:, k, ci:ci + nin], start=(k == 0), stop=(k == 7))
                                return ins
                            P.op("pe", mm, reads=[("wf", sl, vg_), ("uTpad",)] + [("uT", t) for t in tl],
                                 writes=[("psF", pb)])
                        cs = cv_i[0] % 3
                        cv_i[0] += 1
                        for vg_, acc in ((0, accV[cs]), (1, accG[cs])):
                            ps = bank(pbs[vg_])
                            cc = vg_ * NCH + c
                            P.op("act", (lambda ps, acc, cc, n: lambda e: e.activation(
                                out=acc[:, 0:n], in_=ps[:, 1:n + 1], func=AF.Identity, scale=cw[:, cc, 1:2], bias=cw[:, cc, 3:4]))(ps, acc, cc, n),
                                reads=[("psF", pbs[vg_]), ("cw",)], writes=[("acc", cs, vg_)])
                            P.op("dve", (lambda ps, acc, cc, n: lambda e: e.scalar_tensor_tensor(
                                out=acc[:, 0:n], in0=ps[:, 0:n], scalar=cw[:, cc, 0:1], in1=acc[:, 0:n],
                                op0=ALU.mult, op1=ALU.add))(ps, acc, cc, n),
                                reads=[("psF", pbs[vg_]), ("cw",), ("acc", cs, vg_)], writes=[("acc", cs, vg_)])
                            P.op("dve", (lambda ps, acc, cc, n: lambda e: e.scalar_tensor_tensor(
                                out=acc[:, 0:n], in0=ps[:, 2:n + 2], scalar=cw[:, cc, 2:3], in1=acc[:, 0:n],
                                op0=ALU.mult, op1=ALU.add))(ps, acc, cc, n),
                                reads=[("psF", pbs[vg_]), ("cw",), ("acc", cs, vg_)], writes=[("acc", cs, vg_)])
                        if pending[0] is not None:
                            pending[0]()
                        pending[0] = (lambda cs, n, c, o0: lambda: tail_ops(cs, n, c, o0))(cs, n, c, o0)
                if pending[0] is not None:
                    pending[0]()
                    pending[0] = None
                def load_wo(q):
                    k_ = q % NCH
                    P.dma("sp", (lambda q, k_: lambda e: e.dma_start(out=wo2[q % 4], in_=w2bf_d[k_]))(q, k_),
                          reads=[("w2bf", k_)], writes=[("wo2", q % 4)])

                qbase = wo_i[0]
                nq_ = 2 * NCH
                load_wo(qbase)
                load_wo(qbase + 1)
                load_wo(qbase + 2)
                for blk in range(2):
                    for k in range(NCH):
                        q = wo_i[0]
                        ws = q % 4
                        wo_i[0] += 1
                        if q + 3 < qbase + nq_:
                            load_wo(q + 3)

                        for tl_ in range(4):
                            tc0 = (blk * 4 + tl_) * 128

                            def mm2(e, ws=ws, k=k, tl_=tl_, tc0=tc0):
                                ins = None
                                for nh in range(2):
                                    ins = e.matmul(bank(tl_ * 2 + nh), lhsT=hid[:, k, tc0:tc0 + 128],
                                                   rhs=wo2[ws][:, nh * 512:(nh + 1) * 512], start=(k == 0), stop=(k == NCH - 1))
                                return ins
                            P.op("pe", mm2, reads=[("wo2", ws)] + [("hid", k, o0) for (o0, _n) in NTS],
                                 writes=[("psF", tl_ * 2), ("psF", tl_ * 2 + 1)])
                    for tl_ in range(4):
                        i = hf * 8 + blk * 4 + tl_
                        pY = bank(tl_ * 2, 2)
                        es_ = ep_i[0] % 2
                        ep_i[0] += 1
                        ss = stat[:, 0, i:i + 1]
                        ln = stat[:, 1, i:i + 1]
                        rs = stat[:, 2, i:i + 1]
                        pk = [("psF", tl_ * 2), ("psF", tl_ * 2 + 1)]
                        P.op("act", (lambda pY, ss: lambda e: e.activation(out=junk[:, :], in_=pY, func=AF.Square, accum_out=ss))(pY, ss),
                             reads=pk, writes=[("junk",), ("O", i, "ss")])
                        rstd_ops(ss, ln, rs, D, ("O", i))
                        P.op("dve", (lambda es_, pY, rs: lambda e: e.scalar_tensor_tensor(
                            out=tmpO[es_], in0=pY, scalar=rs, in1=grepO, op0=ALU.mult, op1=ALU.mult))(es_, pY, rs),
                            reads=pk + [("O", i, "rstd"), ("grep", "O")], writes=[("tmpO", es_)])
                        P.op("dve" if (hf == 1 and blk == 1 and tl_ % 2 == 1) else "pool",
                             (lambda es_, i: lambda e: e.tensor_tensor(out=h1[:, i, :], in0=h1[:, i, :], in1=tmpO[es_],
                                                                       op=ALU.add))(es_, i),
                             reads=[("tmpO", es_), ("h1", i)], writes=[("h1", i)])
                        P.dma("sp", (lambda i: lambda e: e.dma_start(out=y_d[i * 128:(i + 1) * 128, :], in_=h1[:, i, :]))(i),
                              reads=[("h1", i)])
        except _Stop:
            pass
        dump_all()
        P.finish()
        print("SBUF peak bytes/partition:", A.peak, " ops:", {e: len(P.streams[e]) for e in ENGS})

        @block.tensor
        def _(e):
            P.emit("pe", e, sems)

        @block.scalar
        def _(e):
            P.emit("act", e, sems)

        @block.vector
        def _(e):
            P.emit("dve", e, sems)

        @block.gpsimd
        def _(e):
            P.emit("pool", e, sems)

        @block.sync
        def _(e):
            P.emit("sp", e, sems)
    return nc


def _na_tables(rpb):
    H = rpb.shape[0]
    cq = np.arange(64)
    cs = np.clip(cq - 8, 0, 48)
    ck = np.arange(64)
    in_win = (ck[None, :] >= cs[:, None]) & (ck[None, :] < cs[:, None] + 16)
    dc = np.clip(ck[:, None] - cq[None, :], -15, 15) + 15
    G = rpb[:, :, dc]
    G = np.where(in_win.T[None, None], G, np.float32(NEG)).astype(np.float32)
    tabs = np.full((H, 128, 1664), NEG, np.float32)
    for a in range(2):
        for u in range(16):
            dr = a + 14 - u
            if 0 <= dr <= 14:
                tabs[:, a * 64:(a + 1) * 64, u * 64:(u + 1) * 64] = G[:, dr]
    for c in range(5):
        for b in range(2):
            u = 3 + 2 * c + b
            for a in range(2):
                dr = a + 14 - u
                if 3 <= dr <= 10:
                    tabs[:, a * 64:(a + 1) * 64, 1024 + (2 * c + b) * 64:1024 + (2 * c + b + 1) * 64] = G[:, dr]
    return tabs


def _consts():
    s = np.arange(128)
    same = (s[:, None] // 64) == (s[None, :] // 64)
    trif = np.where(same & (s[:, None] <= s[None, :]), -1.0 / 16.0, 0.0)
    trib = np.where(same & (s[:, None] >= s[None, :]), -1.0 / 16.0, 0.0)
    tri = np.concatenate([trif, trib], axis=1).astype(np.float32)
    mf = (same & (s[:, None] <= s[None, :])).astype(np.float32)
    mb = (same & (s[:, None] > s[None, :])).astype(np.float32)
    maskfb = np.concatenate([mf, mb, mf, mb], axis=1).astype(np.float32)
    ident = np.eye(128, dtype=np.float32)
    return ident, tri, maskfb


def _tile_weights(w_in, w_f1):
    wp = np.zeros((1024, 3200), np.float32)
    wp[:, :3104] = w_in
    w_in_t = np.ascontiguousarray(wp.reshape(8, 128, 25, 128).transpose(2, 1, 0, 3)).reshape(25, 128, 1024)
    w_f1_t = np.ascontiguousarray(w_f1.reshape(8, 128, 44, 128).transpose(2, 1, 0, 3)).reshape(44, 128, 1024)
    return w_in_t, w_f1_t


_NC_CACHE = {}


def kernel(x, meta_tokens, norm_mix_pre, w_in, na_rel_bias, na_out_gain, gla_gate_up_fwd,
           gla_gate_bias_fwd, gla_gate_up_bwd, gla_gate_bias_bwd, gla_out_gain, w_o, norm_mix_post,
           norm_ffn_pre, w_ffn_in, ffn_conv_w, ffn_conv_b, w_ffn_out, norm_ffn_post):
    f = lambda a: np.ascontiguousarray(np.asarray(a, dtype=np.float32))
    x = f(x)
    B = x.shape[0]
    up33 = np.zeros((33, 512), np.float32)
    up33[0:16, 0:256] = f(gla_gate_up_fwd)[0]
    up33[16:32, 256:512] = f(gla_gate_up_bwd)[0]
    up33[32, 0:256] = f(gla_gate_bias_fwd)[0]
    up33[32, 256:512] = f(gla_gate_bias_bwd)[0]
    tabs = _na_tables(f(na_rel_bias)[0])
    gvec_o = np.concatenate([f(na_out_gain)[0], np.tile(f(gla_out_gain)[0], 4)])
    gfm = f(np.concatenate([gvec_o.reshape(8, 128).T, f(norm_mix_pre)[0].reshape(8, 128).T,
                            f(norm_ffn_pre)[0].reshape(8, 128).T], axis=1))
    grep = np.stack([np.broadcast_to(f(v)[0][None, :], (128, D)) for v in
                     (norm_mix_pre, norm_mix_post, norm_ffn_pre, norm_ffn_post)])
    grep = f(grep)
    cwj = f(ffn_conv_w)[0].reshape(3, 44, 128)
    cbj = f(ffn_conv_b)[0].reshape(1, 44, 128)
    cw = f(np.concatenate([cwj, cbj], axis=0).transpose(2, 1, 0).reshape(128, 44 * 4))
    ident, tri, maskfb = _consts()
    w_in_t, w_f1_t = _tile_weights(f(w_in)[0], f(w_ffn_in)[0])
    shared = {
        "meta": f(meta_tokens), "w_in_t": w_in_t, "up33": up33, "tabs": tabs, "w_o": f(w_o)[0],
        "w_f1_t": w_f1_t, "w_f2": f(w_ffn_out)[0], "cw": cw, "gfm": gfm, "grep": grep,
        "identf": ident, "tri": tri, "maskfb": maskfb,
    }
    if "nc" not in _NC_CACHE:
        _NC_CACHE["nc"] = build_program()
    nc = _NC_CACHE["nc"]
    in_maps = []
    for b in range(B):
        m = dict(shared)
        m["x"] = np.ascontiguousarray(x[b])
        in_maps.append(m)
    res = run_bass_kernel_spmd(nc, in_maps, core_ids=list(range(B)))
    out = np.stack([np.asarray(r["y"], dtype=np.float32) for r in res.results], axis=0)
    return out
```
